# Optimizing a Trainium2 kernel written in Bass

```python
import math
import jax, jax.numpy as jnp
from jax import lax
import numpy as np

D_MODEL = 1024
BATCH = 8
SEQ = 4096
DEPTH = 4

GDN_HEADS = 8
GDN_DK = 128
GDN_DV = 128
GDN_QK = GDN_HEADS * GDN_DK
GDN_WIDTH = GDN_HEADS * GDN_DV
GDN_CONV_CH = 2 * GDN_QK + GDN_WIDTH
CONV_WIDTH = 5
CHUNK = 64
DIFF_HEADS = 8
DIFF_DH = 64
DIFF_DV = 2 * DIFF_DH
DIFF_QK = DIFF_HEADS * 2 * DIFF_DH
DIFF_WIDTH = DIFF_HEADS * DIFF_DV
Q_BLOCK = 128
ROPE_THETA = 500000.0
ROPE_DIM = DIFF_DH // 4
EPS = 1e-6

_SEG = (GDN_QK, GDN_QK, GDN_WIDTH, 4 * GDN_HEADS, GDN_WIDTH,
        DIFF_QK, DIFF_QK, DIFF_WIDTH, DIFF_WIDTH, D_MODEL, D_MODEL)
N_IN = sum(_SEG)
SPLIT_IDX = tuple(sum(_SEG[:i + 1]) for i in range(len(_SEG) - 1))

kernel_name = "hybrid_gdn_diffattn_gated_merge_encoder"


def rmsnorm(x, w):
    xf = x.astype(jnp.float32)
    y = xf * lax.rsqrt(jnp.mean(xf * xf, axis=-1, keepdims=True) + EPS)
    return y.astype(x.dtype) * w


def l2norm(x):
    return x * lax.rsqrt(jnp.sum(x * x, axis=-1, keepdims=True) + EPS)


def short_conv(x, w):
    pad = CONV_WIDTH // 2
    return lax.conv_general_dilated(
        x, w[:, None, :].astype(x.dtype), window_strides=(1,), padding=[(pad, pad)],
        dimension_numbers=("NWC", "WIO", "NWC"), feature_group_count=x.shape[-1])


def chunk_gated_delta(q, k, v, g, beta):
    f32 = jnp.float32
    q, k, v, g, beta = (t.astype(f32) for t in (q, k, v, g, beta))
    bsz, nh, seq, dk = k.shape
    dv = v.shape[-1]
    nc = seq // CHUNK
    q = q * dk ** -0.5
    kb = k * beta[..., None]
    vb = v * beta[..., None]
    rs = lambda t: t.reshape(bsz, nh, nc, CHUNK, t.shape[-1])
    q, k, kb, vb = rs(q), rs(k), rs(kb), rs(vb)
    g = jnp.cumsum(g.reshape(bsz, nh, nc, CHUNK), axis=-1)
    lower = jnp.tril(jnp.ones((CHUNK, CHUNK), dtype=bool))
    strict = jnp.tril(jnp.ones((CHUNK, CHUNK), dtype=bool), -1)
    gdiff = g[..., :, None] - g[..., None, :]
    decay = jnp.where(lower, jnp.exp(jnp.where(lower, gdiff, 0.0)), 0.0)
    L = jnp.where(strict, jnp.einsum('bhncd,bhnmd->bhncm', kb, k) * decay, 0.0)
    eye = jnp.eye(CHUNK, dtype=f32)
    T = lax.linalg.triangular_solve(eye + L, jnp.broadcast_to(eye, L.shape),
                                    left_side=True, lower=True, unit_diagonal=True)
    u = jnp.einsum('bhncm,bhnme->bhnce', T, vb)
    w = jnp.einsum('bhncm,bhnmd->bhncd', T, kb * jnp.exp(g)[..., None])
    qk = jnp.where(lower, jnp.einsum('bhncd,bhnmd->bhncm', q, k) * decay, 0.0)

    def step(state, inp):
        q_c, k_c, u_c, w_c, g_c, qk_c = inp
        v_new = u_c - jnp.einsum('bhcd,bhde->bhce', w_c, state)
        o = (jnp.einsum('bhcd,bhde->bhce', q_c * jnp.exp(g_c)[..., None], state)
             + jnp.einsum('bhcm,bhme->bhce', qk_c, v_new))
        g_last = g_c[..., -1]
        state = (state * jnp.exp(g_last)[..., None, None]
                 + jnp.einsum('bhcd,bhce->bhde', k_c * jnp.exp(g_last[..., None] - g_c)[..., None], v_new))
        return state, o

    xs = tuple(jnp.moveaxis(t, 2, 0) for t in (q, k, u, w, g, qk))
    state0 = jnp.zeros((bsz, nh, dk, dv), f32)
    _, o = lax.scan(step, state0, xs)
    return jnp.moveaxis(o, 0, 2).reshape(bsz, nh, seq, dv)


def gated_delta_branch(q, k, v, ab, z, conv_w, a_log, dt_bias, norm_w):
    bsz, seq, _ = q.shape
    f32 = jnp.float32
    qkv = jax.nn.silu(short_conv(jnp.concatenate([q, k, v], axis=-1), conv_w))
    q, k, v = jnp.split(qkv, (GDN_QK, 2 * GDN_QK), axis=-1)
    heads = lambda t, d: t.reshape(bsz, seq, GDN_HEADS, d).transpose(0, 2, 1, 3).astype(f32)
    q = l2norm(heads(q, GDN_DK))
    k = l2norm(heads(k, GDN_DK))
    v = heads(v, GDN_DV)
    ab = ab.reshape(bsz, seq, 4, GDN_HEADS).transpose(2, 0, 3, 1).astype(f32)
    beta = jax.nn.sigmoid(ab[:2])
    g = (-jnp.exp(a_log.astype(f32))[:, None, :, None]
         * jax.nn.softplus(ab[2:] + dt_bias.astype(f32)[:, None, :, None]))
    o_fwd = chunk_gated_delta(q, k, v, g[0], beta[0])
    flip = lambda t: jnp.flip(t, axis=2)
    o_bwd = flip(chunk_gated_delta(flip(q), flip(k), flip(v), flip(g[1]), flip(beta[1])))
    o = rmsnorm(o_fwd + o_bwd, norm_w.astype(f32))
    o = o.transpose(0, 2, 1, 3).reshape(bsz, seq, GDN_WIDTH).astype(z.dtype)
    return o * jax.nn.silu(z)


def partial_rope(t, cos, sin):
    half = ROPE_DIM // 2
    c = cos[:, :, None, None, :].astype(t.dtype)
    s = sin[:, :, None, None, :].astype(t.dtype)
    x1 = t[..., :half]
    x2 = t[..., half:ROPE_DIM]
    return jnp.concatenate([x1 * c - x2 * s, x2 * c + x1 * s, t[..., ROPE_DIM:]], axis=-1)


def diff_attention_branch(q, k, v, z, cos, sin, lam_params, subln_w, lambda_init):
    bsz, seq, _ = q.shape
    f32 = jnp.float32
    q = partial_rope(q.reshape(bsz, seq, DIFF_HEADS, 2, DIFF_DH), cos, sin) * (DIFF_DH ** -0.5)
    k = partial_rope(k.reshape(bsz, seq, DIFF_HEADS, 2, DIFF_DH), cos, sin)
    q = q.transpose(3, 0, 2, 1, 4)
    k = k.transpose(3, 0, 2, 1, 4)
    v = v.reshape(bsz, seq, DIFF_HEADS, DIFF_DV).transpose(0, 2, 1, 3)
    lp = lam_params.astype(f32)
    lam = jnp.exp(jnp.sum(lp[0] * lp[1])) - jnp.exp(jnp.sum(lp[2] * lp[3])) + lambda_init
    n_blk = seq // Q_BLOCK
    qb = q.reshape(2, bsz, DIFF_HEADS, n_blk, Q_BLOCK, DIFF_DH).transpose(3, 0, 1, 2, 4, 5)

    def attend(q_blk):
        s = jnp.einsum('nbhqd,nbhkd->nbhqk', q_blk, k).astype(f32)
        p = jax.nn.softmax(s, axis=-1)
        a = p[0] - lam * p[1]
        return jnp.einsum('bhqk,bhkd->bhqd', a.astype(v.dtype), v)

    o = lax.map(attend, qb)
    o = o.transpose(1, 2, 0, 3, 4).reshape(bsz, DIFF_HEADS, seq, DIFF_DV)
    o = rmsnorm(o, subln_w) * (1.0 - lambda_init)
    o = o.transpose(0, 2, 1, 3).reshape(bsz, seq, DIFF_WIDTH)
    return o * jax.nn.silu(z)


def setup_inputs(seed: int = 0) -> dict:
    key = jax.random.key(seed)
    ks = jax.random.split(key, 16)
    f32 = jnp.float32
    x = jax.random.normal(ks[0], (BATCH, SEQ, D_MODEL), f32)
    positions = (jnp.arange(SEQ, dtype=jnp.int32)[None, :]
                 + jax.random.randint(ks[1], (BATCH, 1), 0, 4096, dtype=jnp.int32))
    norm_w = 1.0 + 0.02 * jax.random.normal(ks[2], (DEPTH, D_MODEL), f32)
    w_in = jax.random.normal(ks[3], (DEPTH, D_MODEL, N_IN), f32) * D_MODEL ** -0.5
    conv_w = jax.random.normal(ks[4], (DEPTH, CONV_WIDTH, GDN_CONV_CH), f32) * CONV_WIDTH ** -0.5
    a_log = jnp.log(jax.random.uniform(ks[5], (DEPTH, 2, GDN_HEADS), f32, 1.0, 16.0))
    dt = jnp.exp(jax.random.uniform(ks[6], (DEPTH, 2, GDN_HEADS), f32, math.log(1e-3), math.log(1e-1)))
    dt_bias = dt + jnp.log(-jnp.expm1(-dt))
    gdn_norm_w = 1.0 + 0.02 * jax.random.normal(ks[7], (DEPTH, GDN_DV), f32)
    diff_lambda = 0.1 * jax.random.normal(ks[8], (DEPTH, 4, DIFF_DH), f32)
    diff_subln_w = 1.0 + 0.02 * jax.random.normal(ks[9], (DEPTH, DIFF_DV), f32)
    w_pa = jax.random.normal(ks[10], (DEPTH, GDN_WIDTH, D_MODEL), f32) * GDN_WIDTH ** -0.5
    w_pb = jax.random.normal(ks[11], (DEPTH, DIFF_WIDTH, D_MODEL), f32) * DIFF_WIDTH ** -0.5
    w_out = jax.random.normal(ks[12], (DEPTH, D_MODEL, D_MODEL), f32) * D_MODEL ** -0.5
    final_norm_w = 1.0 + 0.02 * jax.random.normal(ks[13], (D_MODEL,), f32)
    return {"x": x, "positions": positions, "norm_w": norm_w, "w_in": w_in, "conv_w": conv_w,
            "a_log": a_log, "dt_bias": dt_bias, "gdn_norm_w": gdn_norm_w,
            "diff_lambda": diff_lambda, "diff_subln_w": diff_subln_w, "w_pa": w_pa,
            "w_pb": w_pb, "w_out": w_out, "final_norm_w": final_norm_w}


def reference(x, positions, norm_w, w_in, conv_w, a_log, dt_bias, gdn_norm_w,
              diff_lambda, diff_subln_w, w_pa, w_pb, w_out, final_norm_w):
    inv_freq = ROPE_THETA ** (-(jnp.arange(0, ROPE_DIM, 2, dtype=jnp.float32) / ROPE_DIM))
    angles = positions.astype(jnp.float32)[..., None] * inv_freq
    cos, sin = jnp.cos(angles), jnp.sin(angles)
    for l in range(DEPTH):
        h = rmsnorm(x, norm_w[l])
        proj = h @ w_in[l]
        (q_a, k_a, v_a, ab_a, z_a, q_b, k_b, v_b, z_b,
         gate_a, gate_b) = jnp.split(proj, SPLIT_IDX, axis=-1)
        y_a = gated_delta_branch(q_a, k_a, v_a, ab_a, z_a, conv_w[l], a_log[l], dt_bias[l], gdn_norm_w[l])
        lambda_init = 0.8 - 0.6 * math.exp(-0.3 * l)
        y_b = diff_attention_branch(q_b, k_b, v_b, z_b, cos, sin, diff_lambda[l], diff_subln_w[l], lambda_init)
        merged = (jax.nn.sigmoid(gate_a) * (y_a @ w_pa[l])
                  + jax.nn.sigmoid(gate_b) * (y_b @ w_pb[l]))
        x = x + merged @ w_out[l]
    return rmsnorm(x, final_norm_w)
```

```python
import math
import numpy as np
import concourse.bass as bass
import concourse.mybir as mybir
from concourse.bass_utils import run_bass_kernel_spmd

F32, BF16, I32 = mybir.dt.float32, mybir.dt.bfloat16, mybir.dt.int32
AF = mybir.ActivationFunctionType
ALU = mybir.AluOpType
AX = mybir.AxisListType

S_ = 4096
D_ = 1024
L_ = 4
NIN = 10272
NT = 32
EPS = 1e-6
C_QA, C_KA, C_VA, C_AB, C_ZA, C_QB, C_KB, C_VB, C_ZB, C_GA, C_GB = (
    0, 1024, 2048, 3072, 3104, 4128, 5152, 6176, 7200, 8224, 9248)
NPRM = 664
P_CW, P_ALOG, P_DTB, P_GNW, P_SLW, P_LAM = 0, 120, 136, 152, 280, 408
NCST = 9 * 128 + 2
K_ID, K_PMF, K_NMF, K_PMB, K_NMB, K_TRF, K_TRB, K_PERM, K_ONE = [i * 128 for i in range(9)]
K_INVF, K_SGN = 9 * 128, 9 * 128 + 1
TWO_PI = 2.0 * math.pi
TWO_PI_HI = 6.28125
TWO_PI_LO = TWO_PI - TWO_PI_HI
BIGM = 30000.0


class Tok:
    __slots__ = ("w", "r")

    def __init__(self):
        self.w = None
        self.r = {}


class Sched:
    ENG = ("tensor", "scalar", "vector", "gpsimd", "sync")

    def __init__(self, nc, stack):
        self.nc = nc
        self.sems = []
        self.E = {}
        for n in self.ENG:
            s = stack.enter_context(nc.semaphore("s_" + n))
            self.sems.append(s)
            self.E[n] = dict(si=len(self.sems) - 1, cnt=0, waited={}, prog=[], slots=[], nxt=0)
        for n in ("sync", "gpsimd", "scalar"):
            for k in range(8):
                s = stack.enter_context(nc.semaphore("d_%s%d" % (n, k)))
                self.sems.append(s)
                self.E[n]["slots"].append([len(self.sems) - 1, 0])
        self.nops = 0

    def _deps(self, e, r, w):
        need = {}
        for t in r:
            if t.w is not None:
                s, v = t.w
                if need.get(s, 0) < v:
                    need[s] = v
        for t in w:
            if t.w is not None:
                s, v = t.w
                if need.get(s, 0) < v:
                    need[s] = v
            for s, v in t.r.items():
                if need.get(s, 0) < v:
                    need[s] = v
        waits = []
        for s, v in need.items():
            if s == e["si"] and e is self.E["tensor"]:
                continue
            if e["waited"].get(s, 0) >= v:
                continue
            e["waited"][s] = v
            waits.append((s, v))
        return waits

    def _mark(self, me, r, w):
        s, v = me
        for t in r:
            if t.r.get(s, 0) < v:
                t.r[s] = v
        for t in w:
            t.w = me
            t.r = {}

    def op(self, en, fn, r=(), w=()):
        e = self.E[en]
        waits = self._deps(e, r, w)
        e["cnt"] += 1
        e["prog"].append((waits, fn, (e["si"], 1)))
        self._mark((e["si"], e["cnt"]), r, w)
        self.nops += 1

    def dma(self, en, out, in_, r=(), w=()):
        e = self.E[en]
        waits = self._deps(e, r, w)
        slot = e["slots"][e["nxt"] % len(e["slots"])]
        e["nxt"] += 1
        if slot[1] > 0 and e["waited"].get(slot[0], 0) < slot[1]:
            waits.append((slot[0], slot[1]))
            e["waited"][slot[0]] = slot[1]
        slot[1] += 16
        e["prog"].append((waits, (lambda h, o=out, i=in_: h.dma_start(out=o, in_=i)), (slot[0], 16)))
        self._mark((slot[0], slot[1]), r, w)
        self.nops += 1

    def barrier(self):
        tgt = []
        for n in self.ENG:
            e = self.E[n]
            if e["cnt"] > 0:
                tgt.append((e["si"], e["cnt"]))
            for sl in e["slots"]:
                if sl[1] > 0:
                    tgt.append((sl[0], sl[1]))
        for n in self.ENG:
            e = self.E[n]
            waits = []
            for s, v in tgt:
                if s == e["si"]:
                    continue
                if e["waited"].get(s, 0) >= v:
                    continue
                e["waited"][s] = v
                waits.append((s, v))
            if waits:
                e["prog"].append((waits, None, None))

    def replay(self, en, h):
        for waits, fn, inc in self.E[en]["prog"]:
            for s, v in waits:
                h.wait_ge(self.sems[s], v)
            if fn is not None:
                ins = fn(h)
                ins.then_inc(self.sems[inc[0]], inc[1])


class Arena:
    def __init__(self, big, nbytes):
        self.big = big
        self.cap = nbytes
        self.top = 0

    def mark(self):
        return self.top

    def reset(self, m):
        self.top = m

    def alloc(self, nbytes, dt=BF16):
        off = (self.top + 63) // 64 * 64
        assert off + nbytes <= self.cap, ("SBUF arena overflow", off, nbytes, self.cap)
        self.top = off + nbytes
        ap = self.big[:, off // 2:(off + nbytes) // 2]
        if dt is not BF16:
            ap = ap.bitcast(dt)
        return ap


def v3(ap, a, b):
    return ap.rearrange("p (a b) -> p a b", a=a, b=b)


def build(n_layers=L_, dbg=False, stop_after=None):
    from contextlib import ExitStack
    nc = bass.Bass("TRN2", target_bir_lowering=False)

    def din(name, shape, dt=F32):
        return nc.dram_tensor(name, shape, dt, kind="ExternalInput").ap()

    x_d = din("x", [S_, D_])
    pos_d = din("pos", [128, S_], I32)
    w_in_d = din("w_in", [L_, D_, NIN])
    w_pa_d = din("w_pa", [L_, D_, D_])
    w_pb_d = din("w_pb", [L_, D_, D_])
    w_out_d = din("w_out", [L_, D_, D_])
    cst_d = din("cst", [128, NCST])
    prm_d = din("prm", [128, L_ * NPRM])
    nw_d = din("nw", [L_ + 1, 128, D_])
    out_d = nc.dram_tensor("out", [S_, D_], F32, kind="ExternalOutput").ap()
    kS = "ExternalOutput" if dbg else "Internal"
    scr_fm = nc.dram_tensor("scr_fm", [7, 8, 128, S_], BF16, kind=kS).ap()
    scr_tm = nc.dram_tensor("scr_tm", [3, 8, 128, S_], BF16, kind=kS).ap()
    scr_y = nc.dram_tensor("scr_y", [2, 8, 128, S_], BF16, kind=kS).ap()
    scr_x = nc.dram_tensor("scr_x", [S_, D_], F32, kind=kS).ap()

    with ExitStack() as stack:
        ARENA_BYTES = 212000
        big = stack.enter_context(nc.sbuf_tensor("big", [128, ARENA_BYTES // 2], BF16))
        banks = [stack.enter_context(nc.psum_tensor("ps%d" % i, [128, 512], F32)) for i in range(8)]
        SC = Sched(nc, stack)
        A = Arena(big, ARENA_BYTES)
        op, dma = SC.op, SC.dma

        cst = A.alloc(NCST * 4, F32)
        prm = A.alloc(L_ * NPRM * 4, F32)
        cbf = A.alloc(4 * 128 * 2)
        ident_bf, perm_bf, ones_bf = cbf[:, 0:128], cbf[:, 128:256], cbf[:, 256:384]
        ccol = A.alloc(16 * 4, F32)
        Ct = A.alloc(S_ * 2)
        St = A.alloc(S_ * 2)
        abt = A.alloc(NT * 32 * 4, F32)
        nwb = A.alloc(D_ * 4, F32)
        lamc = A.alloc(8 * 4, F32)
        t_cst, t_prm, t_cbf, t_ccol, t_rope, t_abt, t_gt, t_nwb, t_lam = [Tok() for _ in range(9)]
        ident_f = cst[:, K_ID:K_ID + 128]
        ones_f = cst[:, K_ONE:K_ONE + 128]
        m_H = A.mark()
        hT = A.alloc(8 * S_ * 2)
        hT3 = v3(hT, 8, S_)
        t_hT = [Tok() for _ in range(NT)]
        m_W = A.mark()

        pb = [Tok() for _ in range(8)]

        dma("sync", cst, cst_d, w=[t_cst])
        dma("sync", prm, prm_d, w=[t_prm])
        op("vector", lambda h: h.tensor_copy(out=cbf[:, 0:128], in_=cst[:, K_ID:K_ID + 128]), r=[t_cst], w=[t_cbf])
        op("vector", lambda h: h.tensor_copy(out=cbf[:, 128:256], in_=cst[:, K_PERM:K_PERM + 128]), r=[t_cst], w=[t_cbf])
        op("vector", lambda h: h.tensor_copy(out=cbf[:, 256:384], in_=cst[:, K_ONE:K_ONE + 128]), r=[t_cst], w=[t_cbf])
        CC_EPS, CC_ONE, CC_DKEPS, CC_PI2 = 0, 1, 2, 3
        for ci, val in ((CC_EPS, EPS), (CC_ONE, 1.0), (CC_DKEPS, 128.0 * EPS), (CC_PI2, math.pi / 2)):
            op("vector", lambda h, ci=ci, val=val: h.memset(ccol[:, ci:ci + 1], val), w=[t_ccol])

        def cc(i):
            return ccol[:, i:i + 1]

        m0 = A.mark()
        posi = A.alloc(S_ * 4, I32)
        ang = A.alloc(S_ * 4, F32)
        uu = A.alloc(S_ * 4, F32)
        rr = A.alloc(S_ * 4, F32)
        ki = A.alloc(S_ * 4, I32)
        tp = Tok()
        dma("sync", posi, pos_d, w=[tp])
        V = "vector"
        op(V, lambda h: h.tensor_copy(out=ang, in_=posi), r=[tp], w=[tp])
        op(V, lambda h: h.tensor_scalar(out=ang, in0=ang, scalar1=cst[:, K_INVF:K_INVF + 1], scalar2=None, op0=ALU.mult), r=[tp, t_cst], w=[tp])
        op(V, lambda h: h.tensor_scalar(out=uu, in0=ang, scalar1=1.0 / TWO_PI, scalar2=None, op0=ALU.mult), r=[tp], w=[tp])
        op(V, lambda h: h.tensor_copy(out=ki, in_=uu), r=[tp], w=[tp])
        op(V, lambda h: h.tensor_copy(out=uu, in_=ki), r=[tp], w=[tp])
        op(V, lambda h: h.scalar_tensor_tensor(out=rr, in0=uu, scalar=-TWO_PI_HI, in1=ang, op0=ALU.mult, op1=ALU.add), r=[tp], w=[tp])
        op(V, lambda h: h.scalar_tensor_tensor(out=rr, in0=uu, scalar=-TWO_PI_LO, in1=rr, op0=ALU.mult, op1=ALU.add), r=[tp], w=[tp])

        def fixup(r_):
            op(V, lambda h: h.tensor_scalar(out=uu, in0=r_, scalar1=math.pi, scalar2=None, op0=ALU.is_gt), r=[tp], w=[tp])
            op(V, lambda h: h.scalar_tensor_tensor(out=r_, in0=uu, scalar=-TWO_PI, in1=r_, op0=ALU.mult, op1=ALU.add), r=[tp], w=[tp])
            op(V, lambda h: h.tensor_scalar(out=uu, in0=r_, scalar1=-math.pi, scalar2=None, op0=ALU.is_lt), r=[tp], w=[tp])
            op(V, lambda h: h.scalar_tensor_tensor(out=r_, in0=uu, scalar=TWO_PI, in1=r_, op0=ALU.mult, op1=ALU.add), r=[tp], w=[tp])

        fixup(rr)
        op("scalar", lambda h: h.activation(out=St, in_=rr, func=AF.Sin, scale=cst[:, K_SGN:K_SGN + 1]), r=[tp, t_cst], w=[t_rope])
        op(V, lambda h: h.tensor_scalar(out=ang, in0=rr, scalar1=math.pi / 2, scalar2=None, op0=ALU.add), r=[tp], w=[tp])
        fixup(ang)
        op("scalar", lambda h: h.activation(out=Ct, in_=ang, func=AF.Sin), r=[tp], w=[t_rope])
        SC.barrier()
        A.reset(m0)

        def norm_tile(xt, t_x, t, tmp, final, bank):
            junk, ss, hn, t_tmp = tmp["junk"], tmp["ss"], tmp["hn"], tmp["tok"]
            op("scalar", lambda h: h.activation(out=junk, in_=xt, func=AF.Square, accum_out=ss[:, 0:1]), r=[t_x], w=[t_tmp])
            op("scalar", lambda h: h.activation(out=ss[:, 1:2], in_=ss[:, 0:1], func=AF.Sqrt, scale=1.0 / D_, bias=cc(CC_EPS)), r=[t_ccol], w=[t_tmp])
            op(V, lambda h: h.reciprocal(out=ss[:, 2:3], in_=ss[:, 1:2]), w=[t_tmp])
            if final:
                op(V, lambda h: h.scalar_tensor_tensor(out=xt, in0=xt, scalar=ss[:, 2:3], in1=nwb, op0=ALU.mult, op1=ALU.mult), r=[t_nwb, t_tmp], w=[t_x])
                dma("sync", out_d[t * 128:(t + 1) * 128, :], xt, r=[t_x])
                return
            op(V, lambda h: h.scalar_tensor_tensor(out=hn, in0=xt, scalar=ss[:, 2:3], in1=nwb, op0=ALU.mult, op1=ALU.mult), r=[t_x, t_nwb], w=[t_tmp])
            pbk = banks[bank][:, :].bitcast(BF16)
            for kc in range(8):
                op("tensor", lambda h, kc=kc: h.transpose(out=pbk[:, kc * 128:(kc + 1) * 128], in_=hn[:, kc * 128:(kc + 1) * 128], identity=ident_bf),
                   r=[t_tmp, t_cbf], w=[pb[bank]])
            op("scalar", lambda h: h.copy(out=hT3[:, :, t * 128:(t + 1) * 128], in_=v3(pbk, 8, 128)), r=[pb[bank]], w=[t_hT[t]])

        def norm_tmp():
            hn_ = A.alloc(D_ * 2)
            return dict(junk=hn_, ss=A.alloc(16, F32), hn=hn_, tok=Tok())

        m0 = A.mark()
        dma("sync", nwb, nw_d[0], w=[t_nwb])
        xts = [(A.alloc(D_ * 4, F32), Tok()) for _ in range(3)]
        ntm = [norm_tmp() for _ in range(2)]
        for t in range(NT):
            xt, tx = xts[t % 3]
            dma("sync", xt, x_d[t * 128:(t + 1) * 128, :], w=[tx])
            dma("sync", scr_x[t * 128:(t + 1) * 128, :], xt, r=[tx])
            norm_tile(xt, tx, t, ntm[t % 2], False, t % 2)
        SC.barrier()
        A.reset(m0)

        def phase_B(l, pl):
            T, AC, G = "tensor", "scalar", "gpsimd"
            gta = A.alloc(16 * 1024, F32)

            def gflat(i):
                return gta[:, i * 256:(i + 1) * 256]

            def gtile(i):
                return v3(gflat(i), NT, 8)

            beta = [gtile(0), gtile(1)]
            g_ = [gtile(2), gtile(3)]
            Gc = [gtile(4), gtile(5)]
            nb = [gtile(6), gtile(7)]
            eG = [gtile(8), gtile(9)]
            nbeG = [gtile(10), gtile(11)]
            eGt = [gtile(12), gtile(13)]
            nA = A.alloc(16 * 4, F32)
            t_g = Tok()
            abt3 = v3(abt, NT, 32)
            for d in range(2):
                sl = slice(d * 256, (d + 1) * 256)
                op(AC, lambda h, d=d: h.activation(out=beta[d], in_=abt3[:, :, d * 8:(d + 1) * 8], func=AF.Sigmoid), r=[t_abt], w=[t_g])
                dtb = pl[:, P_DTB + 8 * d:P_DTB + 8 * d + 8].unsqueeze(1).broadcast_to([128, NT, 8])
                op(V, lambda h, d=d, dtb=dtb: h.tensor_tensor(out=g_[d], in0=abt3[:, :, 16 + 8 * d:24 + 8 * d], in1=dtb, op=ALU.add), r=[t_abt, t_prm], w=[t_g])
                op(AC, lambda h, d=d: h.activation(out=g_[d], in_=g_[d], func=AF.Exp), w=[t_g])
                op(AC, lambda h, d=d: h.activation(out=g_[d], in_=g_[d], func=AF.Ln, bias=cc(CC_ONE)), r=[t_ccol], w=[t_g])
                op(AC, lambda h, d=d: h.activation(out=nA[:, 8 * d:8 * d + 8], in_=pl[:, P_ALOG + 8 * d:P_ALOG + 8 * d + 8], func=AF.Exp), r=[t_prm], w=[t_g])
                nAb = nA[:, 8 * d:8 * d + 8].unsqueeze(1).broadcast_to([128, NT, 8])
                op(V, lambda h, d=d, nAb=nAb: h.scalar_tensor_tensor(out=g_[d], in0=g_[d], scalar=-1.0, in1=nAb, op0=ALU.mult, op1=ALU.mult), w=[t_g])
                tri = cst[:, K_TRF:K_TRF + 128] if d == 0 else cst[:, K_TRB:K_TRB + 128]
                op(T, lambda h, d=d, sl=sl, tri=tri: h.matmul(banks[0][:, sl], lhsT=tri, rhs=gflat(2 + d), start=True, stop=True), r=[t_g, t_cst], w=[pb[0]])
                op(T, lambda h, d=d, sl=sl: h.matmul(banks[1][:, sl], lhsT=ones_f, rhs=gflat(2 + d), start=True, stop=True), r=[t_g, t_cst], w=[pb[1]])
                op(V, lambda h, d=d, sl=sl: h.tensor_copy(out=gflat(4 + d), in_=banks[0][:, sl]), w=[t_g, pb[0]])
                op(AC, lambda h, d=d, sl=sl: h.activation(out=gflat(8 + d), in_=banks[0][:, sl], func=AF.Exp), w=[t_g, pb[0]])
                op(AC, lambda h, d=d, sl=sl: h.activation(out=gflat(12 + d), in_=banks[1][:, sl], func=AF.Exp), w=[t_g, pb[1]])
                op(V, lambda h, d=d, sl=sl: h.tensor_tensor(out=gflat(14 + d), in0=banks[1][:, sl], in1=gflat(4 + d), op=ALU.subtract), w=[t_g, pb[1]])
                op(AC, lambda h, d=d: h.activation(out=gflat(14 + d), in_=gflat(14 + d), func=AF.Exp), w=[t_g])
                op(V, lambda h, d=d: h.tensor_scalar(out=gflat(6 + d), in0=gflat(d), scalar1=-1.0, scalar2=None, op0=ALU.mult), w=[t_g])
                op(V, lambda h, d=d: h.tensor_tensor(out=gflat(10 + d), in0=gflat(6 + d), in1=gflat(8 + d), op=ALU.mult), w=[t_g])
            eD = [gtile(14), gtile(15)]

            xpad = A.alloc(4100 * 2)
            acc = A.alloc(S_ * 4, F32)
            sqr = A.alloc(S_ * 2)
            rsb = [A.alloc(512 * 4, F32) for _ in range(2)]
            qT = A.alloc(S_ * 2)
            kT = A.alloc(S_ * 2)
            vb = [A.alloc(S_ * 2) for _ in range(2)]
            kd = [A.alloc(S_ * 2) for _ in range(2)]
            za = A.alloc(S_ * 2)
            oacc = A.alloc(S_ * 4, F32)
            oacc3 = v3(oacc, NT, 128)
            yst = A.alloc(S_ * 2)
            ssum = A.alloc(96 * 4, F32)
            RN = 4
            LA = 2
            TTr = [[A.alloc(512, F32) for _ in range(RN)] for _ in range(2)]
            QKr = [[A.alloc(256) for _ in range(RN)] for _ in range(2)]
            diagG = [A.alloc(512, F32) for _ in range(2)]
            t1 = [A.alloc(512, F32) for _ in range(2)]
            t2 = [A.alloc(512, F32) for _ in range(2)]
            dec = [A.alloc(512, F32) for _ in range(2)]
            decT = [A.alloc(256) for _ in range(2)]
            pair = [[A.alloc(1024, F32) for _ in range(2)] for _ in range(2)]
            X = [[A.alloc(512, F32) for _ in range(2)] for _ in range(2)]
            rhs = [A.alloc(512, F32) for _ in range(2)]
            vnew = [A.alloc(256) for _ in range(2)]
            tmp = [A.alloc(512, F32) for _ in range(2)]
            tmp2 = [A.alloc(512, F32) for _ in range(2)]
            S32 = [A.alloc(512, F32) for _ in range(2)]
            Sbf = [A.alloc(256) for _ in range(2)]
            tk = lambda: Tok()
            t_xp, t_acc, t_sqr, t_qT, t_kT, t_za, t_yst, t_ss = [Tok() for _ in range(8)]
            t_rs = [tk(), tk()]
            t_vb = [tk(), tk()]
            t_kd = [tk(), tk()]
            t_oacc = [tk() for _ in range(NT)]
            t_TT = [[tk() for _ in range(RN)] for _ in range(2)]
            t_QK = [[tk() for _ in range(RN)] for _ in range(2)]
            t_dg, t_t1, t_t2, t_dec, t_decT, t_rhs, t_vnew, t_tmp, t_tmp2, t_S32, t_S = [[tk(), tk()] for _ in range(11)]
            t_pair = [[tk(), tk()], [tk(), tk()]]
            t_X = [[tk(), tk()], [tk(), tk()]]
            t_kk, t_qk, t_gb, t_mt, t_ppq, t_xpq, t_ks, t_vn, t_o1, t_o2 = [[tk(), tk()] for _ in range(10)]
            KK = [banks[d][:, 0:128] for d in range(2)]
            QKT = [banks[d][:, 128:256] for d in range(2)]
            Gb = [banks[d][:, 256:384] for d in range(2)]
            MT = [banks[d][:, 384:512] for d in range(2)]
            PPQ = [banks[2 + d][:, 0:256] for d in range(2)]
            XPQ = [banks[2 + d][:, 256:384] for d in range(2)]
            KS = [banks[4 + d][:, 0:128] for d in range(2)]
            VN = [banks[4 + d][:, 128:256] for d in range(2)]
            O1 = [banks[4 + d][:, 256:384] for d in range(2)]
            O2 = [banks[4 + d][:, 384:512] for d in range(2)]
            DS = [banks[6 + d][:, 0:128] for d in range(2)]
            PMm = [cst[:, K_PMF:K_PMF + 128], cst[:, K_PMB:K_PMB + 128]]
            NMm = [cst[:, K_NMF:K_NMF + 128], cst[:, K_NMB:K_NMB + 128]]

            op(V, lambda h: h.memset(xpad[:, 0:2], 0.0), w=[t_xp])
            op(V, lambda h: h.memset(xpad[:, 4098:4100], 0.0), w=[t_xp])

            for h_ in range(8):
                def tr_scaled(srcT, t_src, dst, t_dst, sc):
                    for cg in range(4):
                        bk = 6 + (cg % 2)
                        pbk = banks[bk][:, :].bitcast(BF16)
                        for j in range(8):
                            c = cg * 8 + j
                            op(T, lambda h, j=j, c=c, pbk=pbk: h.transpose(out=pbk[:, j * 128:(j + 1) * 128], in_=srcT[:, c * 128:(c + 1) * 128], identity=ident_bf),
                               r=[t_src, t_cbf], w=[pb[bk]])
                        for d in range(2):
                            scb = sc[d][:, cg * 8:(cg + 1) * 8, h_:h_ + 1].broadcast_to([128, 8, 128])
                            op(V, lambda h, d=d, cg=cg, pbk=pbk, scb=scb: h.tensor_tensor(out=v3(dst[d], NT, 128)[:, cg * 8:(cg + 1) * 8, :], in0=v3(pbk, 8, 128), in1=scb, op=ALU.mult),
                               r=[pb[bk], t_g], w=[t_dst[d]])

                for ti in range(3):
                    dma("sync", xpad[:, 2:4098], scr_fm[ti, h_], w=[t_xp])
                    base = P_CW + (ti * 8 + h_) * 5
                    op(V, lambda h, base=base: h.tensor_scalar(out=acc, in0=xpad[:, 0:4096], scalar1=pl[:, base:base + 1], scalar2=None, op0=ALU.mult), r=[t_xp, t_prm], w=[t_acc])
                    for k in range(1, 5):
                        op(V, lambda h, base=base, k=k: h.scalar_tensor_tensor(out=acc, in0=xpad[:, k:k + 4096], scalar=pl[:, base + k:base + k + 1], in1=acc, op0=ALU.mult, op1=ALU.add),
                           r=[t_xp, t_prm], w=[t_acc])
                    if ti == 2:
                        op(AC, lambda h: h.activation(out=sqr, in_=acc, func=AF.Silu), r=[t_acc], w=[t_sqr])
                        tr_scaled(sqr, t_sqr, vb, t_vb, beta)
                    else:
                        op(AC, lambda h: h.activation(out=acc, in_=acc, func=AF.Silu), w=[t_acc])
                        op(AC, lambda h: h.activation(out=sqr, in_=acc, func=AF.Square), r=[t_acc], w=[t_sqr])
                        dstT, t_dst = (qT, t_qT) if ti == 0 else (kT, t_kT)
                        for tt in range(8):
                            bk = 6 + (tt % 2)
                            tsl = slice(tt * 512, (tt + 1) * 512)
                            rs = rsb[tt % 2]
                            trs = t_rs[tt % 2]
                            op(T, lambda h, bk=bk, tsl=tsl: h.matmul(banks[bk][:, :], lhsT=ones_bf, rhs=sqr[:, tsl], start=True, stop=True), r=[t_sqr, t_cbf], w=[pb[bk]])
                            op(AC, lambda h, bk=bk, rs=rs, ti=ti: h.activation(out=rs, in_=banks[bk][:, :], func=AF.Sqrt, scale=(128.0 if ti == 0 else 1.0),
                                                                        bias=cc(CC_DKEPS if ti == 0 else CC_EPS)), r=[pb[bk], t_ccol], w=[trs])
                            op(V, lambda h, rs=rs: h.reciprocal(out=rs, in_=rs), w=[trs])
                            op(V, lambda h, rs=rs, tsl=tsl, dstT=dstT: h.tensor_tensor(out=dstT[:, tsl], in0=acc[:, tsl], in1=rs, op=ALU.mult), r=[t_acc, trs], w=[t_dst])
                        if ti == 1:
                            tr_scaled(kT, t_kT, kd, t_kd, eD)
                dma("sync", za, scr_tm[1, h_], w=[t_za])
                op(G, lambda h: h.memset(oacc, 0.0), w=t_oacc)
                for d in range(2):
                    op(G, lambda h, d=d: h.memset(S32[d], 0.0), w=[t_S32[d]])
                    op(G, lambda h, d=d: h.memset(Sbf[d], 0.0), w=[t_S[d]])

                def prep(i, hh=h_):
                    s = i % RN
                    cs = (i, NT - 1 - i)
                    for d in range(2):
                        c = cs[d]
                        kc_ = kT[:, c * 128:(c + 1) * 128]
                        qc_ = qT[:, c * 128:(c + 1) * 128]
                        op(T, lambda h, d=d, kc_=kc_: h.matmul(KK[d], lhsT=kc_, rhs=kc_, start=True, stop=True), r=[t_kT], w=[pb[d]])
                        op(T, lambda h, d=d, kc_=kc_, qc_=qc_: h.matmul(QKT[d], lhsT=kc_, rhs=qc_, start=True, stop=True), r=[t_kT, t_qT], w=[pb[d]])
                        op(G, lambda h, d=d, c=c: h.tensor_scalar(out=diagG[d], in0=ident_f, scalar1=Gc[d][:, c, hh:hh + 1], scalar2=None, op0=ALU.mult), r=[t_g, t_cst], w=[t_dg[d]])
                        op(T, lambda h, d=d: h.matmul(Gb[d], lhsT=ones_f, rhs=diagG[d], start=True, stop=True), r=[t_dg[d], t_cst], w=[pb[d]])
                    for d in range(2):
                        c = cs[d]
                        Gcol = Gc[d][:, c, hh:hh + 1]
                        op(V, lambda h, d=d, Gcol=Gcol: h.scalar_tensor_tensor(out=t1[d], in0=Gb[d], scalar=Gcol, in1=PMm[d], op0=ALU.subtract, op1=ALU.add), r=[t_g, t_cst], w=[t_t1[d], pb[d]])
                        op(AC, lambda h, d=d: h.activation(out=dec[d], in_=t1[d], func=AF.Exp, scale=-1.0), r=[t_t1[d]], w=[t_dec[d]])
                        op(V, lambda h, d=d, Gcol=Gcol: h.scalar_tensor_tensor(out=t2[d], in0=Gb[d], scalar=Gcol, in1=NMm[d], op0=ALU.subtract, op1=ALU.add), r=[t_g, t_cst], w=[t_t2[d], pb[d]])
                        op(AC, lambda h, d=d: h.activation(out=decT[d], in_=t2[d], func=AF.Exp), r=[t_t2[d]], w=[t_decT[d]])
                        op(V, lambda h, d=d, c=c: h.scalar_tensor_tensor(out=pair[d][0][:, 0:128], in0=KK[d], scalar=nb[d][:, c, hh:hh + 1], in1=dec[d], op0=ALU.mult, op1=ALU.mult),
                           r=[t_dec[d], t_g], w=[t_pair[d][0], pb[d]])
                        op(V, lambda h, d=d, s=s: h.tensor_tensor(out=QKr[d][s], in0=QKT[d], in1=decT[d], op=ALU.mult), r=[t_decT[d]], w=[t_QK[d][s], pb[d]])
                    for d in range(2):
                        op(T, lambda h, d=d: h.transpose(out=MT[d], in_=pair[d][0][:, 0:128], identity=ident_f), r=[t_pair[d][0], t_cst], w=[pb[d]])
                        op(AC, lambda h, d=d: h.copy(out=pair[d][0][:, 128:256], in_=MT[d]), w=[t_pair[d][0], pb[d]])
                        op(G, lambda h, d=d: h.tensor_tensor(out=X[d][0], in0=pair[d][0][:, 128:256], in1=ident_f, op=ALU.add), r=[t_pair[d][0], t_cst], w=[t_X[d][0]])
                    for lvl in range(1, 7):
                        a = (lvl - 1) % 2
                        b = lvl % 2
                        n = 256 if lvl < 6 else 128
                        for d in range(2):
                            P_ = pair[d][a][:, 0:128]
                            PT_ = pair[d][a][:, 128:256]
                            op(T, lambda h, d=d, P_=P_, PT_=PT_: h.matmul(PPQ[d][:, 0:128], lhsT=PT_, rhs=P_, start=True, stop=True), r=[t_pair[d][a]], w=[pb[2 + d]])
                            if lvl < 6:
                                op(T, lambda h, d=d, P_=P_, PT_=PT_: h.matmul(PPQ[d][:, 128:256], lhsT=P_, rhs=PT_, start=True, stop=True), r=[t_pair[d][a]], w=[pb[2 + d]])
                            op(AC, lambda h, d=d, b=b, n=n: h.copy(out=pair[d][b][:, 0:n], in_=PPQ[d][:, 0:n]), w=[t_pair[d][b], pb[2 + d]])
                        for d in range(2):
                            dst, tdst = (X[d][b], t_X[d][b]) if lvl < 6 else (TTr[d][s], t_TT[d][s])
                            op(T, lambda h, d=d, a=a, b=b: h.matmul(XPQ[d], lhsT=pair[d][b][:, 0:128], rhs=X[d][a], start=True, stop=True), r=[t_pair[d][b], t_X[d][a]], w=[pb[2 + d]])
                            op(V, lambda h, d=d, a=a, dst=dst: h.tensor_tensor(out=dst, in0=XPQ[d], in1=X[d][a], op=ALU.add), r=[t_X[d][a]], w=[tdst, pb[2 + d]])

                def chain(i, hh=h_):
                    s = i % RN
                    cs = (i, NT - 1 - i)
                    for d in range(2):
                        c = cs[d]
                        op(T, lambda h, d=d, c=c: h.matmul(KS[d], lhsT=kT[:, c * 128:(c + 1) * 128], rhs=Sbf[d], start=True, stop=True), r=[t_kT, t_S[d]], w=[pb[4 + d]])
                    for d in range(2):
                        c = cs[d]
                        op(V, lambda h, d=d, c=c: h.scalar_tensor_tensor(out=rhs[d], in0=KS[d], scalar=nbeG[d][:, c, hh:hh + 1], in1=v3(vb[d], NT, 128)[:, c, :], op0=ALU.mult, op1=ALU.add),
                           r=[t_g, t_vb[d]], w=[t_rhs[d], pb[4 + d]])
                    for d in range(2):
                        op(T, lambda h, d=d, s=s: h.matmul(VN[d], lhsT=TTr[d][s], rhs=rhs[d], start=True, stop=True), r=[t_TT[d][s], t_rhs[d]], w=[pb[4 + d]])
                    for d in range(2):
                        op(AC, lambda h, d=d: h.copy(out=vnew[d], in_=VN[d]), w=[t_vnew[d], pb[4 + d]])
                    for d in range(2):
                        c = cs[d]
                        op(T, lambda h, d=d, c=c: h.matmul(O1[d], lhsT=qT[:, c * 128:(c + 1) * 128], rhs=Sbf[d], start=True, stop=True), r=[t_qT, t_S[d]], w=[pb[4 + d]])
                        op(T, lambda h, d=d, s=s: h.matmul(O2[d], lhsT=QKr[d][s], rhs=vnew[d], start=True, stop=True), r=[t_QK[d][s], t_vnew[d]], w=[pb[4 + d]])
                        op(T, lambda h, d=d, c=c: h.matmul(DS[d], lhsT=v3(kd[d], NT, 128)[:, c, :], rhs=vnew[d], start=True, stop=True), r=[t_kd[d], t_vnew[d]], w=[pb[6 + d]])
                    for d in range(2):
                        c = cs[d]
                        op(V, lambda h, d=d, c=c: h.scalar_tensor_tensor(out=S32[d], in0=S32[d], scalar=eGt[d][:, c, hh:hh + 1], in1=DS[d], op0=ALU.mult, op1=ALU.add),
                           r=[t_g], w=[t_S32[d], pb[6 + d]])
                        op(G, lambda h, d=d: h.tensor_copy(out=Sbf[d], in_=S32[d]), r=[t_S32[d]], w=[t_S[d]])
                        op(AC, lambda h, d=d, c=c: h.activation(out=tmp[d], in_=O1[d], func=AF.Copy, scale=eG[d][:, c, hh:hh + 1]), r=[t_g], w=[t_tmp[d], pb[4 + d]])
                        op(V, lambda h, d=d: h.tensor_tensor(out=tmp2[d], in0=O2[d], in1=tmp[d], op=ALU.add), r=[t_tmp[d]], w=[t_tmp2[d], pb[4 + d]])
                        op(G, lambda h, d=d, c=c: h.tensor_tensor(out=oacc3[:, c, :], in0=oacc3[:, c, :], in1=tmp2[d], op=ALU.add), r=[t_tmp2[d]], w=[t_oacc[c]])

                for i in range(LA):
                    prep(i)
                for i in range(NT):
                    if i + LA < NT:
                        prep(i + LA)
                    chain(i)

                op(AC, lambda h: h.activation(out=acc, in_=oacc, func=AF.Square), r=t_oacc, w=[t_acc])
                op(V, lambda h: h.tensor_reduce(out=ssum[:, 0:32], in_=v3(acc, NT, 128), axis=AX.X, op=ALU.add), r=[t_acc], w=[t_ss])
                op(AC, lambda h: h.activation(out=ssum[:, 32:64], in_=ssum[:, 0:32], func=AF.Sqrt, scale=1.0 / 128, bias=cc(CC_EPS)), r=[t_ccol], w=[t_ss])
                op(V, lambda h: h.reciprocal(out=ssum[:, 64:96], in_=ssum[:, 32:64]), w=[t_ss])
                rb = ssum[:, 64:96].unsqueeze(2).broadcast_to([128, NT, 128])
                op(V, lambda h, rb=rb: h.tensor_tensor(out=oacc3, in0=oacc3, in1=rb, op=ALU.mult), r=[t_ss], w=t_oacc)
                gw = pl[:, P_GNW:P_GNW + 128].unsqueeze(1).broadcast_to([128, NT, 128])
                op(G, lambda h, gw=gw: h.tensor_tensor(out=oacc3, in0=oacc3, in1=gw, op=ALU.mult), r=[t_prm], w=t_oacc)
                op(V, lambda h: h.tensor_tensor(out=sqr, in0=oacc, in1=za, op=ALU.mult), r=t_oacc + [t_za], w=[t_sqr])
                for cg in range(4):
                    bk = 6 + (cg % 2)
                    pbk = banks[bk][:, :].bitcast(BF16)
                    for j in range(8):
                        c = cg * 8 + j
                        op(T, lambda h, j=j, c=c, pbk=pbk: h.transpose(out=pbk[:, j * 128:(j + 1) * 128], in_=sqr[:, c * 128:(c + 1) * 128], identity=ident_bf),
                           r=[t_sqr, t_cbf], w=[pb[bk]])
                    op(AC, lambda h, cg=cg, pbk=pbk: h.copy(out=yst[:, cg * 1024:(cg + 1) * 1024], in_=pbk), r=[pb[bk]], w=[t_yst])
                dma("sync", scr_y[0, h_], yst, r=[t_yst])

        def phase_C(l, pl, lam_init):
            T, AC, G = "tensor", "scalar", "gpsimd"
            om = 1.0 - lam_init
            ltmp = A.alloc(128 * 4, F32)
            lp = v3(pl[:, P_LAM:P_LAM + 256], 4, 64)
            op(V, lambda h: h.tensor_tensor(out=ltmp[:, 0:64], in0=lp[:, 0, :], in1=lp[:, 1, :], op=ALU.mult), r=[t_prm], w=[t_lam])
            op(V, lambda h: h.tensor_tensor(out=ltmp[:, 64:128], in0=lp[:, 2, :], in1=lp[:, 3, :], op=ALU.mult), r=[t_prm], w=[t_lam])
            op(V, lambda h: h.tensor_reduce(out=lamc[:, 0:2], in_=v3(ltmp, 2, 64), axis=AX.X, op=ALU.add), w=[t_lam])
            op(AC, lambda h: h.activation(out=lamc[:, 2:4], in_=lamc[:, 0:2], func=AF.Exp), w=[t_lam])
            op(V, lambda h: h.tensor_tensor(out=lamc[:, 4:5], in0=lamc[:, 2:3], in1=lamc[:, 3:4], op=ALU.subtract), w=[t_lam])
            op(V, lambda h: h.tensor_scalar(out=lamc[:, 5:6], in0=lamc[:, 4:5], scalar1=float(lam_init), scalar2=-1.0, op0=ALU.add, op1=ALU.mult), w=[t_lam])
            op(V, lambda h: h.memset(lamc[:, 6:7], EPS / (om * om)), w=[t_lam])
            nlam = lamc[:, 5:6]
            sl_scale = 1.0 / (128.0 * om * om)

            qraw = A.alloc(S_ * 2)
            kraw = A.alloc(S_ * 2)
            qT = A.alloc(S_ * 2)
            kT = A.alloc(S_ * 2)
            vb1 = A.alloc(NT * 130 * 2)
            vb13 = v3(vb1, NT, 130)
            zb = A.alloc(S_ * 2)
            zb3 = v3(zb, NT, 128)
            yst = A.alloc(S_ * 2)
            NE = 3
            Eb = [[A.alloc(512 * 2) for _ in range(NE)] for _ in range(2)]
            rt1 = A.alloc(512 * 4, F32)
            rt2 = A.alloc(512 * 4, F32)
            osb = [A.alloc(8 * 130 * 4, F32) for _ in range(2)]
            a0 = [A.alloc(128 * 4, F32) for _ in range(2)]
            aa = [A.alloc(128 * 4, F32) for _ in range(2)]
            ytok = [A.alloc(128 * 2) for _ in range(2)]
            jk = [A.alloc(128 * 2) for _ in range(2)]
            rc = [A.alloc(8 * 4, F32) for _ in range(2)]
            t_qr, t_kr, t_qT, t_kT, t_v, t_z, t_yst, t_rt1, t_rt2 = [Tok() for _ in range(9)]
            t_E = [[Tok() for _ in range(NE)] for _ in range(2)]
            t_osb = [Tok(), Tok()]
            t_ep = [Tok(), Tok()]
            op(V, lambda h: h.memset(vb13[:, :, 128:130], 1.0), w=[t_v])
            slwb = pl[:, P_SLW:P_SLW + 128].unsqueeze(1).broadcast_to([128, NT, 128])
            OG = []
            for gi in range(8):
                OG.append(banks[4 + gi // 3][:, (gi % 3) * 130:(gi % 3) * 130 + 129])

            for h_ in range(8):
                dma("sync", qraw, scr_fm[3, h_], w=[t_qr])
                dma("sync", kraw, scr_fm[4, h_], w=[t_kr])
                dma("sync", vb13[:, :, 0:128], scr_tm[0, h_].rearrange("p (a b) -> p a b", a=NT, b=128), w=[t_v])
                dma("sync", zb, scr_tm[2, h_], w=[t_z])
                op(G, lambda h: h.tensor_tensor(out=zb3, in0=zb3, in1=slwb, op=ALU.mult), r=[t_prm], w=[t_z])
                for raw, t_raw, dstT, t_dst in ((qraw, t_qr, qT, t_qT), (kraw, t_kr, kT, t_kT)):
                    for tt in range(8):
                        tsl = slice(tt * 512, (tt + 1) * 512)
                        op(T, lambda h, raw=raw, tsl=tsl: h.matmul(banks[7][:, :], lhsT=perm_bf, rhs=raw[:, tsl], start=True, stop=True), r=[t_raw, t_cbf], w=[pb[7]])
                        op(V, lambda h, tsl=tsl: h.tensor_tensor(out=rt1, in0=banks[7][:, :], in1=St[:, tsl], op=ALU.mult), r=[pb[7], t_rope], w=[t_rt1])
                        op(G, lambda h, raw=raw, tsl=tsl: h.tensor_tensor(out=rt2, in0=raw[:, tsl], in1=Ct[:, tsl], op=ALU.mult), r=[t_raw, t_rope], w=[t_rt2])
                        op(V, lambda h, dstT=dstT, tsl=tsl: h.tensor_tensor(out=dstT[:, tsl], in0=rt1, in1=rt2, op=ALU.add), r=[t_rt1, t_rt2], w=[t_dst])
                ne = 0
                for qt in range(8):
                    qsl = slice(qt * 512, (qt + 1) * 512)
                    for kt in range(NT):
                        ksl = slice(kt * 128, (kt + 1) * 128)
                        par = kt % 2
                        es = ne % NE
                        ne += 1
                        for sm in range(2):
                            bk = sm * 2 + par
                            rows = slice(sm * 64, (sm + 1) * 64)
                            op(T, lambda h, bk=bk, rows=rows, ksl=ksl, qsl=qsl: h.matmul(banks[bk][:, :], lhsT=kT[rows, ksl], rhs=qT[rows, qsl], start=True, stop=True),
                               r=[t_kT, t_qT], w=[pb[bk]])
                        for sm in range(2):
                            bk = sm * 2 + par
                            op(AC, lambda h, bk=bk, sm=sm, es=es: h.activation(out=Eb[sm][es], in_=banks[bk][:, :], func=AF.Exp, scale=0.125), r=[pb[bk]], w=[t_E[sm][es]])
                        for sm in range(2):
                            for sub in range(4):
                                gi = sub * 2 + sm
                                first_in_bank = gi in (0, 4, 6)
                                op(T, lambda h, gi=gi, sm=sm, es=es, sub=sub, kt=kt, fb=first_in_bank: h.matmul(
                                    OG[gi], lhsT=Eb[sm][es][:, sub * 128:(sub + 1) * 128], rhs=vb13[:, kt, 0:129],
                                    start=(kt == 0 and fb), stop=(kt == NT - 1), skip_group_check=True),
                                   r=[t_E[sm][es], t_v], w=[pb[4 + gi // 3]])
                    ob = osb[qt % 2]
                    tob = t_osb[qt % 2]
                    op(V, lambda h, ob=ob: h.tensor_copy(out=ob[:, 0:390], in_=banks[4][:, 0:390]), r=[pb[4]], w=[tob])
                    op(AC, lambda h, ob=ob: h.copy(out=ob[:, 390:780], in_=banks[5][:, 0:390]), r=[pb[5]], w=[tob])
                    op(V, lambda h, ob=ob: h.tensor_copy(out=ob[:, 780:1040], in_=banks[6][:, 0:260]), r=[pb[6]], w=[tob])
                    b7bf = banks[7][:, :].bitcast(BF16)
                    for sub in range(4):
                        e = sub % 2
                        c = qt * 4 + sub
                        o0 = ob[:, (sub * 2) * 130:(sub * 2) * 130 + 130]
                        o1 = ob[:, (sub * 2 + 1) * 130:(sub * 2 + 1) * 130 + 130]
                        rc_, a0_, aa_, yt_, jk_, tep = rc[e], a0[e], aa[e], ytok[e], jk[e], t_ep[e]
                        op(V, lambda h, rc_=rc_, o0=o0: h.reciprocal(out=rc_[:, 0:1], in_=o0[:, 128:129]), r=[tob], w=[tep])
                        op(V, lambda h, rc_=rc_, o1=o1: h.reciprocal(out=rc_[:, 1:2], in_=o1[:, 128:129]), r=[tob], w=[tep])
                        op(V, lambda h, rc_=rc_: h.tensor_tensor(out=rc_[:, 2:3], in0=rc_[:, 1:2], in1=nlam, op=ALU.mult), r=[t_lam], w=[tep])
                        op(V, lambda h, rc_=rc_, a0_=a0_, o0=o0: h.tensor_scalar(out=a0_, in0=o0[:, 0:128], scalar1=rc_[:, 0:1], scalar2=None, op0=ALU.mult), r=[tob], w=[tep])
                        op(V, lambda h, rc_=rc_, a0_=a0_, aa_=aa_, o1=o1: h.scalar_tensor_tensor(out=aa_, in0=o1[:, 0:128], scalar=rc_[:, 2:3], in1=a0_, op0=ALU.mult, op1=ALU.add), r=[tob], w=[tep])
                        op(AC, lambda h, rc_=rc_, aa_=aa_, jk_=jk_: h.activation(out=jk_, in_=aa_, func=AF.Square, accum_out=rc_[:, 3:4]), w=[tep])
                        op(AC, lambda h, rc_=rc_: h.activation(out=rc_[:, 4:5], in_=rc_[:, 3:4], func=AF.Sqrt, scale=sl_scale, bias=lamc[:, 6:7]), r=[t_lam], w=[tep])
                        op(V, lambda h, rc_=rc_: h.reciprocal(out=rc_[:, 5:6], in_=rc_[:, 4:5]), w=[tep])
                        op(V, lambda h, rc_=rc_, aa_=aa_, yt_=yt_, c=c: h.scalar_tensor_tensor(out=yt_, in0=aa_, scalar=rc_[:, 5:6], in1=zb3[:, c, :], op0=ALU.mult, op1=ALU.mult), r=[t_z], w=[tep])
                        op(T, lambda h, yt_=yt_, sub=sub: h.transpose(out=b7bf[:, sub * 128:(sub + 1) * 128], in_=yt_, identity=ident_bf), r=[tep, t_cbf], w=[pb[7]])
                    op(AC, lambda h, qsl=qsl: h.copy(out=yst[:, qsl], in_=b7bf[:, 0:512]), r=[pb[7]], w=[t_yst])
                dma("sync", scr_y[1, h_], yst, r=[t_yst])

        for l in range(n_layers):
            pl = prm[:, l * NPRM:(l + 1) * NPRM]
            lam_init = 0.8 - 0.6 * math.exp(-0.3 * l)
            m0 = A.mark()
            wv = w_in_d[l].rearrange("(kc p) n -> p kc n", p=128)
            wb = [(A.alloc(8 * 512 * 2), Tok()) for _ in range(3)]
            stg = [(A.alloc(S_ * 2), Tok()) for _ in range(2)]
            stT = [(A.alloc(4 * S_ * 2), Tok()) for _ in range(1)]
            nwl = 0
            nst = 0
            nbk = 0
            fm_list = [(C_QA, 0, None), (C_KA, 1, None), (C_VA, 2, None), (C_QB, 3, None), (C_KB, 4, None),
                       (C_GA, 5, AF.Sigmoid), (C_GB, 6, AF.Sigmoid)]
            nev = 0
            for c0, ti, fn_ in fm_list:
                for g in range(2):
                    wt, twt = wb[nwl % 3]
                    nwl += 1
                    wt3 = v3(wt, 8, 512)
                    dma("gpsimd", wt3, wv[:, :, c0 + g * 512:c0 + (g + 1) * 512], w=[twt])
                    for hb in range(4):
                        st, tst = stg[nst % 2]
                        nst += 1
                        for tt in range(8):
                            bk = nbk % 4
                            nbk += 1
                            for kc in range(8):
                                op("tensor", lambda h, bk=bk, wt3=wt3, hb=hb, kc=kc, tt=tt: h.matmul(
                                    banks[bk][:, :], lhsT=wt3[:, kc, hb * 128:(hb + 1) * 128], rhs=hT3[:, kc, tt * 512:(tt + 1) * 512],
                                    start=(kc == 0), stop=(kc == 7)), r=[twt] + t_hT[tt * 4:tt * 4 + 4], w=[pb[bk]])
                            dst = st[:, tt * 512:(tt + 1) * 512]
                            if fn_ is not None:
                                op("scalar", lambda h, bk=bk, dst=dst, fn_=fn_: h.activation(out=dst, in_=banks[bk][:, :], func=fn_), r=[pb[bk]], w=[tst])
                            elif nev % 2 == 0:
                                op("scalar", lambda h, bk=bk, dst=dst: h.copy(out=dst, in_=banks[bk][:, :]), r=[pb[bk]], w=[tst])
                            else:
                                op("vector", lambda h, bk=bk, dst=dst: h.tensor_copy(out=dst, in_=banks[bk][:, :]), r=[pb[bk]], w=[tst])
                            nev += 1
                        dma("sync", scr_fm[ti, g * 4 + hb], st, r=[tst])
            tm_list = [(C_VB, 0, None), (C_ZA, 1, AF.Silu), (C_ZB, 2, AF.Silu)]
            for c0, ti, fn_ in tm_list:
                for g in range(2):
                    wt, twt = wb[nwl % 3]
                    nwl += 1
                    wt3 = v3(wt, 8, 512)
                    dma("gpsimd", wt3, wv[:, :, c0 + g * 512:c0 + (g + 1) * 512], w=[twt])
                    st, tst = stT[0]
                    st4 = st.rearrange("p (a b c) -> p a b c", a=4, b=NT, c=128)
                    for t in range(NT):
                        bk = nbk % 4
                        nbk += 1
                        for kc in range(8):
                            op("tensor", lambda h, bk=bk, wt3=wt3, kc=kc, t=t: h.matmul(
                                banks[bk][:, :], lhsT=hT3[:, kc, t * 128:(t + 1) * 128], rhs=wt3[:, kc, :],
                                start=(kc == 0), stop=(kc == 7)), r=[twt, t_hT[t]], w=[pb[bk]])
                        dst = st4[:, :, t, :]
                        src = v3(banks[bk][:, :], 4, 128)
                        if fn_ is not None:
                            op("scalar", lambda h, dst=dst, src=src, fn_=fn_: h.activation(out=dst, in_=src, func=fn_), r=[pb[bk]], w=[tst])
                        elif t % 2 == 0:
                            op("scalar", lambda h, dst=dst, src=src: h.copy(out=dst, in_=src), r=[pb[bk]], w=[tst])
                        else:
                            op("vector", lambda h, dst=dst, src=src: h.tensor_copy(out=dst, in_=src), r=[pb[bk]], w=[tst])
                    for hb in range(4):
                        dma("sync", scr_tm[ti, g * 4 + hb], st[:, hb * S_:(hb + 1) * S_], r=[tst])
            wt, twt = wb[nwl % 3]
            nwl += 1
            wt3 = v3(wt, 8, 512)
            dma("gpsimd", wt3[:, :, 0:32], wv[:, :, C_AB:C_AB + 32], w=[twt])
            abt3 = v3(abt, NT, 32)
            for t in range(NT):
                bk = nbk % 4
                nbk += 1
                for kc in range(8):
                    op("tensor", lambda h, bk=bk, wt3=wt3, kc=kc, t=t: h.matmul(
                        banks[bk][:, 0:32], lhsT=hT3[:, kc, t * 128:(t + 1) * 128], rhs=wt3[:, kc, 0:32],
                        start=(kc == 0), stop=(kc == 7)), r=[twt, t_hT[t]], w=[pb[bk]])
                op("vector", lambda h, bk=bk, t=t: h.tensor_copy(out=abt3[:, t, :], in_=banks[bk][:, 0:32]), r=[pb[bk]], w=[t_abt])
            SC.barrier()
            A.reset(m0)
            if stop_after == "A":
                break

            A.reset(m_H)
            phase_B(l, pl)
            SC.barrier()
            if stop_after == "B":
                break
            A.reset(m_H)
            phase_C(l, pl, lam_init)
            SC.barrier()
            if stop_after == "C":
                break
            A.reset(m_W)
            last = (l == L_ - 1)
            dma("sync", nwb, nw_d[l + 1], w=[t_nwb])
            wts = []
            for wd in (w_pa_d, w_pb_d, w_out_d):
                wt = A.alloc(8 * D_ * 2)
                tw = Tok()
                wt3 = v3(wt, 8, D_)
                for hh in range(2):
                    dma("gpsimd", wt3[:, :, hh * 512:(hh + 1) * 512], wd[l].rearrange("(kc p) n -> p kc n", p=128)[:, :, hh * 512:(hh + 1) * 512], w=[tw])
                wts.append((wt3, tw))
            (wpa3, twpa), (wpb3, twpb), (wo3, two) = wts
            yb_ = [(A.alloc(8 * 512 * 2), A.alloc(8 * 512 * 2), Tok()) for _ in range(1)]
            gb_ = [(A.alloc(2 * 512 * 2), Tok()) for _ in range(2)]
            mg = A.alloc(8 * 512 * 2)
            mg3 = v3(mg, 8, 512)
            t_mg = Tok()
            tt1 = A.alloc(512 * 4, F32)
            tt2 = A.alloc(512 * 4, F32)
            t_tt = Tok()
            xts = [(A.alloc(D_ * 4, F32), Tok()) for _ in range(2)]
            ntm = [norm_tmp() for _ in range(1)]
            ng = 0
            for tt in range(8):
                ya, yb, ty = yb_[0]
                ya3, yb3 = v3(ya, 8, 512), v3(yb, 8, 512)
                dma("sync", ya3, scr_y[0][:, :, tt * 512:(tt + 1) * 512].rearrange("h p t -> p h t"), w=[ty])
                dma("sync", yb3, scr_y[1][:, :, tt * 512:(tt + 1) * 512].rearrange("h p t -> p h t"), w=[ty])
                for m in range(8):
                    gg, tg = gb_[ng % 2]
                    ng += 1
                    dma("sync", gg[:, 0:512], scr_fm[5, m][:, tt * 512:(tt + 1) * 512], w=[tg])
                    dma("sync", gg[:, 512:1024], scr_fm[6, m][:, tt * 512:(tt + 1) * 512], w=[tg])
                    for kc in range(8):
                        op("tensor", lambda h, kc=kc, m=m, ya3=ya3: h.matmul(banks[0][:, :], lhsT=wpa3[:, kc, m * 128:(m + 1) * 128], rhs=ya3[:, kc, :],
                                                                 start=(kc == 0), stop=(kc == 7)), r=[twpa, ty], w=[pb[0]])
                    for kc in range(8):
                        op("tensor", lambda h, kc=kc, m=m, yb3=yb3: h.matmul(banks[1][:, :], lhsT=wpb3[:, kc, m * 128:(m + 1) * 128], rhs=yb3[:, kc, :],
                                                                 start=(kc == 0), stop=(kc == 7)), r=[twpb, ty], w=[pb[1]])
                    op(V, lambda h, gg=gg: h.tensor_tensor(out=tt1, in0=banks[0][:, :], in1=gg[:, 0:512], op=ALU.mult), r=[pb[0], tg], w=[t_tt])
                    op(V, lambda h, gg=gg: h.tensor_tensor(out=tt2, in0=banks[1][:, :], in1=gg[:, 512:1024], op=ALU.mult), r=[pb[1], tg], w=[t_tt])
                    op("gpsimd", lambda h, m=m: h.tensor_tensor(out=mg3[:, m, :], in0=tt1, in1=tt2, op=ALU.add), r=[t_tt], w=[t_mg])
                for sub in range(4):
                    t = tt * 4 + sub
                    xt, tx = xts[t % 2]
                    dma("sync", xt, scr_x[t * 128:(t + 1) * 128, :], w=[tx])
                    for nh in range(2):
                        bk = 2 + nh
                        for kc in range(8):
                            op("tensor", lambda h, kc=kc, nh=nh, sub=sub, bk=bk: h.matmul(
                                banks[bk][:, :], lhsT=mg3[:, kc, sub * 128:(sub + 1) * 128], rhs=wo3[:, kc, nh * 512:(nh + 1) * 512],
                                start=(kc == 0), stop=(kc == 7)), r=[two, t_mg], w=[pb[bk]])
                        op(V, lambda h, nh=nh, bk=bk, xt=xt: h.tensor_tensor(out=xt[:, nh * 512:(nh + 1) * 512], in0=banks[bk][:, :], in1=xt[:, nh * 512:(nh + 1) * 512], op=ALU.add),
                           r=[pb[bk]], w=[tx])
                    if not last:
                        dma("sync", scr_x[t * 128:(t + 1) * 128, :], xt, r=[tx])
                    norm_tile(xt, tx, t, ntm[0], last, 4 + (t % 2))
            SC.barrier()
            A.reset(m_W)

        SC.barrier()

        with nc.Block() as block:
            @block.sync
            def _(h):
                SC.replay("sync", h)

            @block.scalar
            def _(h):
                SC.replay("scalar", h)

            @block.vector
            def _(h):
                SC.replay("vector", h)

            @block.gpsimd
            def _(h):
                SC.replay("gpsimd", h)

            @block.tensor
            def _(h):
                SC.replay("tensor", h)
    return nc


def _host_consts():
    c = np.zeros((128, NCST), np.float32)
    p = np.arange(128)[:, None]
    f = np.arange(128)[None, :]
    c[:, K_ID:K_ID + 128] = (p == f)
    c[:, K_PMF:K_PMF + 128] = np.where(p > f, 0.0, BIGM)
    c[:, K_NMF:K_NMF + 128] = np.where(f >= p, 0.0, -BIGM)
    c[:, K_PMB:K_PMB + 128] = np.where(p < f, 0.0, BIGM)
    c[:, K_NMB:K_NMB + 128] = np.where(f <= p, 0.0, -BIGM)
    c[:, K_TRF:K_TRF + 128] = (p <= f)
    c[:, K_TRB:K_TRB + 128] = (p >= f)
    perm = np.zeros((128, 128), np.float32)
    for base in (0, 64):
        for d in range(8):
            perm[base + d + 8, base + d] = 1.0
            perm[base + d, base + d + 8] = 1.0
    c[:, K_PERM:K_PERM + 128] = perm
    c[:, K_ONE:K_ONE + 128] = 1.0
    inv_freq = (500000.0 ** (-(np.arange(0, 16, 2, dtype=np.float32) / np.float32(16)))).astype(np.float32)
    for base in (0, 64):
        for d in range(8):
            c[base + d, K_INVF] = inv_freq[d]
            c[base + d + 8, K_INVF] = inv_freq[d]
            c[base + d, K_SGN] = -1.0
            c[base + d + 8, K_SGN] = 1.0
    return c


def _host_params(inputs):
    prm = np.zeros((128, L_, NPRM), np.float32)
    for l in range(L_):
        cw = inputs["conv_w"][l]
        cwr = cw.reshape(5, 3, 8, 128).transpose(3, 1, 2, 0)
        prm[:, l, P_CW:P_CW + 120] = cwr.reshape(128, 120)
        prm[:, l, P_ALOG:P_ALOG + 16] = inputs["a_log"][l].reshape(1, 16)
        prm[:, l, P_DTB:P_DTB + 16] = inputs["dt_bias"][l].reshape(1, 16)
        prm[:, l, P_GNW:P_GNW + 128] = inputs["gdn_norm_w"][l].reshape(1, 128)
        prm[:, l, P_SLW:P_SLW + 128] = inputs["diff_subln_w"][l].reshape(1, 128)
        prm[:, l, P_LAM:P_LAM + 256] = inputs["diff_lambda"][l].reshape(1, 256)
    nw = np.zeros((L_ + 1, 128, D_), np.float32)
    for l in range(L_):
        nw[l] = inputs["norm_w"][l][None, :]
    nw[L_] = inputs["final_norm_w"][None, :]
    return prm.reshape(128, L_ * NPRM), nw


def _in_maps(inputs):
    cst = _host_consts()
    prm, nw = _host_params(inputs)
    f32 = lambda a: np.ascontiguousarray(np.asarray(a, dtype=np.float32))
    w_in, w_pa, w_pb, w_out = (f32(inputs[k]) for k in ("w_in", "w_pa", "w_pb", "w_out"))
    maps = []
    for b in range(8):
        maps.append({
            "x": f32(inputs["x"][b]),
            "pos": np.ascontiguousarray(np.broadcast_to(np.asarray(inputs["positions"][b], dtype=np.int32)[None, :], (128, S_))),
            "w_in": w_in, "w_pa": w_pa, "w_pb": w_pb, "w_out": w_out,
            "cst": cst, "prm": prm, "nw": nw,
        })
    return maps


def kernel(**inputs):
    nc = build()
    maps = _in_maps(inputs)
    res = run_bass_kernel_spmd(nc, maps, core_ids=list(range(8)))
    return np.stack([np.asarray(r["out"], dtype=np.float32) for r in res.results], axis=0)
```

```python
import math
import numpy as np
import concourse.bass as bass
import concourse.mybir as mybir
from concourse.bass_utils import run_bass_kernel_spmd

F32, BF16, I32 = mybir.dt.float32, mybir.dt.bfloat16, mybir.dt.int32
AF = mybir.ActivationFunctionType
ALU = mybir.AluOpType
AX = mybir.AxisListType

S_ = 4096
D_ = 1024
L_ = 4
NIN = 10272
NT = 32
EPS = 1e-6
C_QA, C_KA, C_VA, C_AB, C_ZA, C_QB, C_KB, C_VB, C_ZB, C_GA, C_GB = (
    0, 1024, 2048, 3072, 3104, 4128, 5152, 6176, 7200, 8224, 9248)
NPRM = 664
P_CW, P_ALOG, P_DTB, P_GNW, P_SLW, P_LAM = 0, 120, 136, 152, 280, 408
NCST = 9 * 128 + 2
K_ID, K_PMF, K_NMF, K_PMB, K_NMB, K_TRF, K_TRB, K_PERM, K_ONE = [i * 128 for i in range(9)]
K_INVF, K_SGN = 9 * 128, 9 * 128 + 1
TWO_PI = 2.0 * math.pi
TWO_PI_HI = 6.28125
TWO_PI_LO = TWO_PI - TWO_PI_HI
BIGM = 30000.0


class Tok:
    __slots__ = ("w", "r")

    def __init__(self):
        self.w = None
        self.r = {}


class Sched:
    ENG = ("tensor", "scalar", "vector", "gpsimd", "sync")

    def __init__(self, nc, stack):
        self.nc = nc
        self.sems = []
        self.E = {}
        for n in self.ENG:
            s = stack.enter_context(nc.semaphore("s_" + n))
            self.sems.append(s)
            self.E[n] = dict(si=len(self.sems) - 1, cnt=0, waited={}, prog=[], slots=[], nxt=0)
        for n in ("sync", "gpsimd", "scalar"):
            for k in range(8):
                s = stack.enter_context(nc.semaphore("d_%s%d" % (n, k)))
                self.sems.append(s)
                self.E[n]["slots"].append([len(self.sems) - 1, 0])
        self.nops = 0

    def _deps(self, e, r, w):
        need = {}
        for t in r:
            if t.w is not None:
                s, v = t.w
                if need.get(s, 0) < v:
                    need[s] = v
        for t in w:
            if t.w is not None:
                s, v = t.w
                if need.get(s, 0) < v:
                    need[s] = v
            for s, v in t.r.items():
                if need.get(s, 0) < v:
                    need[s] = v
        waits = []
        for s, v in need.items():
            if s == e["si"] and e is self.E["tensor"]:
                continue
            if e["waited"].get(s, 0) >= v:
                continue
            e["waited"][s] = v
            waits.append((s, v))
        return waits

    def _mark(self, me, r, w):
        s, v = me
        for t in r:
            if t.r.get(s, 0) < v:
                t.r[s] = v
        for t in w:
            t.w = me
            t.r = {}

    def op(self, en, fn, r=(), w=()):
        e = self.E[en]
        waits = self._deps(e, r, w)
        e["cnt"] += 1
        e["prog"].append((waits, fn, (e["si"], 1)))
        self._mark((e["si"], e["cnt"]), r, w)
        self.nops += 1

    def dma(self, en, out, in_, r=(), w=()):
        e = self.E[en]
        waits = self._deps(e, r, w)
        slot = e["slots"][e["nxt"] % len(e["slots"])]
        e["nxt"] += 1
        if slot[1] > 0 and e["waited"].get(slot[0], 0) < slot[1]:
            waits.append((slot[0], slot[1]))
            e["waited"][slot[0]] = slot[1]
        slot[1] += 16
        e["prog"].append((waits, (lambda h, o=out, i=in_: h.dma_start(out=o, in_=i)), (slot[0], 16)))
        self._mark((slot[0], slot[1]), r, w)
        self.nops += 1

    def barrier(self):
        tgt = []
        for n in self.ENG:
            e = self.E[n]
            if e["cnt"] > 0:
                tgt.append((e["si"], e["cnt"]))
            for sl in e["slots"]:
                if sl[1] > 0:
                    tgt.append((sl[0], sl[1]))
        for n in self.ENG:
            e = self.E[n]
            waits = []
            for s, v in tgt:
                if s == e["si"]:
                    continue
                if e["waited"].get(s, 0) >= v:
                    continue
                e["waited"][s] = v
                waits.append((s, v))
            if waits:
                e["prog"].append((waits, None, None))

    def replay(self, en, h):
        for waits, fn, inc in self.E[en]["prog"]:
            for s, v in waits:
                h.wait_ge(self.sems[s], v)
            if fn is not None:
                ins = fn(h)
                ins.then_inc(self.sems[inc[0]], inc[1])


class Arena:
    def __init__(self, big, nbytes):
        self.big = big
        self.cap = nbytes
        self.top = 0

    def mark(self):
        return self.top

    def reset(self, m):
        self.top = m

    def alloc(self, nbytes, dt=BF16):
        off = (self.top + 63) // 64 * 64
        assert off + nbytes <= self.cap, ("SBUF arena overflow", off, nbytes, self.cap)
        self.top = off + nbytes
        ap = self.big[:, off // 2:(off + nbytes) // 2]
        if dt is not BF16:
            ap = ap.bitcast(dt)
        return ap


def v3(ap, a, b):
    return ap.rearrange("p (a b) -> p a b", a=a, b=b)


def build(n_layers=L_, dbg=False, stop_after=None):
    from contextlib import ExitStack
    nc = bass.Bass("TRN2", target_bir_lowering=False)

    def din(name, shape, dt=F32):
        return nc.dram_tensor(name, shape, dt, kind="ExternalInput").ap()

    x_d = din("x", [S_, D_])
    pos_d = din("pos", [128, S_], I32)
    w_in_d = din("w_in", [L_, D_, NIN])
    w_pa_d = din("w_pa", [L_, D_, D_])
    w_pb_d = din("w_pb", [L_, D_, D_])
    w_out_d = din("w_out", [L_, D_, D_])
    cst_d = din("cst", [128, NCST])
    prm_d = din("prm", [128, L_ * NPRM])
    nw_d = din("nw", [L_ + 1, 128, D_])
    out_d = nc.dram_tensor("out", [S_, D_], F32, kind="ExternalOutput").ap()
    kS = "ExternalOutput" if dbg else "Internal"
    scr_fm = nc.dram_tensor("scr_fm", [7, 8, 128, S_], BF16, kind=kS).ap()
    scr_tm = nc.dram_tensor("scr_tm", [3, 8, 128, S_], BF16, kind=kS).ap()
    scr_y = nc.dram_tensor("scr_y", [2, 8, 128, S_], BF16, kind=kS).ap()
    scr_x = nc.dram_tensor("scr_x", [S_, D_], F32, kind=kS).ap()

    with ExitStack() as stack:
        ARENA_BYTES = 212000
        big = stack.enter_context(nc.sbuf_tensor("big", [128, ARENA_BYTES // 2], BF16))
        psp = [stack.enter_context(nc.psum_tensor("pp%d" % i, [128, 1024], F32)) for i in range(4)]
        banks = [psp[i // 2][:, (i % 2) * 512:(i % 2 + 1) * 512] for i in range(8)]
        SC = Sched(nc, stack)
        A = Arena(big, ARENA_BYTES)
        op, dma = SC.op, SC.dma

        cst = A.alloc(NCST * 4, F32)
        prm = A.alloc(L_ * NPRM * 4, F32)
        cbf = A.alloc(4 * 128 * 2)
        ident_bf, perm_bf, ones_bf = cbf[:, 0:128], cbf[:, 128:256], cbf[:, 256:384]
        ccol = A.alloc(16 * 4, F32)
        Ct = A.alloc(S_ * 2)
        St = A.alloc(S_ * 2)
        abt = A.alloc(NT * 32 * 4, F32)
        nwb = A.alloc(D_ * 4, F32)
        lamc = A.alloc(8 * 4, F32)
        t_cst, t_prm, t_cbf, t_ccol, t_rope, t_abt, t_gt, t_nwb, t_lam = [Tok() for _ in range(9)]
        ident_f = cst[:, K_ID:K_ID + 128]
        ones_f = cst[:, K_ONE:K_ONE + 128]
        m_H = A.mark()
        hT = A.alloc(8 * S_ * 2)
        hT3 = v3(hT, 8, S_)
        t_hT = [Tok() for _ in range(NT)]
        m_W = A.mark()

        pb = [Tok() for _ in range(8)]

        dma("sync", cst, cst_d, w=[t_cst])
        dma("sync", prm, prm_d, w=[t_prm])
        op("vector", lambda h: h.tensor_copy(out=cbf[:, 0:128], in_=cst[:, K_ID:K_ID + 128]), r=[t_cst], w=[t_cbf])
        op("vector", lambda h: h.tensor_copy(out=cbf[:, 128:256], in_=cst[:, K_PERM:K_PERM + 128]), r=[t_cst], w=[t_cbf])
        op("vector", lambda h: h.tensor_copy(out=cbf[:, 256:384], in_=cst[:, K_ONE:K_ONE + 128]), r=[t_cst], w=[t_cbf])
        CC_EPS, CC_ONE, CC_DKEPS, CC_PI2 = 0, 1, 2, 3
        for ci, val in ((CC_EPS, EPS), (CC_ONE, 1.0), (CC_DKEPS, 128.0 * EPS), (CC_PI2, math.pi / 2)):
            op("vector", lambda h, ci=ci, val=val: h.memset(ccol[:, ci:ci + 1], val), w=[t_ccol])

        def cc(i):
            return ccol[:, i:i + 1]

        m0 = A.mark()
        posi = A.alloc(S_ * 4, I32)
        ang = A.alloc(S_ * 4, F32)
        uu = A.alloc(S_ * 4, F32)
        rr = A.alloc(S_ * 4, F32)
        ki = A.alloc(S_ * 4, I32)
        tp = Tok()
        dma("sync", posi, pos_d, w=[tp])
        V = "vector"
        op(V, lambda h: h.tensor_copy(out=ang, in_=posi), r=[tp], w=[tp])
        op(V, lambda h: h.tensor_scalar(out=ang, in0=ang, scalar1=cst[:, K_INVF:K_INVF + 1], scalar2=None, op0=ALU.mult), r=[tp, t_cst], w=[tp])
        op(V, lambda h: h.tensor_scalar(out=uu, in0=ang, scalar1=1.0 / TWO_PI, scalar2=None, op0=ALU.mult), r=[tp], w=[tp])
        op(V, lambda h: h.tensor_copy(out=ki, in_=uu), r=[tp], w=[tp])
        op(V, lambda h: h.tensor_copy(out=uu, in_=ki), r=[tp], w=[tp])
        op(V, lambda h: h.scalar_tensor_tensor(out=rr, in0=uu, scalar=-TWO_PI_HI, in1=ang, op0=ALU.mult, op1=ALU.add), r=[tp], w=[tp])
        op(V, lambda h: h.scalar_tensor_tensor(out=rr, in0=uu, scalar=-TWO_PI_LO, in1=rr, op0=ALU.mult, op1=ALU.add), r=[tp], w=[tp])

        def fixup(r_):
            op(V, lambda h: h.tensor_scalar(out=uu, in0=r_, scalar1=math.pi, scalar2=None, op0=ALU.is_gt), r=[tp], w=[tp])
            op(V, lambda h: h.scalar_tensor_tensor(out=r_, in0=uu, scalar=-TWO_PI, in1=r_, op0=ALU.mult, op1=ALU.add), r=[tp], w=[tp])
            op(V, lambda h: h.tensor_scalar(out=uu, in0=r_, scalar1=-math.pi, scalar2=None, op0=ALU.is_lt), r=[tp], w=[tp])
            op(V, lambda h: h.scalar_tensor_tensor(out=r_, in0=uu, scalar=TWO_PI, in1=r_, op0=ALU.mult, op1=ALU.add), r=[tp], w=[tp])

        fixup(rr)
        op("scalar", lambda h: h.activation(out=St, in_=rr, func=AF.Sin, scale=cst[:, K_SGN:K_SGN + 1]), r=[tp, t_cst], w=[t_rope])
        op(V, lambda h: h.tensor_scalar(out=ang, in0=rr, scalar1=math.pi / 2, scalar2=None, op0=ALU.add), r=[tp], w=[tp])
        fixup(ang)
        op("scalar", lambda h: h.activation(out=Ct, in_=ang, func=AF.Sin), r=[tp], w=[t_rope])
        SC.barrier()
        A.reset(m0)

        def norm_tile(xt, t_x, t, tmp, final, bank):
            junk, ss, hn, t_tmp = tmp["junk"], tmp["ss"], tmp["hn"], tmp["tok"]
            op("scalar", lambda h: h.activation(out=junk, in_=xt, func=AF.Square, accum_out=ss[:, 0:1]), r=[t_x], w=[t_tmp])
            op("scalar", lambda h: h.activation(out=ss[:, 1:2], in_=ss[:, 0:1], func=AF.Sqrt, scale=1.0 / D_, bias=cc(CC_EPS)), r=[t_ccol], w=[t_tmp])
            op(V, lambda h: h.reciprocal(out=ss[:, 2:3], in_=ss[:, 1:2]), w=[t_tmp])
            if final:
                op(V, lambda h: h.scalar_tensor_tensor(out=xt, in0=xt, scalar=ss[:, 2:3], in1=nwb, op0=ALU.mult, op1=ALU.mult), r=[t_nwb, t_tmp], w=[t_x])
                dma("sync", out_d[t * 128:(t + 1) * 128, :], xt, r=[t_x])
                return
            op(V, lambda h: h.scalar_tensor_tensor(out=hn, in0=xt, scalar=ss[:, 2:3], in1=nwb, op0=ALU.mult, op1=ALU.mult), r=[t_x, t_nwb], w=[t_tmp])
            pbk = banks[bank][:, :].bitcast(BF16)
            for kc in range(8):
                op("tensor", lambda h, kc=kc: h.transpose(out=pbk[:, kc * 128:(kc + 1) * 128], in_=hn[:, kc * 128:(kc + 1) * 128], identity=ident_bf),
                   r=[t_tmp, t_cbf], w=[pb[bank]])
            op("scalar", lambda h: h.copy(out=hT3[:, :, t * 128:(t + 1) * 128], in_=v3(pbk, 8, 128)), r=[pb[bank]], w=[t_hT[t]])

        def norm_tmp():
            hn_ = A.alloc(D_ * 2)
            return dict(junk=hn_, ss=A.alloc(16, F32), hn=hn_, tok=Tok())

        m0 = A.mark()
        dma("sync", nwb, nw_d[0], w=[t_nwb])
        xts = [(A.alloc(D_ * 4, F32), Tok()) for _ in range(3)]
        ntm = [norm_tmp() for _ in range(2)]
        for t in range(NT):
            xt, tx = xts[t % 3]
            dma("sync", xt, x_d[t * 128:(t + 1) * 128, :], w=[tx])
            dma("sync", scr_x[t * 128:(t + 1) * 128, :], xt, r=[tx])
            norm_tile(xt, tx, t, ntm[t % 2], False, t % 2)
        SC.barrier()
        A.reset(m0)

        def phase_B(l, pl):
            T, AC, G = "tensor", "scalar", "gpsimd"
            gta = A.alloc(16 * 1024, F32)

            def gflat(i):
                return gta[:, i * 256:(i + 1) * 256]

            def gtile(i):
                return v3(gflat(i), NT, 8)

            beta = [gtile(0), gtile(1)]
            g_ = [gtile(2), gtile(3)]
            Gc = [gtile(4), gtile(5)]
            nb = [gtile(6), gtile(7)]
            eG = [gtile(8), gtile(9)]
            nbeG = [gtile(10), gtile(11)]
            eGt = [gtile(12), gtile(13)]
            nA = A.alloc(16 * 4, F32)
            t_g = Tok()
            abt3 = v3(abt, NT, 32)
            for d in range(2):
                sl = slice(d * 256, (d + 1) * 256)
                op(AC, lambda h, d=d: h.activation(out=beta[d], in_=abt3[:, :, d * 8:(d + 1) * 8], func=AF.Sigmoid), r=[t_abt], w=[t_g])
                dtb = pl[:, P_DTB + 8 * d:P_DTB + 8 * d + 8].unsqueeze(1).broadcast_to([128, NT, 8])
                op(V, lambda h, d=d, dtb=dtb: h.tensor_tensor(out=g_[d], in0=abt3[:, :, 16 + 8 * d:24 + 8 * d], in1=dtb, op=ALU.add), r=[t_abt, t_prm], w=[t_g])
                op(AC, lambda h, d=d: h.activation(out=g_[d], in_=g_[d], func=AF.Exp), w=[t_g])
                op(AC, lambda h, d=d: h.activation(out=g_[d], in_=g_[d], func=AF.Ln, bias=cc(CC_ONE)), r=[t_ccol], w=[t_g])
                op(AC, lambda h, d=d: h.activation(out=nA[:, 8 * d:8 * d + 8], in_=pl[:, P_ALOG + 8 * d:P_ALOG + 8 * d + 8], func=AF.Exp), r=[t_prm], w=[t_g])
                nAb = nA[:, 8 * d:8 * d + 8].unsqueeze(1).broadcast_to([128, NT, 8])
                op(V, lambda h, d=d, nAb=nAb: h.scalar_tensor_tensor(out=g_[d], in0=g_[d], scalar=-1.0, in1=nAb, op0=ALU.mult, op1=ALU.mult), w=[t_g])
                tri = cst[:, K_TRF:K_TRF + 128] if d == 0 else cst[:, K_TRB:K_TRB + 128]
                op(T, lambda h, d=d, sl=sl, tri=tri: h.matmul(banks[0][:, sl], lhsT=tri, rhs=gflat(2 + d), start=True, stop=True), r=[t_g, t_cst], w=[pb[0]])
                op(T, lambda h, d=d, sl=sl: h.matmul(banks[1][:, sl], lhsT=ones_f, rhs=gflat(2 + d), start=True, stop=True), r=[t_g, t_cst], w=[pb[1]])
                op(V, lambda h, d=d, sl=sl: h.tensor_copy(out=gflat(4 + d), in_=banks[0][:, sl]), w=[t_g, pb[0]])
                op(AC, lambda h, d=d, sl=sl: h.activation(out=gflat(8 + d), in_=banks[0][:, sl], func=AF.Exp), w=[t_g, pb[0]])
                op(AC, lambda h, d=d, sl=sl: h.activation(out=gflat(12 + d), in_=banks[1][:, sl], func=AF.Exp), w=[t_g, pb[1]])
                op(V, lambda h, d=d, sl=sl: h.tensor_tensor(out=gflat(14 + d), in0=banks[1][:, sl], in1=gflat(4 + d), op=ALU.subtract), w=[t_g, pb[1]])
                op(AC, lambda h, d=d: h.activation(out=gflat(14 + d), in_=gflat(14 + d), func=AF.Exp), w=[t_g])
                op(V, lambda h, d=d: h.tensor_scalar(out=gflat(6 + d), in0=gflat(d), scalar1=-1.0, scalar2=None, op0=ALU.mult), w=[t_g])
                op(V, lambda h, d=d: h.tensor_tensor(out=gflat(10 + d), in0=gflat(6 + d), in1=gflat(8 + d), op=ALU.mult), w=[t_g])
            eD = [gtile(14), gtile(15)]

            xpad = A.alloc(4100 * 2)
            acc = A.alloc(S_ * 4, F32)
            sqr = A.alloc(S_ * 2)
            rsb = [A.alloc(512 * 4, F32) for _ in range(2)]
            qT = A.alloc(S_ * 2)
            kT = A.alloc(S_ * 2)
            vb = [A.alloc(S_ * 2) for _ in range(2)]
            kd = [A.alloc(S_ * 2) for _ in range(2)]
            za = A.alloc(S_ * 2)
            oacc = A.alloc(S_ * 4, F32)
            oacc3 = v3(oacc, NT, 128)
            yst = xpad[:, 2:2 + S_]
            ssum = A.alloc(96 * 4, F32)
            RN = 4
            NI = 4
            TTr = [[A.alloc(512, F32) for _ in range(RN)] for _ in range(2)]
            QKr = [[A.alloc(256) for _ in range(RN)] for _ in range(2)]
            tA = [A.alloc(512, F32) for _ in range(NI)]
            tB = [A.alloc(512, F32) for _ in range(NI)]
            dec = [A.alloc(512, F32) for _ in range(NI)]
            decT = [A.alloc(256) for _ in range(NI)]
            pair = [[A.alloc(1024, F32) for _ in range(2)] for _ in range(NI)]
            X = [[A.alloc(512, F32) for _ in range(2)] for _ in range(NI)]
            rhs = [A.alloc(512, F32) for _ in range(2)]
            vnew = [A.alloc(256) for _ in range(2)]
            tmp = [A.alloc(512, F32) for _ in range(2)]
            tmp2 = [A.alloc(512, F32) for _ in range(2)]
            S32 = [A.alloc(512, F32) for _ in range(2)]
            Sbf = [A.alloc(256) for _ in range(2)]
            tk = lambda: Tok()
            t_xp, t_acc, t_sqr, t_qT, t_kT, t_za, t_ss = [Tok() for _ in range(7)]
            t_yst = t_xp
            t_rs = [tk(), tk()]
            t_vb = [tk(), tk()]
            t_kd = [tk(), tk()]
            t_oacc = [tk() for _ in range(NT)]
            t_TT = [[tk() for _ in range(RN)] for _ in range(2)]
            t_QK = [[tk() for _ in range(RN)] for _ in range(2)]
            t_rhs, t_vnew, t_tmp, t_tmp2, t_S32, t_S = [[tk(), tk()] for _ in range(6)]
            t_tA, t_tB, t_dec, t_decT = [[tk() for _ in range(NI)] for _ in range(4)]
            t_pair = [[tk(), tk()] for _ in range(NI)]
            t_X = [[tk(), tk()] for _ in range(NI)]
            KK = [banks[it][:, 0:128] for it in range(NI)]
            QKT = [banks[it][:, 128:256] for it in range(NI)]
            Gb = [banks[it][:, 256:384] for it in range(NI)]
            MT = [banks[it][:, 384:512] for it in range(NI)]
            PPQ = [banks[it][:, 0:256] for it in range(NI)]
            XPQ = [banks[it][:, 256:384] for it in range(NI)]
            KS = [banks[4 + d][:, 0:128] for d in range(2)]
            VN = [banks[4 + d][:, 128:256] for d in range(2)]
            O1 = [banks[4 + d][:, 256:384] for d in range(2)]
            O2 = [banks[4 + d][:, 384:512] for d in range(2)]
            DS = [banks[6 + d][:, 0:128] for d in range(2)]
            PMm = [cst[:, K_PMF:K_PMF + 128], cst[:, K_PMB:K_PMB + 128]]
            NMm = [cst[:, K_NMF:K_NMF + 128], cst[:, K_NMB:K_NMB + 128]]

            op(V, lambda h: h.memset(xpad[:, 0:2], 0.0), w=[t_xp])
            op(V, lambda h: h.memset(xpad[:, 4098:4100], 0.0), w=[t_xp])

            for h_ in range(8):
                def tr_scaled(srcT, t_src, dst, t_dst, sc):
                    for cg in range(4):
                        bk = 6 + (cg % 2)
                        pbk = banks[bk][:, :].bitcast(BF16)
                        for j in range(8):
                            c = cg * 8 + j
                            op(T, lambda h, j=j, c=c, pbk=pbk: h.transpose(out=pbk[:, j * 128:(j + 1) * 128], in_=srcT[:, c * 128:(c + 1) * 128], identity=ident_bf),
                               r=[t_src, t_cbf], w=[pb[bk]])
                        for d in range(2):
                            scb = sc[d][:, cg * 8:(cg + 1) * 8, h_:h_ + 1].broadcast_to([128, 8, 128])
                            op(V, lambda h, d=d, cg=cg, pbk=pbk, scb=scb: h.tensor_tensor(out=v3(dst[d], NT, 128)[:, cg * 8:(cg + 1) * 8, :], in0=v3(pbk, 8, 128), in1=scb, op=ALU.mult),
                               r=[pb[bk], t_g], w=[t_dst[d]])

                for ti in range(3):
                    dma("sync", xpad[:, 2:4098], scr_fm[ti, h_], w=[t_xp])
                    base = P_CW + (ti * 8 + h_) * 5
                    op(V, lambda h, base=base: h.tensor_scalar(out=acc, in0=xpad[:, 0:4096], scalar1=pl[:, base:base + 1], scalar2=None, op0=ALU.mult), r=[t_xp, t_prm], w=[t_acc])
                    for k in range(1, 5):
                        op(V, lambda h, base=base, k=k: h.scalar_tensor_tensor(out=acc, in0=xpad[:, k:k + 4096], scalar=pl[:, base + k:base + k + 1], in1=acc, op0=ALU.mult, op1=ALU.add),
                           r=[t_xp, t_prm], w=[t_acc])
                    if ti == 2:
                        op(AC, lambda h: h.activation(out=sqr, in_=acc, func=AF.Silu), r=[t_acc], w=[t_sqr])
                        tr_scaled(sqr, t_sqr, vb, t_vb, beta)
                    else:
                        op(AC, lambda h: h.activation(out=acc, in_=acc, func=AF.Silu), w=[t_acc])
                        op(AC, lambda h: h.activation(out=sqr, in_=acc, func=AF.Square), r=[t_acc], w=[t_sqr])
                        dstT, t_dst = (qT, t_qT) if ti == 0 else (kT, t_kT)
                        for tt in range(8):
                            bk = 6 + (tt % 2)
                            tsl = slice(tt * 512, (tt + 1) * 512)
                            rs = rsb[tt % 2]
                            trs = t_rs[tt % 2]
                            op(T, lambda h, bk=bk, tsl=tsl: h.matmul(banks[bk][:, :], lhsT=ones_bf, rhs=sqr[:, tsl], start=True, stop=True), r=[t_sqr, t_cbf], w=[pb[bk]])
                            op(AC, lambda h, bk=bk, rs=rs, ti=ti: h.activation(out=rs, in_=banks[bk][:, :], func=AF.Ln, scale=(128.0 if ti == 0 else 1.0),
                                                                        bias=cc(CC_DKEPS if ti == 0 else CC_EPS)), r=[pb[bk], t_ccol], w=[trs])
                            op(AC, lambda h, rs=rs: h.activation(out=rs, in_=rs, func=AF.Exp, scale=-0.5), w=[trs])
                            op(V, lambda h, rs=rs, tsl=tsl, dstT=dstT: h.tensor_tensor(out=dstT[:, tsl], in0=acc[:, tsl], in1=rs, op=ALU.mult), r=[t_acc, trs], w=[t_dst])
                        if ti == 1:
                            tr_scaled(kT, t_kT, kd, t_kd, eD)
                dma("sync", za, scr_tm[1, h_], w=[t_za])
                op(G, lambda h: h.memset(oacc, 0.0), w=t_oacc)
                for d in range(2):
                    op(G, lambda h, d=d: h.memset(S32[d], 0.0), w=[t_S32[d]])
                    op(G, lambda h, d=d: h.memset(Sbf[d], 0.0), w=[t_S[d]])

                def prep_batch(steps, hh=h_):
                    items = []
                    for si, i in enumerate(steps):
                        for d in range(2):
                            items.append((si * 2 + d, d, (i, NT - 1 - i)[d], i % RN))
                    for it, d, c, s in items:
                        kc_ = kT[:, c * 128:(c + 1) * 128]
                        qc_ = qT[:, c * 128:(c + 1) * 128]
                        op(T, lambda h, it=it, kc_=kc_: h.matmul(KK[it], lhsT=kc_, rhs=kc_, start=True, stop=True), r=[t_kT], w=[pb[it]])
                        op(T, lambda h, it=it, kc_=kc_, qc_=qc_: h.matmul(QKT[it], lhsT=kc_, rhs=qc_, start=True, stop=True), r=[t_kT, t_qT], w=[pb[it]])
                        op(G, lambda h, it=it, d=d, c=c: h.tensor_scalar(out=tA[it], in0=ident_f, scalar1=Gc[d][:, c, hh:hh + 1], scalar2=None, op0=ALU.mult), r=[t_g, t_cst], w=[t_tA[it]])
                        op(T, lambda h, it=it: h.matmul(Gb[it], lhsT=ones_f, rhs=tA[it], start=True, stop=True), r=[t_tA[it], t_cst], w=[pb[it]])
                    yield
                    for it, d, c, s in items:
                        op(V, lambda h, it=it, d=d, c=c: h.scalar_tensor_tensor(out=tB[it], in0=Gb[it], scalar=Gc[d][:, c, hh:hh + 1], in1=PMm[d], op0=ALU.subtract, op1=ALU.add),
                           r=[t_g, t_cst], w=[t_tB[it], pb[it]])
                    yield
                    for it, d, c, s in items:
                        op(AC, lambda h, it=it: h.activation(out=dec[it], in_=tB[it], func=AF.Exp, scale=-1.0), r=[t_tB[it]], w=[t_dec[it]])
                        op(V, lambda h, it=it, d=d, c=c: h.scalar_tensor_tensor(out=tA[it], in0=Gb[it], scalar=Gc[d][:, c, hh:hh + 1], in1=NMm[d], op0=ALU.subtract, op1=ALU.add),
                           r=[t_g, t_cst], w=[t_tA[it], pb[it]])
                    yield
                    for it, d, c, s in items:
                        op(AC, lambda h, it=it: h.activation(out=decT[it], in_=tA[it], func=AF.Exp), r=[t_tA[it]], w=[t_decT[it]])
                        op(V, lambda h, it=it, d=d, c=c: h.scalar_tensor_tensor(out=pair[it][0][:, 0:128], in0=KK[it], scalar=nb[d][:, c, hh:hh + 1], in1=dec[it], op0=ALU.mult, op1=ALU.mult),
                           r=[t_dec[it], t_g], w=[t_pair[it][0], pb[it]])
                    yield
                    for it, d, c, s in items:
                        op(V, lambda h, it=it, d=d, s=s: h.tensor_tensor(out=QKr[d][s], in0=QKT[it], in1=decT[it], op=ALU.mult), r=[t_decT[it]], w=[t_QK[d][s], pb[it]])
                        op(T, lambda h, it=it: h.transpose(out=MT[it], in_=pair[it][0][:, 0:128], identity=ident_f), r=[t_pair[it][0], t_cst], w=[pb[it]])
                    yield
                    for it, d, c, s in items:
                        op(AC, lambda h, it=it: h.copy(out=pair[it][0][:, 128:256], in_=MT[it]), w=[t_pair[it][0], pb[it]])
                        op(G, lambda h, it=it: h.tensor_tensor(out=X[it][0], in0=pair[it][0][:, 128:256], in1=ident_f, op=ALU.add), r=[t_pair[it][0], t_cst], w=[t_X[it][0]])
                    yield
                    for lvl in range(1, 7):
                        a = (lvl - 1) % 2
                        b = lvl % 2
                        n = 256 if lvl < 6 else 128
                        for it, d, c, s in items:
                            P_ = pair[it][a][:, 0:128]
                            PT_ = pair[it][a][:, 128:256]
                            op(T, lambda h, it=it, P_=P_, PT_=PT_: h.matmul(PPQ[it][:, 0:128], lhsT=PT_, rhs=P_, start=True, stop=True), r=[t_pair[it][a]], w=[pb[it]])
                            if lvl < 6:
                                op(T, lambda h, it=it, P_=P_, PT_=PT_: h.matmul(PPQ[it][:, 128:256], lhsT=P_, rhs=PT_, start=True, stop=True), r=[t_pair[it][a]], w=[pb[it]])
                        yield
                        for it, d, c, s in items:
                            op(AC, lambda h, it=it, b=b, n=n: h.copy(out=pair[it][b][:, 0:n], in_=PPQ[it][:, 0:n]), w=[t_pair[it][b], pb[it]])
                        yield
                        for it, d, c, s in items:
                            op(T, lambda h, it=it, a=a, b=b: h.matmul(XPQ[it], lhsT=pair[it][b][:, 0:128], rhs=X[it][a], start=True, stop=True), r=[t_pair[it][b], t_X[it][a]], w=[pb[it]])
                        yield
                        for it, d, c, s in items:
                            dst, tdst = (X[it][b], t_X[it][b]) if lvl < 6 else (TTr[d][s], t_TT[d][s])
                            op(V, lambda h, it=it, a=a, dst=dst: h.tensor_tensor(out=dst, in0=XPQ[it], in1=X[it][a], op=ALU.add), r=[t_X[it][a]], w=[tdst, pb[it]])
                        yield

                def chain_steps(steps, hh=h_):
                    for i in steps:
                        s = i % RN
                        cs = (i, NT - 1 - i)
                        for d in range(2):
                            c = cs[d]
                            op(T, lambda h, d=d, c=c: h.matmul(KS[d], lhsT=kT[:, c * 128:(c + 1) * 128], rhs=Sbf[d], start=True, stop=True), r=[t_kT, t_S[d]], w=[pb[4 + d]])
                        yield
                        for d in range(2):
                            c = cs[d]
                            op(V, lambda h, d=d, c=c: h.scalar_tensor_tensor(out=rhs[d], in0=KS[d], scalar=nbeG[d][:, c, hh:hh + 1], in1=v3(vb[d], NT, 128)[:, c, :], op0=ALU.mult, op1=ALU.add),
                               r=[t_g, t_vb[d]], w=[t_rhs[d], pb[4 + d]])
                        yield
                        for d in range(2):
                            op(T, lambda h, d=d, s=s: h.matmul(VN[d], lhsT=TTr[d][s], rhs=rhs[d], start=True, stop=True), r=[t_TT[d][s], t_rhs[d]], w=[pb[4 + d]])
                        yield
                        for d in range(2):
                            op(AC, lambda h, d=d: h.copy(out=vnew[d], in_=VN[d]), w=[t_vnew[d], pb[4 + d]])
                        yield
                        for d in range(2):
                            c = cs[d]
                            op(T, lambda h, d=d, c=c: h.matmul(DS[d], lhsT=v3(kd[d], NT, 128)[:, c, :], rhs=vnew[d], start=True, stop=True), r=[t_kd[d], t_vnew[d]], w=[pb[6 + d]])
                            op(T, lambda h, d=d, c=c: h.matmul(O1[d], lhsT=qT[:, c * 128:(c + 1) * 128], rhs=Sbf[d], start=True, stop=True), r=[t_qT, t_S[d]], w=[pb[4 + d]])
                            op(T, lambda h, d=d, s=s: h.matmul(O2[d], lhsT=QKr[d][s], rhs=vnew[d], start=True, stop=True), r=[t_QK[d][s], t_vnew[d]], w=[pb[4 + d]])
                        yield
                        for d in range(2):
                            c = cs[d]
                            op(V, lambda h, d=d, c=c: h.scalar_tensor_tensor(out=Sbf[d], in0=S32[d], scalar=eGt[d][:, c, hh:hh + 1], in1=DS[d], op0=ALU.mult, op1=ALU.add),
                               r=[t_g, t_S32[d]], w=[t_S[d], pb[6 + d]])
                            op(V, lambda h, d=d, c=c: h.scalar_tensor_tensor(out=S32[d], in0=S32[d], scalar=eGt[d][:, c, hh:hh + 1], in1=DS[d], op0=ALU.mult, op1=ALU.add),
                               r=[t_g], w=[t_S32[d], pb[6 + d]])
                            op(AC, lambda h, d=d, c=c: h.activation(out=tmp[d], in_=O1[d], func=AF.Copy, scale=eG[d][:, c, hh:hh + 1]), r=[t_g], w=[t_tmp[d], pb[4 + d]])
                        yield
                        for d in range(2):
                            c = cs[d]
                            op(V, lambda h, d=d: h.tensor_tensor(out=tmp2[d], in0=O2[d], in1=tmp[d], op=ALU.add), r=[t_tmp[d]], w=[t_tmp2[d], pb[4 + d]])
                            op(G, lambda h, d=d, c=c: h.tensor_tensor(out=oacc3[:, c, :], in0=oacc3[:, c, :], in1=tmp2[d], op=ALU.add), r=[t_tmp2[d]], w=[t_oacc[c]])
                        yield

                for _ in prep_batch((0, 1)):
                    pass
                for bb in range(NT // 2):
                    pg = prep_batch((2 * bb + 2, 2 * bb + 3)) if bb + 1 < NT // 2 else None
                    cg = chain_steps((2 * bb, 2 * bb + 1))
                    while pg is not None or cg is not None:
                        if pg is not None:
                            for _ in range(2):
                                if next(pg, "END") == "END":
                                    pg = None
                                    break
                        if cg is not None:
                            if next(cg, "END") == "END":
                                cg = None

                op(AC, lambda h: h.activation(out=acc, in_=oacc, func=AF.Square), r=t_oacc, w=[t_acc])
                op(V, lambda h: h.tensor_reduce(out=ssum[:, 0:32], in_=v3(acc, NT, 128), axis=AX.X, op=ALU.add), r=[t_acc], w=[t_ss])
                op(AC, lambda h: h.activation(out=ssum[:, 32:64], in_=ssum[:, 0:32], func=AF.Sqrt, scale=1.0 / 128, bias=cc(CC_EPS)), r=[t_ccol], w=[t_ss])
                op(V, lambda h: h.reciprocal(out=ssum[:, 64:96], in_=ssum[:, 32:64]), w=[t_ss])
                rb = ssum[:, 64:96].unsqueeze(2).broadcast_to([128, NT, 128])
                op(V, lambda h, rb=rb: h.tensor_tensor(out=oacc3, in0=oacc3, in1=rb, op=ALU.mult), r=[t_ss], w=t_oacc)
                gw = pl[:, P_GNW:P_GNW + 128].unsqueeze(1).broadcast_to([128, NT, 128])
                op(G, lambda h, gw=gw: h.tensor_tensor(out=oacc3, in0=oacc3, in1=gw, op=ALU.mult), r=[t_prm], w=t_oacc)
                op(V, lambda h: h.tensor_tensor(out=sqr, in0=oacc, in1=za, op=ALU.mult), r=t_oacc + [t_za], w=[t_sqr])
                for cg in range(4):
                    bk = 6 + (cg % 2)
                    pbk = banks[bk][:, :].bitcast(BF16)
                    for j in range(8):
                        c = cg * 8 + j
                        op(T, lambda h, j=j, c=c, pbk=pbk: h.transpose(out=pbk[:, j * 128:(j + 1) * 128], in_=sqr[:, c * 128:(c + 1) * 128], identity=ident_bf),
                           r=[t_sqr, t_cbf], w=[pb[bk]])
                    op(AC, lambda h, cg=cg, pbk=pbk: h.copy(out=yst[:, cg * 1024:(cg + 1) * 1024], in_=pbk), r=[pb[bk]], w=[t_yst])
                dma("sync", scr_y[0, h_], yst, r=[t_yst])

        def phase_C(l, pl, lam_init):
            T, AC, G = "tensor", "scalar", "gpsimd"
            om = 1.0 - lam_init
            ltmp = A.alloc(128 * 4, F32)
            lp = v3(pl[:, P_LAM:P_LAM + 256], 4, 64)
            op(V, lambda h: h.tensor_tensor(out=ltmp[:, 0:64], in0=lp[:, 0, :], in1=lp[:, 1, :], op=ALU.mult), r=[t_prm], w=[t_lam])
            op(V, lambda h: h.tensor_tensor(out=ltmp[:, 64:128], in0=lp[:, 2, :], in1=lp[:, 3, :], op=ALU.mult), r=[t_prm], w=[t_lam])
            op(V, lambda h: h.tensor_reduce(out=lamc[:, 0:2], in_=v3(ltmp, 2, 64), axis=AX.X, op=ALU.add), w=[t_lam])
            op(AC, lambda h: h.activation(out=lamc[:, 2:4], in_=lamc[:, 0:2], func=AF.Exp), w=[t_lam])
            op(V, lambda h: h.tensor_tensor(out=lamc[:, 4:5], in0=lamc[:, 2:3], in1=lamc[:, 3:4], op=ALU.subtract), w=[t_lam])
            op(V, lambda h: h.tensor_scalar(out=lamc[:, 5:6], in0=lamc[:, 4:5], scalar1=float(lam_init), scalar2=-1.0, op0=ALU.add, op1=ALU.mult), w=[t_lam])
            op(V, lambda h: h.memset(lamc[:, 6:7], EPS / (om * om)), w=[t_lam])
            nlam = lamc[:, 5:6]
            sl_scale = 1.0 / (128.0 * om * om)

            qraw = A.alloc(S_ * 2)
            kraw = A.alloc(S_ * 2)
            qT = A.alloc(S_ * 2)
            kT = A.alloc(S_ * 2)
            vb1 = A.alloc(NT * 130 * 2)
            vb13 = v3(vb1, NT, 130)
            zb = A.alloc(S_ * 2)
            zb3 = v3(zb, NT, 128)
            yst = A.alloc(S_ * 2)
            NE = 3
            Eb = [A.alloc(1024 * 2) for _ in range(NE)]
            rt1 = [A.alloc(512 * 4, F32) for _ in range(2)]
            rt2 = [A.alloc(512 * 4, F32) for _ in range(2)]
            osb = [A.alloc(8 * 130 * 4, F32) for _ in range(2)]
            a0 = [A.alloc(128 * 4, F32) for _ in range(2)]
            aa = [A.alloc(128 * 4, F32) for _ in range(2)]
            ytok = [A.alloc(128 * 2) for _ in range(2)]
            jk = [A.alloc(128 * 2) for _ in range(2)]
            rc = [A.alloc(8 * 4, F32) for _ in range(2)]
            t_qr, t_kr, t_qT, t_kT, t_v, t_z, t_yst = [Tok() for _ in range(7)]
            t_rt1 = [Tok(), Tok()]
            t_rt2 = [Tok(), Tok()]
            t_E = [Tok() for _ in range(NE)]
            t_osb = [Tok(), Tok()]
            t_ep = [Tok(), Tok()]
            op(V, lambda h: h.memset(vb13[:, :, 128:130], 1.0), w=[t_v])
            slwb = pl[:, P_SLW:P_SLW + 128].unsqueeze(1).broadcast_to([128, NT, 128])
            OG = []
            for gi in range(8):
                OG.append(banks[4 + gi // 3][:, (gi % 3) * 130:(gi % 3) * 130 + 129])

            for h_ in range(8):
                dma("sync", qraw, scr_fm[3, h_], w=[t_qr])
                dma("sync", kraw, scr_fm[4, h_], w=[t_kr])
                dma("sync", vb13[:, :, 0:128], scr_tm[0, h_].rearrange("p (a b) -> p a b", a=NT, b=128), w=[t_v])
                dma("sync", zb, scr_tm[2, h_], w=[t_z])
                op(G, lambda h: h.tensor_tensor(out=zb3, in0=zb3, in1=slwb, op=ALU.mult), r=[t_prm], w=[t_z])
                for raw, t_raw, dstT, t_dst in ((qraw, t_qr, qT, t_qT), (kraw, t_kr, kT, t_kT)):
                    for tt in range(8):
                        tsl = slice(tt * 512, (tt + 1) * 512)
                        rb_ = tt % 4
                        r1, r2, tr1, tr2 = rt1[tt % 2], rt2[tt % 2], t_rt1[tt % 2], t_rt2[tt % 2]
                        op(T, lambda h, raw=raw, tsl=tsl, rb_=rb_: h.matmul(banks[rb_][:, :], lhsT=perm_bf, rhs=raw[:, tsl], start=True, stop=True), r=[t_raw, t_cbf], w=[pb[rb_]])
                        op(V, lambda h, tsl=tsl, rb_=rb_, r1=r1: h.tensor_tensor(out=r1, in0=banks[rb_][:, :], in1=St[:, tsl], op=ALU.mult), r=[pb[rb_], t_rope], w=[tr1])
                        op(G, lambda h, raw=raw, tsl=tsl, r2=r2: h.tensor_tensor(out=r2, in0=raw[:, tsl], in1=Ct[:, tsl], op=ALU.mult), r=[t_raw, t_rope], w=[tr2])
                        op(V, lambda h, dstT=dstT, tsl=tsl, r1=r1, r2=r2: h.tensor_tensor(out=dstT[:, tsl], in0=r1, in1=r2, op=ALU.add), r=[tr1, tr2], w=[t_dst])
                ne = 0
                for qt in range(8):
                    qsl = slice(qt * 512, (qt + 1) * 512)
                    def emit_S(kt, qsl=qsl):
                        ksl = slice(kt * 128, (kt + 1) * 128)
                        for sm in range(2):
                            bk = (kt % 2) * 2 + sm
                            rows = slice(sm * 64, (sm + 1) * 64)
                            op(T, lambda h, bk=bk, rows=rows, ksl=ksl, qsl=qsl: h.matmul(banks[bk][:, :], lhsT=kT[rows, ksl], rhs=qT[rows, qsl], start=True, stop=True),
                               r=[t_kT, t_qT], w=[pb[bk]])

                    emit_S(0)
                    for kt in range(NT):
                        par = kt % 2
                        es = ne % NE
                        ne += 1
                        if kt + 1 < NT:
                            emit_S(kt + 1)
                        op(AC, lambda h, par=par, es=es: h.activation(out=Eb[es], in_=psp[par][:, :], func=AF.Exp, scale=0.125), r=[pb[2 * par], pb[2 * par + 1]], w=[t_E[es]])
                        for sm in range(2):
                            for sub in range(4):
                                gi = sub * 2 + sm
                                first_in_bank = gi in (0, 4, 6)
                                op(T, lambda h, gi=gi, sm=sm, es=es, sub=sub, kt=kt, fb=first_in_bank: h.matmul(
                                    OG[gi], lhsT=Eb[es][:, sm * 512 + sub * 128:sm * 512 + (sub + 1) * 128], rhs=vb13[:, kt, 0:129],
                                    start=(kt == 0 and fb), stop=(kt == NT - 1), skip_group_check=True),
                                   r=[t_E[es], t_v], w=[pb[4 + gi // 3]])
                    ob = osb[qt % 2]
                    tob = t_osb[qt % 2]
                    op(V, lambda h, ob=ob: h.tensor_copy(out=ob[:, 0:390], in_=banks[4][:, 0:390]), r=[pb[4]], w=[tob])
                    op(AC, lambda h, ob=ob: h.copy(out=ob[:, 390:780], in_=banks[5][:, 0:390]), r=[pb[5]], w=[tob])
                    op(V, lambda h, ob=ob: h.tensor_copy(out=ob[:, 780:1040], in_=banks[6][:, 0:260]), r=[pb[6]], w=[tob])
                    b7bf = banks[7][:, :].bitcast(BF16)
                    for sub in range(4):
                        e = sub % 2
                        c = qt * 4 + sub
                        o0 = ob[:, (sub * 2) * 130:(sub * 2) * 130 + 130]
                        o1 = ob[:, (sub * 2 + 1) * 130:(sub * 2 + 1) * 130 + 130]
                        rc_, a0_, aa_, yt_, jk_, tep = rc[e], a0[e], aa[e], ytok[e], jk[e], t_ep[e]
                        op(V, lambda h, rc_=rc_, o0=o0: h.reciprocal(out=rc_[:, 0:1], in_=o0[:, 128:129]), r=[tob], w=[tep])
                        op(V, lambda h, rc_=rc_, o1=o1: h.reciprocal(out=rc_[:, 1:2], in_=o1[:, 128:129]), r=[tob], w=[tep])
                        op(V, lambda h, rc_=rc_: h.tensor_tensor(out=rc_[:, 2:3], in0=rc_[:, 1:2], in1=nlam, op=ALU.mult), r=[t_lam], w=[tep])
                        op(V, lambda h, rc_=rc_, a0_=a0_, o0=o0: h.tensor_scalar(out=a0_, in0=o0[:, 0:128], scalar1=rc_[:, 0:1], scalar2=None, op0=ALU.mult), r=[tob], w=[tep])
                        op(V, lambda h, rc_=rc_, a0_=a0_, aa_=aa_, o1=o1: h.scalar_tensor_tensor(out=aa_, in0=o1[:, 0:128], scalar=rc_[:, 2:3], in1=a0_, op0=ALU.mult, op1=ALU.add), r=[tob], w=[tep])
                        op(AC, lambda h, rc_=rc_, aa_=aa_, jk_=jk_: h.activation(out=jk_, in_=aa_, func=AF.Square, accum_out=rc_[:, 3:4]), w=[tep])
                        op(AC, lambda h, rc_=rc_: h.activation(out=rc_[:, 4:5], in_=rc_[:, 3:4], func=AF.Sqrt, scale=sl_scale, bias=lamc[:, 6:7]), r=[t_lam], w=[tep])
                        op(V, lambda h, rc_=rc_: h.reciprocal(out=rc_[:, 5:6], in_=rc_[:, 4:5]), w=[tep])
                        op(V, lambda h, rc_=rc_, aa_=aa_, yt_=yt_, c=c: h.scalar_tensor_tensor(out=yt_, in0=aa_, scalar=rc_[:, 5:6], in1=zb3[:, c, :], op0=ALU.mult, op1=ALU.mult), r=[t_z], w=[tep])
                        op(T, lambda h, yt_=yt_, sub=sub: h.transpose(out=b7bf[:, sub * 128:(sub + 1) * 128], in_=yt_, identity=ident_bf), r=[tep, t_cbf], w=[pb[7]])
                    op(AC, lambda h, qsl=qsl: h.copy(out=yst[:, qsl], in_=b7bf[:, 0:512]), r=[pb[7]], w=[t_yst])
                dma("sync", scr_y[1, h_], yst, r=[t_yst])

        for l in range(n_layers):
            pl = prm[:, l * NPRM:(l + 1) * NPRM]
            lam_init = 0.8 - 0.6 * math.exp(-0.3 * l)
            m0 = A.mark()
            wv = w_in_d[l].rearrange("(kc p) n -> p kc n", p=128)
            wb = [(A.alloc(8 * 512 * 2), Tok()) for _ in range(3)]
            stg = [(A.alloc(S_ * 2), Tok()) for _ in range(2)]
            stT = [(A.alloc(4 * S_ * 2), Tok()) for _ in range(1)]
            nwl = 0
            nst = 0
            nbk = 0
            fm_list = [(C_QA, 0, None), (C_KA, 1, None), (C_VA, 2, None), (C_QB, 3, None), (C_KB, 4, None),
                       (C_GA, 5, AF.Sigmoid), (C_GB, 6, AF.Sigmoid)]
            nev = 0
            for c0, ti, fn_ in fm_list:
                for g in range(2):
                    wt, twt = wb[nwl % 3]
                    nwl += 1
                    wt3 = v3(wt, 8, 512)
                    dma("gpsimd", wt3, wv[:, :, c0 + g * 512:c0 + (g + 1) * 512], w=[twt])
                    for hb in range(4):
                        st, tst = stg[nst % 2]
                        nst += 1
                        for tt in range(8):
                            bk = nbk % 4
                            nbk += 1
                            for kc in range(8):
                                op("tensor", lambda h, bk=bk, wt3=wt3, hb=hb, kc=kc, tt=tt: h.matmul(
                                    banks[bk][:, :], lhsT=wt3[:, kc, hb * 128:(hb + 1) * 128], rhs=hT3[:, kc, tt * 512:(tt + 1) * 512],
                                    start=(kc == 0), stop=(kc == 7)), r=[twt] + t_hT[tt * 4:tt * 4 + 4], w=[pb[bk]])
                            dst = st[:, tt * 512:(tt + 1) * 512]
                            if fn_ is not None:
                                op("scalar", lambda h, bk=bk, dst=dst, fn_=fn_: h.activation(out=dst, in_=banks[bk][:, :], func=fn_), r=[pb[bk]], w=[tst])
                            elif nev % 2 == 0:
                                op("scalar", lambda h, bk=bk, dst=dst: h.copy(out=dst, in_=banks[bk][:, :]), r=[pb[bk]], w=[tst])
                            else:
                                op("vector", lambda h, bk=bk, dst=dst: h.tensor_copy(out=dst, in_=banks[bk][:, :]), r=[pb[bk]], w=[tst])
                            nev += 1
                        dma("sync", scr_fm[ti, g * 4 + hb], st, r=[tst])
            tm_list = [(C_VB, 0, None), (C_ZA, 1, AF.Silu), (C_ZB, 2, AF.Silu)]
            for c0, ti, fn_ in tm_list:
                for g in range(2):
                    wt, twt = wb[nwl % 3]
                    nwl += 1
                    wt3 = v3(wt, 8, 512)
                    dma("gpsimd", wt3, wv[:, :, c0 + g * 512:c0 + (g + 1) * 512], w=[twt])
                    st, tst = stT[0]
                    st4 = st.rearrange("p (a b c) -> p a b c", a=4, b=NT, c=128)
                    for t in range(NT):
                        bk = nbk % 4
                        nbk += 1
                        for kc in range(8):
                            op("tensor", lambda h, bk=bk, wt3=wt3, kc=kc, t=t: h.matmul(
                                banks[bk][:, :], lhsT=hT3[:, kc, t * 128:(t + 1) * 128], rhs=wt3[:, kc, :],
                                start=(kc == 0), stop=(kc == 7)), r=[twt, t_hT[t]], w=[pb[bk]])
                        dst = st4[:, :, t, :]
                        src = v3(banks[bk][:, :], 4, 128)
                        if fn_ is not None:
                            op("scalar", lambda h, dst=dst, src=src, fn_=fn_: h.activation(out=dst, in_=src, func=fn_), r=[pb[bk]], w=[tst])
                        elif t % 2 == 0:
                            op("scalar", lambda h, dst=dst, src=src: h.copy(out=dst, in_=src), r=[pb[bk]], w=[tst])
                        else:
                            op("vector", lambda h, dst=dst, src=src: h.tensor_copy(out=dst, in_=src), r=[pb[bk]], w=[tst])
                    for hb in range(4):
                        dma("sync", scr_tm[ti, g * 4 + hb], st[:, hb * S_:(hb + 1) * S_], r=[tst])
            wt, twt = wb[nwl % 3]
            nwl += 1
            wt3 = v3(wt, 8, 512)
            dma("gpsimd", wt3[:, :, 0:32], wv[:, :, C_AB:C_AB + 32], w=[twt])
            abt3 = v3(abt, NT, 32)
            for t in range(NT):
                bk = nbk % 4
                nbk += 1
                for kc in range(8):
                    op("tensor", lambda h, bk=bk, wt3=wt3, kc=kc, t=t: h.matmul(
                        banks[bk][:, 0:32], lhsT=hT3[:, kc, t * 128:(t + 1) * 128], rhs=wt3[:, kc, 0:32],
                        start=(kc == 0), stop=(kc == 7)), r=[twt, t_hT[t]], w=[pb[bk]])
                op("vector", lambda h, bk=bk, t=t: h.tensor_copy(out=abt3[:, t, :], in_=banks[bk][:, 0:32]), r=[pb[bk]], w=[t_abt])
            SC.barrier()
            A.reset(m0)
            if stop_after == "A":
                break

            A.reset(m_H)
            phase_B(l, pl)
            SC.barrier()
            if stop_after == "B":
                break
            A.reset(m_H)
            phase_C(l, pl, lam_init)
            SC.barrier()
            if stop_after == "C":
                break
            A.reset(m_W)
            last = (l == L_ - 1)
            dma("sync", nwb, nw_d[l + 1], w=[t_nwb])
            wts = []
            for wd in (w_pa_d, w_pb_d, w_out_d):
                wt = A.alloc(8 * D_ * 2)
                tw = Tok()
                wt3 = v3(wt, 8, D_)
                for hh in range(2):
                    dma("gpsimd", wt3[:, :, hh * 512:(hh + 1) * 512], wd[l].rearrange("(kc p) n -> p kc n", p=128)[:, :, hh * 512:(hh + 1) * 512], w=[tw])
                wts.append((wt3, tw))
            (wpa3, twpa), (wpb3, twpb), (wo3, two) = wts
            yb_ = [(A.alloc(8 * 512 * 2), A.alloc(8 * 512 * 2), Tok()) for _ in range(1)]
            gb_ = [(A.alloc(2 * 512 * 2), Tok()) for _ in range(2)]
            mg = A.alloc(8 * 512 * 2)
            mg3 = v3(mg, 8, 512)
            t_mg = Tok()
            tt1 = A.alloc(512 * 4, F32)
            tt2 = A.alloc(512 * 4, F32)
            t_tt = Tok()
            xts = [(A.alloc(D_ * 4, F32), Tok()) for _ in range(2)]
            ntm = [norm_tmp() for _ in range(1)]
            ng = 0
            for tt in range(8):
                ya, yb, ty = yb_[0]
                ya3, yb3 = v3(ya, 8, 512), v3(yb, 8, 512)
                dma("sync", ya3, scr_y[0][:, :, tt * 512:(tt + 1) * 512].rearrange("h p t -> p h t"), w=[ty])
                dma("sync", yb3, scr_y[1][:, :, tt * 512:(tt + 1) * 512].rearrange("h p t -> p h t"), w=[ty])
                for m in range(8):
                    gg, tg = gb_[ng % 2]
                    ng += 1
                    dma("sync", gg[:, 0:512], scr_fm[5, m][:, tt * 512:(tt + 1) * 512], w=[tg])
                    dma("sync", gg[:, 512:1024], scr_fm[6, m][:, tt * 512:(tt + 1) * 512], w=[tg])
                    for kc in range(8):
                        op("tensor", lambda h, kc=kc, m=m, ya3=ya3: h.matmul(banks[0][:, :], lhsT=wpa3[:, kc, m * 128:(m + 1) * 128], rhs=ya3[:, kc, :],
                                                                 start=(kc == 0), stop=(kc == 7)), r=[twpa, ty], w=[pb[0]])
                    for kc in range(8):
                        op("tensor", lambda h, kc=kc, m=m, yb3=yb3: h.matmul(banks[1][:, :], lhsT=wpb3[:, kc, m * 128:(m + 1) * 128], rhs=yb3[:, kc, :],
                                                                 start=(kc == 0), stop=(kc == 7)), r=[twpb, ty], w=[pb[1]])
                    op(V, lambda h, gg=gg: h.tensor_tensor(out=tt1, in0=banks[0][:, :], in1=gg[:, 0:512], op=ALU.mult), r=[pb[0], tg], w=[t_tt])
                    op(V, lambda h, gg=gg: h.tensor_tensor(out=tt2, in0=banks[1][:, :], in1=gg[:, 512:1024], op=ALU.mult), r=[pb[1], tg], w=[t_tt])
                    op("gpsimd", lambda h, m=m: h.tensor_tensor(out=mg3[:, m, :], in0=tt1, in1=tt2, op=ALU.add), r=[t_tt], w=[t_mg])
                for sub in range(4):
                    t = tt * 4 + sub
                    xt, tx = xts[t % 2]
                    dma("sync", xt, scr_x[t * 128:(t + 1) * 128, :], w=[tx])
                    for nh in range(2):
                        bk = 2 + nh
                        for kc in range(8):
                            op("tensor", lambda h, kc=kc, nh=nh, sub=sub, bk=bk: h.matmul(
                                banks[bk][:, :], lhsT=mg3[:, kc, sub * 128:(sub + 1) * 128], rhs=wo3[:, kc, nh * 512:(nh + 1) * 512],
                                start=(kc == 0), stop=(kc == 7)), r=[two, t_mg], w=[pb[bk]])
                        op(V, lambda h, nh=nh, bk=bk, xt=xt: h.tensor_tensor(out=xt[:, nh * 512:(nh + 1) * 512], in0=banks[bk][:, :], in1=xt[:, nh * 512:(nh + 1) * 512], op=ALU.add),
                           r=[pb[bk]], w=[tx])
                    if not last:
                        dma("sync", scr_x[t * 128:(t + 1) * 128, :], xt, r=[tx])
                    norm_tile(xt, tx, t, ntm[0], last, 4 + (t % 2))
            SC.barrier()
            A.reset(m_W)

        SC.barrier()

        with nc.Block() as block:
            @block.sync
            def _(h):
                SC.replay("sync", h)

            @block.scalar
            def _(h):
                SC.replay("scalar", h)

            @block.vector
            def _(h):
                SC.replay("vector", h)

            @block.gpsimd
            def _(h):
                SC.replay("gpsimd", h)

            @block.tensor
            def _(h):
                SC.replay("tensor", h)
    return nc


def _host_consts():
    c = np.zeros((128, NCST), np.float32)
    p = np.arange(128)[:, None]
    f = np.arange(128)[None, :]
    c[:, K_ID:K_ID + 128] = (p == f)
    c[:, K_PMF:K_PMF + 128] = np.where(p > f, 0.0, BIGM)
    c[:, K_NMF:K_NMF + 128] = np.where(f >= p, 0.0, -BIGM)
    c[:, K_PMB:K_PMB + 128] = np.where(p < f, 0.0, BIGM)
    c[:, K_NMB:K_NMB + 128] = np.where(f <= p, 0.0, -BIGM)
    c[:, K_TRF:K_TRF + 128] = (p <= f)
    c[:, K_TRB:K_TRB + 128] = (p >= f)
    perm = np.zeros((128, 128), np.float32)
    for base in (0, 64):
        for d in range(8):
            perm[base + d + 8, base + d] = 1.0
            perm[base + d, base + d + 8] = 1.0
    c[:, K_PERM:K_PERM + 128] = perm
    c[:, K_ONE:K_ONE + 128] = 1.0
    inv_freq = (500000.0 ** (-(np.arange(0, 16, 2, dtype=np.float32) / np.float32(16)))).astype(np.float32)
    for base in (0, 64):
        for d in range(8):
            c[base + d, K_INVF] = inv_freq[d]
            c[base + d + 8, K_INVF] = inv_freq[d]
            c[base + d, K_SGN] = -1.0
            c[base + d + 8, K_SGN] = 1.0
    return c


def _host_params(inputs):
    prm = np.zeros((128, L_, NPRM), np.float32)
    for l in range(L_):
        cw = inputs["conv_w"][l]
        cwr = cw.reshape(5, 3, 8, 128).transpose(3, 1, 2, 0)
        prm[:, l, P_CW:P_CW + 120] = cwr.reshape(128, 120)
        prm[:, l, P_ALOG:P_ALOG + 16] = inputs["a_log"][l].reshape(1, 16)
        prm[:, l, P_DTB:P_DTB + 16] = inputs["dt_bias"][l].reshape(1, 16)
        prm[:, l, P_GNW:P_GNW + 128] = inputs["gdn_norm_w"][l].reshape(1, 128)
        prm[:, l, P_SLW:P_SLW + 128] = inputs["diff_subln_w"][l].reshape(1, 128)
        prm[:, l, P_LAM:P_LAM + 256] = inputs["diff_lambda"][l].reshape(1, 256)
    nw = np.zeros((L_ + 1, 128, D_), np.float32)
    for l in range(L_):
        nw[l] = inputs["norm_w"][l][None, :]
    nw[L_] = inputs["final_norm_w"][None, :]
    return prm.reshape(128, L_ * NPRM), nw


def _in_maps(inputs):
    cst = _host_consts()
    prm, nw = _host_params(inputs)
    f32 = lambda a: np.ascontiguousarray(np.asarray(a, dtype=np.float32))
    w_in, w_pa, w_pb, w_out = (f32(inputs[k]) for k in ("w_in", "w_pa", "w_pb", "w_out"))
    maps = []
    for b in range(8):
        maps.append({
            "x": f32(inputs["x"][b]),
            "pos": np.ascontiguousarray(np.broadcast_to(np.asarray(inputs["positions"][b], dtype=np.int32)[None, :], (128, S_))),
            "w_in": w_in, "w_pa": w_pa, "w_pb": w_pb, "w_out": w_out,
            "cst": cst, "prm": prm, "nw": nw,
        })
    return maps


def kernel(**inputs):
    nc = build()
    maps = _in_maps(inputs)
    res = run_bass_kernel_spmd(nc, maps, core_ids=list(range(8)))
    return np.stack([np.asarray(r["out"], dtype=np.float32) for r in res.results], axis=0)
```

```python
import math
import numpy as np
import concourse.bass as bass
import concourse.mybir as mybir
from concourse.bass_utils import run_bass_kernel_spmd

F32, BF16, I32 = mybir.dt.float32, mybir.dt.bfloat16, mybir.dt.int32
AF = mybir.ActivationFunctionType
ALU = mybir.AluOpType
AX = mybir.AxisListType

S_ = 4096
D_ = 1024
L_ = 4
NIN = 10272
NT = 32
EPS = 1e-6
C_QA, C_KA, C_VA, C_AB, C_ZA, C_QB, C_KB, C_VB, C_ZB, C_GA, C_GB = (
    0, 1024, 2048, 3072, 3104, 4128, 5152, 6176, 7200, 8224, 9248)
NPRM = 664
P_CW, P_ALOG, P_DTB, P_GNW, P_SLW, P_LAM = 0, 120, 136, 152, 280, 408
NCST = 9 * 128 + 2
K_ID, K_PMF, K_NMF, K_PMB, K_NMB, K_TRF, K_TRB, K_PERM, K_ONE = [i * 128 for i in range(9)]
K_INVF, K_SGN = 9 * 128, 9 * 128 + 1
TWO_PI = 2.0 * math.pi
TWO_PI_HI = 6.28125
TWO_PI_LO = TWO_PI - TWO_PI_HI
BIGM = 30000.0


class Tok:
    __slots__ = ("w", "r")

    def __init__(self):
        self.w = None
        self.r = {}


class Sched:
    ENG = ("tensor", "scalar", "vector", "gpsimd", "sync")

    def __init__(self, nc, stack):
        self.nc = nc
        self.sems = []
        self.E = {}
        for n in self.ENG:
            s = stack.enter_context(nc.semaphore("s_" + n))
            self.sems.append(s)
            self.E[n] = dict(si=len(self.sems) - 1, cnt=0, waited={}, prog=[], slots=[], nxt=0)
        for n in ("sync", "gpsimd", "scalar"):
            for k in range(8):
                s = stack.enter_context(nc.semaphore("d_%s%d" % (n, k)))
                self.sems.append(s)
                self.E[n]["slots"].append([len(self.sems) - 1, 0])
        self.nops = 0

    def _deps(self, e, r, w):
        need = {}
        for t in r:
            if t.w is not None:
                s, v = t.w
                if need.get(s, 0) < v:
                    need[s] = v
        for t in w:
            if t.w is not None:
                s, v = t.w
                if need.get(s, 0) < v:
                    need[s] = v
            for s, v in t.r.items():
                if need.get(s, 0) < v:
                    need[s] = v
        waits = []
        for s, v in need.items():
            if s == e["si"] and e is self.E["tensor"]:
                continue
            if e["waited"].get(s, 0) >= v:
                continue
            e["waited"][s] = v
            waits.append((s, v))
        return waits

    def _mark(self, me, r, w):
        s, v = me
        for t in r:
            if t.r.get(s, 0) < v:
                t.r[s] = v
        for t in w:
            t.w = me
            t.r = {}

    def op(self, en, fn, r=(), w=()):
        e = self.E[en]
        waits = self._deps(e, r, w)
        e["cnt"] += 1
        e["prog"].append((waits, fn, (e["si"], 1)))
        self._mark((e["si"], e["cnt"]), r, w)
        self.nops += 1

    def dma(self, en, out, in_, r=(), w=()):
        e = self.E[en]
        waits = self._deps(e, r, w)
        slot = e["slots"][e["nxt"] % len(e["slots"])]
        e["nxt"] += 1
        if slot[1] > 0 and e["waited"].get(slot[0], 0) < slot[1]:
            waits.append((slot[0], slot[1]))
            e["waited"][slot[0]] = slot[1]
        slot[1] += 16
        e["prog"].append((waits, (lambda h, o=out, i=in_: h.dma_start(out=o, in_=i)), (slot[0], 16)))
        self._mark((slot[0], slot[1]), r, w)
        self.nops += 1

    def barrier(self):
        tgt = []
        for n in self.ENG:
            e = self.E[n]
            if e["cnt"] > 0:
                tgt.append((e["si"], e["cnt"]))
            for sl in e["slots"]:
                if sl[1] > 0:
                    tgt.append((sl[0], sl[1]))
        for n in self.ENG:
            e = self.E[n]
            waits = []
            for s, v in tgt:
                if s == e["si"]:
                    continue
                if e["waited"].get(s, 0) >= v:
                    continue
                e["waited"][s] = v
                waits.append((s, v))
            if waits:
                e["prog"].append((waits, None, None))

    def replay(self, en, h):
        for waits, fn, inc in self.E[en]["prog"]:
            for s, v in waits:
                h.wait_ge(self.sems[s], v)
            if fn is not None:
                ins = fn(h)
                ins.then_inc(self.sems[inc[0]], inc[1])


class Arena:
    def __init__(self, big, nbytes):
        self.big = big
        self.cap = nbytes
        self.top = 0

    def mark(self):
        return self.top

    def reset(self, m):
        self.top = m

    def alloc(self, nbytes, dt=BF16):
        off = (self.top + 63) // 64 * 64
        assert off + nbytes <= self.cap, ("SBUF arena overflow", off, nbytes, self.cap)
        self.top = off + nbytes
        ap = self.big[:, off // 2:(off + nbytes) // 2]
        if dt is not BF16:
            ap = ap.bitcast(dt)
        return ap


def v3(ap, a, b):
    return ap.rearrange("p (a b) -> p a b", a=a, b=b)


def build(n_layers=L_, dbg=False, stop_after=None):
    from contextlib import ExitStack
    nc = bass.Bass("TRN2", target_bir_lowering=False)

    def din(name, shape, dt=F32):
        return nc.dram_tensor(name, shape, dt, kind="ExternalInput").ap()

    x_d = din("x", [S_, D_])
    pos_d = din("pos", [128, S_], I32)
    w_in_d = din("w_in", [L_, D_, NIN])
    w_pa_d = din("w_pa", [L_, D_, D_])
    w_pb_d = din("w_pb", [L_, D_, D_])
    w_out_d = din("w_out", [L_, D_, D_])
    cst_d = din("cst", [128, NCST])
    prm_d = din("prm", [128, L_ * NPRM])
    nw_d = din("nw", [L_ + 1, 128, D_])
    out_d = nc.dram_tensor("out", [S_, D_], F32, kind="ExternalOutput").ap()
    kS = "ExternalOutput" if dbg else "Internal"
    scr_fm = nc.dram_tensor("scr_fm", [7, 8, 128, S_], BF16, kind=kS).ap()
    scr_tm = nc.dram_tensor("scr_tm", [3, 8, 128, S_], BF16, kind=kS).ap()
    scr_y = nc.dram_tensor("scr_y", [2, 8, 128, S_], BF16, kind=kS).ap()
    scr_x = nc.dram_tensor("scr_x", [S_, D_], F32, kind=kS).ap()

    with ExitStack() as stack:
        ARENA_BYTES = 212000
        big = stack.enter_context(nc.sbuf_tensor("big", [128, ARENA_BYTES // 2], BF16))
        psp = [stack.enter_context(nc.psum_tensor("pp%d" % i, [128, 1024], F32)) for i in range(4)]
        banks = [psp[i // 2][:, (i % 2) * 512:(i % 2 + 1) * 512] for i in range(8)]
        SC = Sched(nc, stack)
        A = Arena(big, ARENA_BYTES)
        op, dma = SC.op, SC.dma

        cst = A.alloc(NCST * 4, F32)
        prm = A.alloc(L_ * NPRM * 4, F32)
        cbf = A.alloc(4 * 128 * 2)
        ident_bf, perm_bf, ones_bf = cbf[:, 0:128], cbf[:, 128:256], cbf[:, 256:384]
        ccol = A.alloc(16 * 4, F32)
        Ct = A.alloc(S_ * 2)
        St = A.alloc(S_ * 2)
        abt = A.alloc(NT * 32 * 4, F32)
        nwb = A.alloc(D_ * 4, F32)
        lamc = A.alloc(8 * 4, F32)
        t_cst, t_prm, t_cbf, t_ccol, t_rope, t_abt, t_gt, t_nwb, t_lam = [Tok() for _ in range(9)]
        ident_f = cst[:, K_ID:K_ID + 128]
        ones_f = cst[:, K_ONE:K_ONE + 128]
        m_H = A.mark()
        hT = A.alloc(8 * S_ * 2)
        hT3 = v3(hT, 8, S_)
        t_hT = [Tok() for _ in range(NT)]
        m_W = A.mark()

        pb = [Tok() for _ in range(8)]

        dma("sync", cst, cst_d, w=[t_cst])
        dma("sync", prm, prm_d, w=[t_prm])
        op("vector", lambda h: h.tensor_copy(out=cbf[:, 0:128], in_=cst[:, K_ID:K_ID + 128]), r=[t_cst], w=[t_cbf])
        op("vector", lambda h: h.tensor_copy(out=cbf[:, 128:256], in_=cst[:, K_PERM:K_PERM + 128]), r=[t_cst], w=[t_cbf])
        op("vector", lambda h: h.tensor_copy(out=cbf[:, 256:384], in_=cst[:, K_ONE:K_ONE + 128]), r=[t_cst], w=[t_cbf])
        CC_EPS, CC_ONE, CC_DKEPS, CC_PI2 = 0, 1, 2, 3
        for ci, val in ((CC_EPS, EPS), (CC_ONE, 1.0), (CC_DKEPS, 128.0 * EPS), (CC_PI2, math.pi / 2)):
            op("vector", lambda h, ci=ci, val=val: h.memset(ccol[:, ci:ci + 1], val), w=[t_ccol])

        def cc(i):
            return ccol[:, i:i + 1]

        m0 = A.mark()
        posi = A.alloc(S_ * 4, I32)
        ang = A.alloc(S_ * 4, F32)
        uu = A.alloc(S_ * 4, F32)
        rr = A.alloc(S_ * 4, F32)
        ki = A.alloc(S_ * 4, I32)
        tp = Tok()
        dma("sync", posi, pos_d, w=[tp])
        V = "vector"
        op(V, lambda h: h.tensor_copy(out=ang, in_=posi), r=[tp], w=[tp])
        op(V, lambda h: h.tensor_scalar(out=ang, in0=ang, scalar1=cst[:, K_INVF:K_INVF + 1], scalar2=None, op0=ALU.mult), r=[tp, t_cst], w=[tp])
        op(V, lambda h: h.tensor_scalar(out=uu, in0=ang, scalar1=1.0 / TWO_PI, scalar2=None, op0=ALU.mult), r=[tp], w=[tp])
        op(V, lambda h: h.tensor_copy(out=ki, in_=uu), r=[tp], w=[tp])
        op(V, lambda h: h.tensor_copy(out=uu, in_=ki), r=[tp], w=[tp])
        op(V, lambda h: h.scalar_tensor_tensor(out=rr, in0=uu, scalar=-TWO_PI_HI, in1=ang, op0=ALU.mult, op1=ALU.add), r=[tp], w=[tp])
        op(V, lambda h: h.scalar_tensor_tensor(out=rr, in0=uu, scalar=-TWO_PI_LO, in1=rr, op0=ALU.mult, op1=ALU.add), r=[tp], w=[tp])

        def fixup(r_):
            op(V, lambda h: h.tensor_scalar(out=uu, in0=r_, scalar1=math.pi, scalar2=None, op0=ALU.is_gt), r=[tp], w=[tp])
            op(V, lambda h: h.scalar_tensor_tensor(out=r_, in0=uu, scalar=-TWO_PI, in1=r_, op0=ALU.mult, op1=ALU.add), r=[tp], w=[tp])
            op(V, lambda h: h.tensor_scalar(out=uu, in0=r_, scalar1=-math.pi, scalar2=None, op0=ALU.is_lt), r=[tp], w=[tp])
            op(V, lambda h: h.scalar_tensor_tensor(out=r_, in0=uu, scalar=TWO_PI, in1=r_, op0=ALU.mult, op1=ALU.add), r=[tp], w=[tp])

        fixup(rr)
        op("scalar", lambda h: h.activation(out=St, in_=rr, func=AF.Sin, scale=cst[:, K_SGN:K_SGN + 1]), r=[tp, t_cst], w=[t_rope])
        op(V, lambda h: h.tensor_scalar(out=ang, in0=rr, scalar1=math.pi / 2, scalar2=None, op0=ALU.add), r=[tp], w=[tp])
        fixup(ang)
        op("scalar", lambda h: h.activation(out=Ct, in_=ang, func=AF.Sin), r=[tp], w=[t_rope])
        SC.barrier()
        A.reset(m0)

        def norm_tile(xt, t_x, t, tmp, final, bank):
            junk, ss, hn, t_tmp = tmp["junk"], tmp["ss"], tmp["hn"], tmp["tok"]
            op("scalar", lambda h: h.activation(out=junk, in_=xt, func=AF.Square, accum_out=ss[:, 0:1]), r=[t_x], w=[t_tmp])
            op("scalar", lambda h: h.activation(out=ss[:, 1:2], in_=ss[:, 0:1], func=AF.Sqrt, scale=1.0 / D_, bias=cc(CC_EPS)), r=[t_ccol], w=[t_tmp])
            op(V, lambda h: h.reciprocal(out=ss[:, 2:3], in_=ss[:, 1:2]), w=[t_tmp])
            if final:
                op(V, lambda h: h.scalar_tensor_tensor(out=xt, in0=xt, scalar=ss[:, 2:3], in1=nwb, op0=ALU.mult, op1=ALU.mult), r=[t_nwb, t_tmp], w=[t_x])
                dma("sync", out_d[t * 128:(t + 1) * 128, :], xt, r=[t_x])
                return
            op(V, lambda h: h.scalar_tensor_tensor(out=hn, in0=xt, scalar=ss[:, 2:3], in1=nwb, op0=ALU.mult, op1=ALU.mult), r=[t_x, t_nwb], w=[t_tmp])
            pbk = banks[bank][:, :].bitcast(BF16)
            for kc in range(8):
                op("tensor", lambda h, kc=kc: h.transpose(out=pbk[:, kc * 128:(kc + 1) * 128], in_=hn[:, kc * 128:(kc + 1) * 128], identity=ident_bf),
                   r=[t_tmp, t_cbf], w=[pb[bank]])
            op("scalar", lambda h: h.copy(out=hT3[:, :, t * 128:(t + 1) * 128], in_=v3(pbk, 8, 128)), r=[pb[bank]], w=[t_hT[t]])

        def norm_tmp():
            hn_ = A.alloc(D_ * 2)
            return dict(junk=hn_, ss=A.alloc(16, F32), hn=hn_, tok=Tok())

        m0 = A.mark()
        dma("sync", nwb, nw_d[0], w=[t_nwb])
        xts = [(A.alloc(D_ * 4, F32), Tok()) for _ in range(3)]
        ntm = [norm_tmp() for _ in range(2)]
        for t in range(NT):
            xt, tx = xts[t % 3]
            dma("sync", xt, x_d[t * 128:(t + 1) * 128, :], w=[tx])
            dma("sync", scr_x[t * 128:(t + 1) * 128, :], xt, r=[tx])
            norm_tile(xt, tx, t, ntm[t % 2], False, t % 2)
        SC.barrier()
        A.reset(m0)

        def phase_B(l, pl):
            T, AC, G = "tensor", "scalar", "gpsimd"
            gta = A.alloc(16 * 1024, F32)

            def gflat(i):
                return gta[:, i * 256:(i + 1) * 256]

            def gtile(i):
                return v3(gflat(i), NT, 8)

            beta = [gtile(0), gtile(1)]
            g_ = [gtile(2), gtile(3)]
            Gc = [gtile(4), gtile(5)]
            nb = [gtile(6), gtile(7)]
            eG = [gtile(8), gtile(9)]
            nbeG = [gtile(10), gtile(11)]
            eGt = [gtile(12), gtile(13)]
            nA = A.alloc(16 * 4, F32)
            t_g = Tok()
            abt3 = v3(abt, NT, 32)
            for d in range(2):
                sl = slice(d * 256, (d + 1) * 256)
                op(AC, lambda h, d=d: h.activation(out=beta[d], in_=abt3[:, :, d * 8:(d + 1) * 8], func=AF.Sigmoid), r=[t_abt], w=[t_g])
                dtb = pl[:, P_DTB + 8 * d:P_DTB + 8 * d + 8].unsqueeze(1).broadcast_to([128, NT, 8])
                op(V, lambda h, d=d, dtb=dtb: h.tensor_tensor(out=g_[d], in0=abt3[:, :, 16 + 8 * d:24 + 8 * d], in1=dtb, op=ALU.add), r=[t_abt, t_prm], w=[t_g])
                op(AC, lambda h, d=d: h.activation(out=g_[d], in_=g_[d], func=AF.Exp), w=[t_g])
                op(AC, lambda h, d=d: h.activation(out=g_[d], in_=g_[d], func=AF.Ln, bias=cc(CC_ONE)), r=[t_ccol], w=[t_g])
                op(AC, lambda h, d=d: h.activation(out=nA[:, 8 * d:8 * d + 8], in_=pl[:, P_ALOG + 8 * d:P_ALOG + 8 * d + 8], func=AF.Exp), r=[t_prm], w=[t_g])
                nAb = nA[:, 8 * d:8 * d + 8].unsqueeze(1).broadcast_to([128, NT, 8])
                op(V, lambda h, d=d, nAb=nAb: h.scalar_tensor_tensor(out=g_[d], in0=g_[d], scalar=-1.0, in1=nAb, op0=ALU.mult, op1=ALU.mult), w=[t_g])
                tri = cst[:, K_TRF:K_TRF + 128] if d == 0 else cst[:, K_TRB:K_TRB + 128]
                op(T, lambda h, d=d, sl=sl, tri=tri: h.matmul(banks[0][:, sl], lhsT=tri, rhs=gflat(2 + d), start=True, stop=True), r=[t_g, t_cst], w=[pb[0]])
                op(T, lambda h, d=d, sl=sl: h.matmul(banks[1][:, sl], lhsT=ones_f, rhs=gflat(2 + d), start=True, stop=True), r=[t_g, t_cst], w=[pb[1]])
                op(V, lambda h, d=d, sl=sl: h.tensor_copy(out=gflat(4 + d), in_=banks[0][:, sl]), w=[t_g, pb[0]])
                op(AC, lambda h, d=d, sl=sl: h.activation(out=gflat(8 + d), in_=banks[0][:, sl], func=AF.Exp), w=[t_g, pb[0]])
                op(AC, lambda h, d=d, sl=sl: h.activation(out=gflat(12 + d), in_=banks[1][:, sl], func=AF.Exp), w=[t_g, pb[1]])
                op(V, lambda h, d=d, sl=sl: h.tensor_tensor(out=gflat(14 + d), in0=banks[1][:, sl], in1=gflat(4 + d), op=ALU.subtract), w=[t_g, pb[1]])
                op(AC, lambda h, d=d: h.activation(out=gflat(14 + d), in_=gflat(14 + d), func=AF.Exp), w=[t_g])
                op(V, lambda h, d=d: h.tensor_scalar(out=gflat(6 + d), in0=gflat(d), scalar1=-1.0, scalar2=None, op0=ALU.mult), w=[t_g])
                op(V, lambda h, d=d: h.tensor_tensor(out=gflat(10 + d), in0=gflat(6 + d), in1=gflat(8 + d), op=ALU.mult), w=[t_g])
            eD = [gtile(14), gtile(15)]

            xpad = A.alloc(4100 * 2)
            acc = A.alloc(S_ * 4, F32)
            sqr = A.alloc(S_ * 2)
            rsb = [A.alloc(512 * 4, F32) for _ in range(2)]
            qT = A.alloc(S_ * 2)
            kT = A.alloc(S_ * 2)
            vb = [A.alloc(S_ * 2) for _ in range(2)]
            kd = [A.alloc(S_ * 2) for _ in range(2)]
            za = A.alloc(S_ * 2)
            oacc = A.alloc(S_ * 4, F32)
            oacc3 = v3(oacc, NT, 128)
            yst = xpad[:, 2:2 + S_]
            ssum = A.alloc(96 * 4, F32)
            RN = 4
            NI = 4
            TTr = [[A.alloc(512, F32) for _ in range(RN)] for _ in range(2)]
            QKr = [[A.alloc(256) for _ in range(RN)] for _ in range(2)]
            tA = [A.alloc(512, F32) for _ in range(NI)]
            tB = [A.alloc(512, F32) for _ in range(NI)]
            dec = [A.alloc(512, F32) for _ in range(NI)]
            decT = [A.alloc(256) for _ in range(NI)]
            pair = [[A.alloc(1024, F32) for _ in range(2)] for _ in range(NI)]
            X = [[A.alloc(512, F32) for _ in range(2)] for _ in range(NI)]
            rhs = [A.alloc(512, F32) for _ in range(2)]
            vnew = [A.alloc(256) for _ in range(2)]
            tmp = [A.alloc(512, F32) for _ in range(2)]
            tmp2 = [A.alloc(512, F32) for _ in range(2)]
            S32 = [A.alloc(512, F32) for _ in range(2)]
            Sbf = [A.alloc(256) for _ in range(2)]
            tk = lambda: Tok()
            t_xp, t_acc, t_sqr, t_qT, t_kT, t_za, t_ss = [Tok() for _ in range(7)]
            t_yst = t_xp
            t_rs = [tk(), tk()]
            t_vb = [tk(), tk()]
            t_kd = [tk(), tk()]
            t_oacc = [tk() for _ in range(NT)]
            t_TT = [[tk() for _ in range(RN)] for _ in range(2)]
            t_QK = [[tk() for _ in range(RN)] for _ in range(2)]
            t_rhs, t_vnew, t_tmp, t_tmp2, t_S32, t_S = [[tk(), tk()] for _ in range(6)]
            t_tA, t_tB, t_dec, t_decT = [[tk() for _ in range(NI)] for _ in range(4)]
            t_pair = [[tk(), tk()] for _ in range(NI)]
            t_X = [[tk(), tk()] for _ in range(NI)]
            KK = [banks[it][:, 0:128] for it in range(NI)]
            QKT = [banks[it][:, 128:256] for it in range(NI)]
            Gb = [banks[it][:, 256:384] for it in range(NI)]
            MT = [banks[it][:, 384:512] for it in range(NI)]
            PPQ = [banks[it][:, 0:256] for it in range(NI)]
            XPQ = [banks[it][:, 256:384] for it in range(NI)]
            KS = [banks[4 + d][:, 0:128] for d in range(2)]
            VN = [banks[4 + d][:, 128:256] for d in range(2)]
            O1 = [banks[4 + d][:, 256:384] for d in range(2)]
            O2 = [banks[4 + d][:, 384:512] for d in range(2)]
            DS = [banks[6 + d][:, 0:128] for d in range(2)]
            PMm = [cst[:, K_PMF:K_PMF + 128], cst[:, K_PMB:K_PMB + 128]]
            NMm = [cst[:, K_NMF:K_NMF + 128], cst[:, K_NMB:K_NMB + 128]]

            op(V, lambda h: h.memset(xpad[:, 0:2], 0.0), w=[t_xp])
            op(V, lambda h: h.memset(xpad[:, 4098:4100], 0.0), w=[t_xp])

            for h_ in range(8):
                def tr_scaled(srcT, t_src, dst, t_dst, sc):
                    for cg in range(4):
                        bk = 6 + (cg % 2)
                        pbk = banks[bk][:, :].bitcast(BF16)
                        for j in range(8):
                            c = cg * 8 + j
                            op(T, lambda h, j=j, c=c, pbk=pbk: h.transpose(out=pbk[:, j * 128:(j + 1) * 128], in_=srcT[:, c * 128:(c + 1) * 128], identity=ident_bf),
                               r=[t_src, t_cbf], w=[pb[bk]])
                        for d in range(2):
                            scb = sc[d][:, cg * 8:(cg + 1) * 8, h_:h_ + 1].broadcast_to([128, 8, 128])
                            op(V, lambda h, d=d, cg=cg, pbk=pbk, scb=scb: h.tensor_tensor(out=v3(dst[d], NT, 128)[:, cg * 8:(cg + 1) * 8, :], in0=v3(pbk, 8, 128), in1=scb, op=ALU.mult),
                               r=[pb[bk], t_g], w=[t_dst[d]])

                for ti in range(3):
                    dma("sync", xpad[:, 2:4098], scr_fm[ti, h_], w=[t_xp])
                    base = P_CW + (ti * 8 + h_) * 5
                    op(V, lambda h, base=base: h.tensor_scalar(out=acc, in0=xpad[:, 0:4096], scalar1=pl[:, base:base + 1], scalar2=None, op0=ALU.mult), r=[t_xp, t_prm], w=[t_acc])
                    for k in range(1, 5):
                        op(V, lambda h, base=base, k=k: h.scalar_tensor_tensor(out=acc, in0=xpad[:, k:k + 4096], scalar=pl[:, base + k:base + k + 1], in1=acc, op0=ALU.mult, op1=ALU.add),
                           r=[t_xp, t_prm], w=[t_acc])
                    if ti == 2:
                        op(AC, lambda h: h.activation(out=sqr, in_=acc, func=AF.Silu), r=[t_acc], w=[t_sqr])
                        tr_scaled(sqr, t_sqr, vb, t_vb, beta)
                    else:
                        op(AC, lambda h: h.activation(out=acc, in_=acc, func=AF.Silu), w=[t_acc])
                        op(AC, lambda h: h.activation(out=sqr, in_=acc, func=AF.Square), r=[t_acc], w=[t_sqr])
                        dstT, t_dst = (qT, t_qT) if ti == 0 else (kT, t_kT)
                        for tt in range(8):
                            bk = 6 + (tt % 2)
                            tsl = slice(tt * 512, (tt + 1) * 512)
                            rs = rsb[tt % 2]
                            trs = t_rs[tt % 2]
                            op(T, lambda h, bk=bk, tsl=tsl: h.matmul(banks[bk][:, :], lhsT=ones_bf, rhs=sqr[:, tsl], start=True, stop=True), r=[t_sqr, t_cbf], w=[pb[bk]])
                            op(AC, lambda h, bk=bk, rs=rs, ti=ti: h.activation(out=rs, in_=banks[bk][:, :], func=AF.Ln, scale=(128.0 if ti == 0 else 1.0),
                                                                        bias=cc(CC_DKEPS if ti == 0 else CC_EPS)), r=[pb[bk], t_ccol], w=[trs])
                            op(AC, lambda h, rs=rs: h.activation(out=rs, in_=rs, func=AF.Exp, scale=-0.5), w=[trs])
                            op(V, lambda h, rs=rs, tsl=tsl, dstT=dstT: h.tensor_tensor(out=dstT[:, tsl], in0=acc[:, tsl], in1=rs, op=ALU.mult), r=[t_acc, trs], w=[t_dst])
                        if ti == 1:
                            tr_scaled(kT, t_kT, kd, t_kd, eD)
                dma("sync", za, scr_tm[1, h_], w=[t_za])
                op(G, lambda h: h.memset(oacc, 0.0), w=t_oacc)
                for d in range(2):
                    op(G, lambda h, d=d: h.memset(S32[d], 0.0), w=[t_S32[d]])
                    op(G, lambda h, d=d: h.memset(Sbf[d], 0.0), w=[t_S[d]])

                def prep_batch(steps, hh=h_):
                    items = []
                    for si, i in enumerate(steps):
                        for d in range(2):
                            items.append((si * 2 + d, d, (i, NT - 1 - i)[d], i % RN))
                    for it, d, c, s in items:
                        kc_ = kT[:, c * 128:(c + 1) * 128]
                        qc_ = qT[:, c * 128:(c + 1) * 128]
                        op(T, lambda h, it=it, kc_=kc_: h.matmul(KK[it], lhsT=kc_, rhs=kc_, start=True, stop=True), r=[t_kT], w=[pb[it]])
                        op(T, lambda h, it=it, kc_=kc_, qc_=qc_: h.matmul(QKT[it], lhsT=kc_, rhs=qc_, start=True, stop=True), r=[t_kT, t_qT], w=[pb[it]])
                        op(G, lambda h, it=it, d=d, c=c: h.tensor_scalar(out=tA[it], in0=ident_f, scalar1=Gc[d][:, c, hh:hh + 1], scalar2=None, op0=ALU.mult), r=[t_g, t_cst], w=[t_tA[it]])
                        op(T, lambda h, it=it: h.matmul(Gb[it], lhsT=ones_f, rhs=tA[it], start=True, stop=True), r=[t_tA[it], t_cst], w=[pb[it]])
                    yield
                    for it, d, c, s in items:
                        op(V, lambda h, it=it, d=d, c=c: h.scalar_tensor_tensor(out=tB[it], in0=Gb[it], scalar=Gc[d][:, c, hh:hh + 1], in1=PMm[d], op0=ALU.subtract, op1=ALU.add),
                           r=[t_g, t_cst], w=[t_tB[it], pb[it]])
                    yield
                    for it, d, c, s in items:
                        op(AC, lambda h, it=it: h.activation(out=dec[it], in_=tB[it], func=AF.Exp, scale=-1.0), r=[t_tB[it]], w=[t_dec[it]])
                        op(V, lambda h, it=it, d=d, c=c: h.scalar_tensor_tensor(out=tA[it], in0=Gb[it], scalar=Gc[d][:, c, hh:hh + 1], in1=NMm[d], op0=ALU.subtract, op1=ALU.add),
                           r=[t_g, t_cst], w=[t_tA[it], pb[it]])
                    yield
                    for it, d, c, s in items:
                        op(AC, lambda h, it=it: h.activation(out=decT[it], in_=tA[it], func=AF.Exp), r=[t_tA[it]], w=[t_decT[it]])
                        op(V, lambda h, it=it, d=d, c=c: h.scalar_tensor_tensor(out=pair[it][0][:, 0:128], in0=KK[it], scalar=nb[d][:, c, hh:hh + 1], in1=dec[it], op0=ALU.mult, op1=ALU.mult),
                           r=[t_dec[it], t_g], w=[t_pair[it][0], pb[it]])
                    yield
                    for it, d, c, s in items:
                        op(V, lambda h, it=it, d=d, s=s: h.tensor_tensor(out=QKr[d][s], in0=QKT[it], in1=decT[it], op=ALU.mult), r=[t_decT[it]], w=[t_QK[d][s], pb[it]])
                        op(T, lambda h, it=it: h.transpose(out=MT[it], in_=pair[it][0][:, 0:128], identity=ident_f), r=[t_pair[it][0], t_cst], w=[pb[it]])
                    yield
                    for it, d, c, s in items:
                        op(AC, lambda h, it=it: h.copy(out=pair[it][0][:, 128:256], in_=MT[it]), w=[t_pair[it][0], pb[it]])
                        op(G, lambda h, it=it: h.tensor_tensor(out=X[it][0], in0=pair[it][0][:, 128:256], in1=ident_f, op=ALU.add), r=[t_pair[it][0], t_cst], w=[t_X[it][0]])
                    yield
                    for lvl in range(1, 7):
                        a = (lvl - 1) % 2
                        b = lvl % 2
                        n = 256 if lvl < 6 else 128
                        for it, d, c, s in items:
                            P_ = pair[it][a][:, 0:128]
                            PT_ = pair[it][a][:, 128:256]
                            op(T, lambda h, it=it, P_=P_, PT_=PT_: h.matmul(PPQ[it][:, 0:128], lhsT=PT_, rhs=P_, start=True, stop=True), r=[t_pair[it][a]], w=[pb[it]])
                            if lvl < 6:
                                op(T, lambda h, it=it, P_=P_, PT_=PT_: h.matmul(PPQ[it][:, 128:256], lhsT=P_, rhs=PT_, start=True, stop=True), r=[t_pair[it][a]], w=[pb[it]])
                        yield
                        for it, d, c, s in items:
                            op(AC, lambda h, it=it, b=b, n=n: h.copy(out=pair[it][b][:, 0:n], in_=PPQ[it][:, 0:n]), w=[t_pair[it][b], pb[it]])
                        yield
                        for it, d, c, s in items:
                            op(T, lambda h, it=it, a=a, b=b: h.matmul(XPQ[it], lhsT=pair[it][b][:, 0:128], rhs=X[it][a], start=True, stop=True), r=[t_pair[it][b], t_X[it][a]], w=[pb[it]])
                        yield
                        for it, d, c, s in items:
                            dst, tdst = (X[it][b], t_X[it][b]) if lvl < 6 else (TTr[d][s], t_TT[d][s])
                            op(V, lambda h, it=it, a=a, dst=dst: h.tensor_tensor(out=dst, in0=XPQ[it], in1=X[it][a], op=ALU.add), r=[t_X[it][a]], w=[tdst, pb[it]])
                        yield

                def chain_steps(steps, hh=h_):
                    for i in steps:
                        s = i % RN
                        cs = (i, NT - 1 - i)
                        for d in range(2):
                            c = cs[d]
                            op(T, lambda h, d=d, c=c: h.matmul(KS[d], lhsT=kT[:, c * 128:(c + 1) * 128], rhs=Sbf[d], start=True, stop=True), r=[t_kT, t_S[d]], w=[pb[4 + d]])
                        yield
                        for d in range(2):
                            c = cs[d]
                            op(V, lambda h, d=d, c=c: h.scalar_tensor_tensor(out=rhs[d], in0=KS[d], scalar=nbeG[d][:, c, hh:hh + 1], in1=v3(vb[d], NT, 128)[:, c, :], op0=ALU.mult, op1=ALU.add),
                               r=[t_g, t_vb[d]], w=[t_rhs[d], pb[4 + d]])
                        yield
                        for d in range(2):
                            op(T, lambda h, d=d, s=s: h.matmul(VN[d], lhsT=TTr[d][s], rhs=rhs[d], start=True, stop=True), r=[t_TT[d][s], t_rhs[d]], w=[pb[4 + d]])
                        yield
                        for d in range(2):
                            op(AC, lambda h, d=d: h.copy(out=vnew[d], in_=VN[d]), w=[t_vnew[d], pb[4 + d]])
                        yield
                        for d in range(2):
                            c = cs[d]
                            op(T, lambda h, d=d, c=c: h.matmul(DS[d], lhsT=v3(kd[d], NT, 128)[:, c, :], rhs=vnew[d], start=True, stop=True), r=[t_kd[d], t_vnew[d]], w=[pb[6 + d]])
                            op(T, lambda h, d=d, c=c: h.matmul(O1[d], lhsT=qT[:, c * 128:(c + 1) * 128], rhs=Sbf[d], start=True, stop=True), r=[t_qT, t_S[d]], w=[pb[4 + d]])
                            op(T, lambda h, d=d, s=s: h.matmul(O2[d], lhsT=QKr[d][s], rhs=vnew[d], start=True, stop=True), r=[t_QK[d][s], t_vnew[d]], w=[pb[4 + d]])
                        yield
                        for d in range(2):
                            c = cs[d]
                            op(V, lambda h, d=d, c=c: h.scalar_tensor_tensor(out=Sbf[d], in0=S32[d], scalar=eGt[d][:, c, hh:hh + 1], in1=DS[d], op0=ALU.mult, op1=ALU.add),
                               r=[t_g, t_S32[d]], w=[t_S[d], pb[6 + d]])
                            op(V, lambda h, d=d, c=c: h.scalar_tensor_tensor(out=S32[d], in0=S32[d], scalar=eGt[d][:, c, hh:hh + 1], in1=DS[d], op0=ALU.mult, op1=ALU.add),
                               r=[t_g], w=[t_S32[d], pb[6 + d]])
                            op(AC, lambda h, d=d, c=c: h.activation(out=tmp[d], in_=O1[d], func=AF.Copy, scale=eG[d][:, c, hh:hh + 1]), r=[t_g], w=[t_tmp[d], pb[4 + d]])
                        yield
                        for d in range(2):
                            c = cs[d]
                            op(V, lambda h, d=d: h.tensor_tensor(out=tmp2[d], in0=O2[d], in1=tmp[d], op=ALU.add), r=[t_tmp[d]], w=[t_tmp2[d], pb[4 + d]])
                            op(G, lambda h, d=d, c=c: h.tensor_tensor(out=oacc3[:, c, :], in0=oacc3[:, c, :], in1=tmp2[d], op=ALU.add), r=[t_tmp2[d]], w=[t_oacc[c]])
                        yield

                for _ in prep_batch((0, 1)):
                    pass
                for bb in range(NT // 2):
                    pg = prep_batch((2 * bb + 2, 2 * bb + 3)) if bb + 1 < NT // 2 else None
                    cg = chain_steps((2 * bb, 2 * bb + 1))
                    while pg is not None or cg is not None:
                        if pg is not None:
                            for _ in range(2):
                                if next(pg, "END") == "END":
                                    pg = None
                                    break
                        if cg is not None:
                            if next(cg, "END") == "END":
                                cg = None

                op(AC, lambda h: h.activation(out=acc, in_=oacc, func=AF.Square), r=t_oacc, w=[t_acc])
                op(V, lambda h: h.tensor_reduce(out=ssum[:, 0:32], in_=v3(acc, NT, 128), axis=AX.X, op=ALU.add), r=[t_acc], w=[t_ss])
                op(AC, lambda h: h.activation(out=ssum[:, 32:64], in_=ssum[:, 0:32], func=AF.Sqrt, scale=1.0 / 128, bias=cc(CC_EPS)), r=[t_ccol], w=[t_ss])
                op(V, lambda h: h.reciprocal(out=ssum[:, 64:96], in_=ssum[:, 32:64]), w=[t_ss])
                rb = ssum[:, 64:96].unsqueeze(2).broadcast_to([128, NT, 128])
                op(V, lambda h, rb=rb: h.tensor_tensor(out=oacc3, in0=oacc3, in1=rb, op=ALU.mult), r=[t_ss], w=t_oacc)
                gw = pl[:, P_GNW:P_GNW + 128].unsqueeze(1).broadcast_to([128, NT, 128])
                op(G, lambda h, gw=gw: h.tensor_tensor(out=oacc3, in0=oacc3, in1=gw, op=ALU.mult), r=[t_prm], w=t_oacc)
                op(V, lambda h: h.tensor_tensor(out=sqr, in0=oacc, in1=za, op=ALU.mult), r=t_oacc + [t_za], w=[t_sqr])
                for cg in range(4):
                    bk = 6 + (cg % 2)
                    pbk = banks[bk][:, :].bitcast(BF16)
                    for j in range(8):
                        c = cg * 8 + j
                        op(T, lambda h, j=j, c=c, pbk=pbk: h.transpose(out=pbk[:, j * 128:(j + 1) * 128], in_=sqr[:, c * 128:(c + 1) * 128], identity=ident_bf),
                           r=[t_sqr, t_cbf], w=[pb[bk]])
                    op(AC, lambda h, cg=cg, pbk=pbk: h.copy(out=yst[:, cg * 1024:(cg + 1) * 1024], in_=pbk), r=[pb[bk]], w=[t_yst])
                dma("sync", scr_y[0, h_], yst, r=[t_yst])

        def phase_C(l, pl, lam_init):
            T, AC, G = "tensor", "scalar", "gpsimd"
            om = 1.0 - lam_init
            ltmp = A.alloc(128 * 4, F32)
            lp = v3(pl[:, P_LAM:P_LAM + 256], 4, 64)
            op(V, lambda h: h.tensor_tensor(out=ltmp[:, 0:64], in0=lp[:, 0, :], in1=lp[:, 1, :], op=ALU.mult), r=[t_prm], w=[t_lam])
            op(V, lambda h: h.tensor_tensor(out=ltmp[:, 64:128], in0=lp[:, 2, :], in1=lp[:, 3, :], op=ALU.mult), r=[t_prm], w=[t_lam])
            op(V, lambda h: h.tensor_reduce(out=lamc[:, 0:2], in_=v3(ltmp, 2, 64), axis=AX.X, op=ALU.add), w=[t_lam])
            op(AC, lambda h: h.activation(out=lamc[:, 2:4], in_=lamc[:, 0:2], func=AF.Exp), w=[t_lam])
            op(V, lambda h: h.tensor_tensor(out=lamc[:, 4:5], in0=lamc[:, 2:3], in1=lamc[:, 3:4], op=ALU.subtract), w=[t_lam])
            op(V, lambda h: h.tensor_scalar(out=lamc[:, 5:6], in0=lamc[:, 4:5], scalar1=float(lam_init), scalar2=-1.0, op0=ALU.add, op1=ALU.mult), w=[t_lam])
            op(V, lambda h: h.memset(lamc[:, 6:7], EPS / (om * om)), w=[t_lam])
            nlam = lamc[:, 5:6]
            sl_scale = 1.0 / (128.0 * om * om)

            qraw = A.alloc(S_ * 2)
            kraw = A.alloc(S_ * 2)
            qT = A.alloc(S_ * 2)
            kT = A.alloc(S_ * 2)
            vb1 = A.alloc(NT * 130 * 2)
            vb13 = v3(vb1, NT, 130)
            zb = A.alloc(S_ * 2)
            zb3 = v3(zb, NT, 128)
            yst = A.alloc(S_ * 2)
            NE = 3
            Eb = [A.alloc(1024 * 2) for _ in range(NE)]
            rt1 = [A.alloc(512 * 4, F32) for _ in range(2)]
            rt2 = [A.alloc(512 * 4, F32) for _ in range(2)]
            osb = [A.alloc(8 * 130 * 4, F32) for _ in range(2)]
            a0 = [A.alloc(128 * 4, F32) for _ in range(2)]
            aa = [A.alloc(128 * 4, F32) for _ in range(2)]
            ytok = [A.alloc(128 * 2) for _ in range(2)]
            jk = [A.alloc(128 * 2) for _ in range(2)]
            rc = [A.alloc(8 * 4, F32) for _ in range(2)]
            t_qr, t_kr, t_qT, t_kT, t_v, t_z, t_yst = [Tok() for _ in range(7)]
            t_rt1 = [Tok(), Tok()]
            t_rt2 = [Tok(), Tok()]
            t_E = [Tok() for _ in range(NE)]
            t_osb = [Tok(), Tok()]
            t_ep = [Tok(), Tok()]
            op(V, lambda h: h.memset(vb13[:, :, 128:130], 1.0), w=[t_v])
            slwb = pl[:, P_SLW:P_SLW + 128].unsqueeze(1).broadcast_to([128, NT, 128])
            OG = []
            for gi in range(8):
                OG.append(banks[4 + gi // 3][:, (gi % 3) * 130:(gi % 3) * 130 + 129])

            for h_ in range(8):
                dma("sync", qraw, scr_fm[3, h_], w=[t_qr])
                dma("sync", kraw, scr_fm[4, h_], w=[t_kr])
                dma("sync", vb13[:, :, 0:128], scr_tm[0, h_].rearrange("p (a b) -> p a b", a=NT, b=128), w=[t_v])
                dma("sync", zb, scr_tm[2, h_], w=[t_z])
                op(G, lambda h: h.tensor_tensor(out=zb3, in0=zb3, in1=slwb, op=ALU.mult), r=[t_prm], w=[t_z])
                for raw, t_raw, dstT, t_dst in ((qraw, t_qr, qT, t_qT), (kraw, t_kr, kT, t_kT)):
                    for tt in range(8):
                        tsl = slice(tt * 512, (tt + 1) * 512)
                        rb_ = tt % 4
                        r1, r2, tr1, tr2 = rt1[tt % 2], rt2[tt % 2], t_rt1[tt % 2], t_rt2[tt % 2]
                        op(T, lambda h, raw=raw, tsl=tsl, rb_=rb_: h.matmul(banks[rb_][:, :], lhsT=perm_bf, rhs=raw[:, tsl], start=True, stop=True), r=[t_raw, t_cbf], w=[pb[rb_]])
                        op(V, lambda h, tsl=tsl, rb_=rb_, r1=r1: h.tensor_tensor(out=r1, in0=banks[rb_][:, :], in1=St[:, tsl], op=ALU.mult), r=[pb[rb_], t_rope], w=[tr1])
                        op(G, lambda h, raw=raw, tsl=tsl, r2=r2: h.tensor_tensor(out=r2, in0=raw[:, tsl], in1=Ct[:, tsl], op=ALU.mult), r=[t_raw, t_rope], w=[tr2])
                        op(V, lambda h, dstT=dstT, tsl=tsl, r1=r1, r2=r2: h.tensor_tensor(out=dstT[:, tsl], in0=r1, in1=r2, op=ALU.add), r=[tr1, tr2], w=[t_dst])
                ne = 0
                for qt in range(8):
                    qsl = slice(qt * 512, (qt + 1) * 512)
                    def emit_S(kt, qsl=qsl):
                        ksl = slice(kt * 128, (kt + 1) * 128)
                        for sm in range(2):
                            bk = (kt % 2) * 2 + sm
                            rows = slice(sm * 64, (sm + 1) * 64)
                            op(T, lambda h, bk=bk, rows=rows, ksl=ksl, qsl=qsl: h.matmul(banks[bk][:, :], lhsT=kT[rows, ksl], rhs=qT[rows, qsl], start=True, stop=True),
                               r=[t_kT, t_qT], w=[pb[bk]])

                    emit_S(0)
                    for kt in range(NT):
                        par = kt % 2
                        es = ne % NE
                        ne += 1
                        if kt + 1 < NT:
                            emit_S(kt + 1)
                        op(AC, lambda h, par=par, es=es: h.activation(out=Eb[es], in_=psp[par][:, :], func=AF.Exp, scale=0.125), r=[pb[2 * par], pb[2 * par + 1]], w=[t_E[es]])
                        for sm in range(2):
                            for sub in range(4):
                                gi = sub * 2 + sm
                                first_in_bank = gi in (0, 4, 6)
                                op(T, lambda h, gi=gi, sm=sm, es=es, sub=sub, kt=kt, fb=first_in_bank: h.matmul(
                                    OG[gi], lhsT=Eb[es][:, sm * 512 + sub * 128:sm * 512 + (sub + 1) * 128], rhs=vb13[:, kt, 0:129],
                                    start=(kt == 0 and fb), stop=(kt == NT - 1), skip_group_check=True),
                                   r=[t_E[es], t_v], w=[pb[4 + gi // 3]])
                    ob = osb[qt % 2]
                    tob = t_osb[qt % 2]
                    op(V, lambda h, ob=ob: h.tensor_copy(out=ob[:, 0:390], in_=banks[4][:, 0:390]), r=[pb[4]], w=[tob])
                    op(AC, lambda h, ob=ob: h.copy(out=ob[:, 390:780], in_=banks[5][:, 0:390]), r=[pb[5]], w=[tob])
                    op(V, lambda h, ob=ob: h.tensor_copy(out=ob[:, 780:1040], in_=banks[6][:, 0:260]), r=[pb[6]], w=[tob])
                    b7bf = banks[7][:, :].bitcast(BF16)
                    for sub in range(4):
                        e = sub % 2
                        c = qt * 4 + sub
                        o0 = ob[:, (sub * 2) * 130:(sub * 2) * 130 + 130]
                        o1 = ob[:, (sub * 2 + 1) * 130:(sub * 2 + 1) * 130 + 130]
                        rc_, a0_, aa_, yt_, jk_, tep = rc[e], a0[e], aa[e], ytok[e], jk[e], t_ep[e]
                        op(V, lambda h, rc_=rc_, o0=o0: h.reciprocal(out=rc_[:, 0:1], in_=o0[:, 128:129]), r=[tob], w=[tep])
                        op(V, lambda h, rc_=rc_, o1=o1: h.reciprocal(out=rc_[:, 1:2], in_=o1[:, 128:129]), r=[tob], w=[tep])
                        op(V, lambda h, rc_=rc_: h.tensor_tensor(out=rc_[:, 2:3], in0=rc_[:, 1:2], in1=nlam, op=ALU.mult), r=[t_lam], w=[tep])
                        op(V, lambda h, rc_=rc_, a0_=a0_, o0=o0: h.tensor_scalar(out=a0_, in0=o0[:, 0:128], scalar1=rc_[:, 0:1], scalar2=None, op0=ALU.mult), r=[tob], w=[tep])
                        op(V, lambda h, rc_=rc_, a0_=a0_, aa_=aa_, o1=o1: h.scalar_tensor_tensor(out=aa_, in0=o1[:, 0:128], scalar=rc_[:, 2:3], in1=a0_, op0=ALU.mult, op1=ALU.add), r=[tob], w=[tep])
                        op(V, lambda h, rc_=rc_, aa_=aa_, jk_=jk_: h.scalar_tensor_tensor(out=jk_, in0=aa_, scalar=1.0, in1=aa_, op0=ALU.mult, op1=ALU.mult, accum_out=rc_[:, 3:4]), w=[tep])
                        op(AC, lambda h, rc_=rc_: h.activation(out=rc_[:, 4:5], in_=rc_[:, 3:4], func=AF.Ln, scale=sl_scale, bias=lamc[:, 6:7]), r=[t_lam], w=[tep])
                        op(AC, lambda h, rc_=rc_: h.activation(out=rc_[:, 5:6], in_=rc_[:, 4:5], func=AF.Exp, scale=-0.5), w=[tep])
                        op(V, lambda h, rc_=rc_, aa_=aa_, yt_=yt_, c=c: h.scalar_tensor_tensor(out=yt_, in0=aa_, scalar=rc_[:, 5:6], in1=zb3[:, c, :], op0=ALU.mult, op1=ALU.mult), r=[t_z], w=[tep])
                        op(T, lambda h, yt_=yt_, sub=sub: h.transpose(out=b7bf[:, sub * 128:(sub + 1) * 128], in_=yt_, identity=ident_bf), r=[tep, t_cbf], w=[pb[7]])
                    op(AC, lambda h, qsl=qsl: h.copy(out=yst[:, qsl], in_=b7bf[:, 0:512]), r=[pb[7]], w=[t_yst])
                dma("sync", scr_y[1, h_], yst, r=[t_yst])

        for l in range(n_layers):
            pl = prm[:, l * NPRM:(l + 1) * NPRM]
            lam_init = 0.8 - 0.6 * math.exp(-0.3 * l)
            m0 = A.mark()
            wv = w_in_d[l].rearrange("(kc p) n -> p kc n", p=128)
            wb = [(A.alloc(8 * 512 * 2), Tok()) for _ in range(3)]
            stg = [(A.alloc(S_ * 2), Tok()) for _ in range(2)]
            stT = [(A.alloc(4 * S_ * 2), Tok()) for _ in range(1)]
            nwl = 0
            nst = 0
            nbk = 0
            fm_list = [(C_QA, 0, None), (C_KA, 1, None), (C_VA, 2, None), (C_QB, 3, None), (C_KB, 4, None),
                       (C_GA, 5, AF.Sigmoid), (C_GB, 6, AF.Sigmoid)]
            nev = 0
            for c0, ti, fn_ in fm_list:
                for g in range(2):
                    wt, twt = wb[nwl % 3]
                    nwl += 1
                    wt3 = v3(wt, 8, 512)
                    dma("gpsimd", wt3, wv[:, :, c0 + g * 512:c0 + (g + 1) * 512], w=[twt])
                    for hb in range(4):
                        st, tst = stg[nst % 2]
                        nst += 1
                        for tt in range(8):
                            bk = nbk % 4
                            nbk += 1
                            for kc in range(8):
                                op("tensor", lambda h, bk=bk, wt3=wt3, hb=hb, kc=kc, tt=tt: h.matmul(
                                    banks[bk][:, :], lhsT=wt3[:, kc, hb * 128:(hb + 1) * 128], rhs=hT3[:, kc, tt * 512:(tt + 1) * 512],
                                    start=(kc == 0), stop=(kc == 7)), r=[twt] + t_hT[tt * 4:tt * 4 + 4], w=[pb[bk]])
                            dst = st[:, tt * 512:(tt + 1) * 512]
                            if fn_ is not None:
                                op("scalar", lambda h, bk=bk, dst=dst, fn_=fn_: h.activation(out=dst, in_=banks[bk][:, :], func=fn_), r=[pb[bk]], w=[tst])
                            elif nev % 2 == 0:
                                op("scalar", lambda h, bk=bk, dst=dst: h.copy(out=dst, in_=banks[bk][:, :]), r=[pb[bk]], w=[tst])
                            else:
                                op("vector", lambda h, bk=bk, dst=dst: h.tensor_copy(out=dst, in_=banks[bk][:, :]), r=[pb[bk]], w=[tst])
                            nev += 1
                        dma("sync", scr_fm[ti, g * 4 + hb], st, r=[tst])
            tm_list = [(C_VB, 0, None), (C_ZA, 1, AF.Silu), (C_ZB, 2, AF.Silu)]
            for c0, ti, fn_ in tm_list:
                for g in range(2):
                    wt, twt = wb[nwl % 3]
                    nwl += 1
                    wt3 = v3(wt, 8, 512)
                    dma("gpsimd", wt3, wv[:, :, c0 + g * 512:c0 + (g + 1) * 512], w=[twt])
                    st, tst = stT[0]
                    st4 = st.rearrange("p (a b c) -> p a b c", a=4, b=NT, c=128)
                    for t in range(NT):
                        bk = nbk % 4
                        nbk += 1
                        for kc in range(8):
                            op("tensor", lambda h, bk=bk, wt3=wt3, kc=kc, t=t: h.matmul(
                                banks[bk][:, :], lhsT=hT3[:, kc, t * 128:(t + 1) * 128], rhs=wt3[:, kc, :],
                                start=(kc == 0), stop=(kc == 7)), r=[twt, t_hT[t]], w=[pb[bk]])
                        dst = st4[:, :, t, :]
                        src = v3(banks[bk][:, :], 4, 128)
                        if fn_ is not None:
                            op("scalar", lambda h, dst=dst, src=src, fn_=fn_: h.activation(out=dst, in_=src, func=fn_), r=[pb[bk]], w=[tst])
                        elif t % 2 == 0:
                            op("scalar", lambda h, dst=dst, src=src: h.copy(out=dst, in_=src), r=[pb[bk]], w=[tst])
                        else:
                            op("vector", lambda h, dst=dst, src=src: h.tensor_copy(out=dst, in_=src), r=[pb[bk]], w=[tst])
                    for hb in range(4):
                        dma("sync", scr_tm[ti, g * 4 + hb], st[:, hb * S_:(hb + 1) * S_], r=[tst])
            wt, twt = wb[nwl % 3]
            nwl += 1
            wt3 = v3(wt, 8, 512)
            dma("gpsimd", wt3[:, :, 0:32], wv[:, :, C_AB:C_AB + 32], w=[twt])
            abt3 = v3(abt, NT, 32)
            for t in range(NT):
                bk = nbk % 4
                nbk += 1
                for kc in range(8):
                    op("tensor", lambda h, bk=bk, wt3=wt3, kc=kc, t=t: h.matmul(
                        banks[bk][:, 0:32], lhsT=hT3[:, kc, t * 128:(t + 1) * 128], rhs=wt3[:, kc, 0:32],
                        start=(kc == 0), stop=(kc == 7)), r=[twt, t_hT[t]], w=[pb[bk]])
                op("vector", lambda h, bk=bk, t=t: h.tensor_copy(out=abt3[:, t, :], in_=banks[bk][:, 0:32]), r=[pb[bk]], w=[t_abt])
            SC.barrier()
            A.reset(m0)
            if stop_after == "A":
                break

            A.reset(m_H)
            phase_B(l, pl)
            SC.barrier()
            if stop_after == "B":
                break
            A.reset(m_H)
            phase_C(l, pl, lam_init)
            SC.barrier()
            if stop_after == "C":
                break
            A.reset(m_W)
            last = (l == L_ - 1)
            dma("sync", nwb, nw_d[l + 1], w=[t_nwb])
            wts = []
            for wd in (w_pa_d, w_pb_d, w_out_d):
                wt = A.alloc(8 * D_ * 2)
                tw = Tok()
                wt3 = v3(wt, 8, D_)
                for hh in range(2):
                    dma("gpsimd", wt3[:, :, hh * 512:(hh + 1) * 512], wd[l].rearrange("(kc p) n -> p kc n", p=128)[:, :, hh * 512:(hh + 1) * 512], w=[tw])
                wts.append((wt3, tw))
            (wpa3, twpa), (wpb3, twpb), (wo3, two) = wts
            yb_ = [(A.alloc(8 * 512 * 2), A.alloc(8 * 512 * 2), Tok()) for _ in range(1)]
            gb_ = [(A.alloc(2 * 512 * 2), Tok()) for _ in range(2)]
            mg = A.alloc(8 * 512 * 2)
            mg3 = v3(mg, 8, 512)
            t_mg = Tok()
            tt1 = A.alloc(512 * 4, F32)
            tt2 = A.alloc(512 * 4, F32)
            t_tt = Tok()
            xts = [(A.alloc(D_ * 4, F32), Tok()) for _ in range(2)]
            ntm = [norm_tmp() for _ in range(1)]
            ng = 0
            for tt in range(8):
                ya, yb, ty = yb_[0]
                ya3, yb3 = v3(ya, 8, 512), v3(yb, 8, 512)
                dma("sync", ya3, scr_y[0][:, :, tt * 512:(tt + 1) * 512].rearrange("h p t -> p h t"), w=[ty])
                dma("sync", yb3, scr_y[1][:, :, tt * 512:(tt + 1) * 512].rearrange("h p t -> p h t"), w=[ty])
                for m in range(8):
                    gg, tg = gb_[ng % 2]
                    ng += 1
                    dma("sync", gg[:, 0:512], scr_fm[5, m][:, tt * 512:(tt + 1) * 512], w=[tg])
                    dma("sync", gg[:, 512:1024], scr_fm[6, m][:, tt * 512:(tt + 1) * 512], w=[tg])
                    for kc in range(8):
                        op("tensor", lambda h, kc=kc, m=m, ya3=ya3: h.matmul(banks[0][:, :], lhsT=wpa3[:, kc, m * 128:(m + 1) * 128], rhs=ya3[:, kc, :],
                                                                 start=(kc == 0), stop=(kc == 7)), r=[twpa, ty], w=[pb[0]])
                    for kc in range(8):
                        op("tensor", lambda h, kc=kc, m=m, yb3=yb3: h.matmul(banks[1][:, :], lhsT=wpb3[:, kc, m * 128:(m + 1) * 128], rhs=yb3[:, kc, :],
                                                                 start=(kc == 0), stop=(kc == 7)), r=[twpb, ty], w=[pb[1]])
                    op(V, lambda h, gg=gg: h.tensor_tensor(out=tt1, in0=banks[0][:, :], in1=gg[:, 0:512], op=ALU.mult), r=[pb[0], tg], w=[t_tt])
                    op(V, lambda h, gg=gg: h.tensor_tensor(out=tt2, in0=banks[1][:, :], in1=gg[:, 512:1024], op=ALU.mult), r=[pb[1], tg], w=[t_tt])
                    op("gpsimd", lambda h, m=m: h.tensor_tensor(out=mg3[:, m, :], in0=tt1, in1=tt2, op=ALU.add), r=[t_tt], w=[t_mg])
                for sub in range(4):
                    t = tt * 4 + sub
                    xt, tx = xts[t % 2]
                    dma("sync", xt, scr_x[t * 128:(t + 1) * 128, :], w=[tx])
                    for nh in range(2):
                        bk = 2 + nh
                        for kc in range(8):
                            op("tensor", lambda h, kc=kc, nh=nh, sub=sub, bk=bk: h.matmul(
                                banks[bk][:, :], lhsT=mg3[:, kc, sub * 128:(sub + 1) * 128], rhs=wo3[:, kc, nh * 512:(nh + 1) * 512],
                                start=(kc == 0), stop=(kc == 7)), r=[two, t_mg], w=[pb[bk]])
                        op(V, lambda h, nh=nh, bk=bk, xt=xt: h.tensor_tensor(out=xt[:, nh * 512:(nh + 1) * 512], in0=banks[bk][:, :], in1=xt[:, nh * 512:(nh + 1) * 512], op=ALU.add),
                           r=[pb[bk]], w=[tx])
                    if not last:
                        dma("sync", scr_x[t * 128:(t + 1) * 128, :], xt, r=[tx])
                    norm_tile(xt, tx, t, ntm[0], last, 4 + (t % 2))
            SC.barrier()
            A.reset(m_W)

        SC.barrier()

        with nc.Block() as block:
            @block.sync
            def _(h):
                SC.replay("sync", h)

            @block.scalar
            def _(h):
                SC.replay("scalar", h)

            @block.vector
            def _(h):
                SC.replay("vector", h)

            @block.gpsimd
            def _(h):
                SC.replay("gpsimd", h)

            @block.tensor
            def _(h):
                SC.replay("tensor", h)
    return nc


def _host_consts():
    c = np.zeros((128, NCST), np.float32)
    p = np.arange(128)[:, None]
    f = np.arange(128)[None, :]
    c[:, K_ID:K_ID + 128] = (p == f)
    c[:, K_PMF:K_PMF + 128] = np.where(p > f, 0.0, BIGM)
    c[:, K_NMF:K_NMF + 128] = np.where(f >= p, 0.0, -BIGM)
    c[:, K_PMB:K_PMB + 128] = np.where(p < f, 0.0, BIGM)
    c[:, K_NMB:K_NMB + 128] = np.where(f <= p, 0.0, -BIGM)
    c[:, K_TRF:K_TRF + 128] = (p <= f)
    c[:, K_TRB:K_TRB + 128] = (p >= f)
    perm = np.zeros((128, 128), np.float32)
    for base in (0, 64):
        for d in range(8):
            perm[base + d + 8, base + d] = 1.0
            perm[base + d, base + d + 8] = 1.0
    c[:, K_PERM:K_PERM + 128] = perm
    c[:, K_ONE:K_ONE + 128] = 1.0
    inv_freq = (500000.0 ** (-(np.arange(0, 16, 2, dtype=np.float32) / np.float32(16)))).astype(np.float32)
    for base in (0, 64):
        for d in range(8):
            c[base + d, K_INVF] = inv_freq[d]
            c[base + d + 8, K_INVF] = inv_freq[d]
            c[base + d, K_SGN] = -1.0
            c[base + d + 8, K_SGN] = 1.0
    return c


def _host_params(inputs):
    prm = np.zeros((128, L_, NPRM), np.float32)
    for l in range(L_):
        cw = inputs["conv_w"][l]
        cwr = cw.reshape(5, 3, 8, 128).transpose(3, 1, 2, 0)
        prm[:, l, P_CW:P_CW + 120] = cwr.reshape(128, 120)
        prm[:, l, P_ALOG:P_ALOG + 16] = inputs["a_log"][l].reshape(1, 16)
        prm[:, l, P_DTB:P_DTB + 16] = inputs["dt_bias"][l].reshape(1, 16)
        prm[:, l, P_GNW:P_GNW + 128] = inputs["gdn_norm_w"][l].reshape(1, 128)
        prm[:, l, P_SLW:P_SLW + 128] = inputs["diff_subln_w"][l].reshape(1, 128)
        prm[:, l, P_LAM:P_LAM + 256] = inputs["diff_lambda"][l].reshape(1, 256)
    nw = np.zeros((L_ + 1, 128, D_), np.float32)
    for l in range(L_):
        nw[l] = inputs["norm_w"][l][None, :]
    nw[L_] = inputs["final_norm_w"][None, :]
    return prm.reshape(128, L_ * NPRM), nw


def _in_maps(inputs):
    cst = _host_consts()
    prm, nw = _host_params(inputs)
    f32 = lambda a: np.ascontiguousarray(np.asarray(a, dtype=np.float32))
    w_in, w_pa, w_pb, w_out = (f32(inputs[k]) for k in ("w_in", "w_pa", "w_pb", "w_out"))
    maps = []
    for b in range(8):
        maps.append({
            "x": f32(inputs["x"][b]),
            "pos": np.ascontiguousarray(np.broadcast_to(np.asarray(inputs["positions"][b], dtype=np.int32)[None, :], (128, S_))),
            "w_in": w_in, "w_pa": w_pa, "w_pb": w_pb, "w_out": w_out,
            "cst": cst, "prm": prm, "nw": nw,
        })
    return maps


def kernel(**inputs):
    nc = build()
    maps = _in_maps(inputs)
    res = run_bass_kernel_spmd(nc, maps, core_ids=list(range(8)))
    return np.stack([np.asarray(r["out"], dtype=np.float32) for r in res.results], axis=0)
```

```python
import math
import numpy as np
import concourse.bass as bass
import concourse.mybir as mybir
from concourse.bass_utils import run_bass_kernel_spmd

F32, BF16, I32 = mybir.dt.float32, mybir.dt.bfloat16, mybir.dt.int32
AF = mybir.ActivationFunctionType
ALU = mybir.AluOpType
AX = mybir.AxisListType

S_ = 4096
D_ = 1024
L_ = 4
NIN = 10272
NT = 32
EPS = 1e-6
C_QA, C_KA, C_VA, C_AB, C_ZA, C_QB, C_KB, C_VB, C_ZB, C_GA, C_GB = (
    0, 1024, 2048, 3072, 3104, 4128, 5152, 6176, 7200, 8224, 9248)
NPRM = 664
P_CW, P_ALOG, P_DTB, P_GNW, P_SLW, P_LAM = 0, 120, 136, 152, 280, 408
NCST = 9 * 128 + 2
K_ID, K_PMF, K_NMF, K_PMB, K_NMB, K_TRF, K_TRB, K_PERM, K_ONE = [i * 128 for i in range(9)]
K_INVF, K_SGN = 9 * 128, 9 * 128 + 1
TWO_PI = 2.0 * math.pi
TWO_PI_HI = 6.28125
TWO_PI_LO = TWO_PI - TWO_PI_HI
BIGM = 30000.0


class Tok:
    __slots__ = ("w", "r")

    def __init__(self):
        self.w = None
        self.r = {}


class Sched:
    ENG = ("tensor", "scalar", "vector", "gpsimd", "sync")

    def __init__(self, nc, stack):
        self.nc = nc
        self.sems = []
        self.E = {}
        for n in self.ENG:
            s = stack.enter_context(nc.semaphore("s_" + n))
            self.sems.append(s)
            self.E[n] = dict(si=len(self.sems) - 1, cnt=0, waited={}, prog=[], slots=[], nxt=0)
        for n in ("sync", "gpsimd", "scalar"):
            for k in range(8):
                s = stack.enter_context(nc.semaphore("d_%s%d" % (n, k)))
                self.sems.append(s)
                self.E[n]["slots"].append([len(self.sems) - 1, 0])
        self.nops = 0

    def _deps(self, e, r, w):
        need = {}
        for t in r:
            if t.w is not None:
                s, v = t.w
                if need.get(s, 0) < v:
                    need[s] = v
        for t in w:
            if t.w is not None:
                s, v = t.w
                if need.get(s, 0) < v:
                    need[s] = v
            for s, v in t.r.items():
                if need.get(s, 0) < v:
                    need[s] = v
        waits = []
        for s, v in need.items():
            if s == e["si"] and e is self.E["tensor"]:
                continue
            if e["waited"].get(s, 0) >= v:
                continue
            e["waited"][s] = v
            waits.append((s, v))
        return waits

    def _mark(self, me, r, w):
        s, v = me
        for t in r:
            if t.r.get(s, 0) < v:
                t.r[s] = v
        for t in w:
            t.w = me
            t.r = {}

    def op(self, en, fn, r=(), w=()):
        e = self.E[en]
        waits = self._deps(e, r, w)
        e["cnt"] += 1
        e["prog"].append((waits, fn, (e["si"], 1)))
        self._mark((e["si"], e["cnt"]), r, w)
        self.nops += 1

    def dma(self, en, out, in_, r=(), w=()):
        e = self.E[en]
        waits = self._deps(e, r, w)
        slot = e["slots"][e["nxt"] % len(e["slots"])]
        e["nxt"] += 1
        if slot[1] > 0 and e["waited"].get(slot[0], 0) < slot[1]:
            waits.append((slot[0], slot[1]))
            e["waited"][slot[0]] = slot[1]
        slot[1] += 16
        e["prog"].append((waits, (lambda h, o=out, i=in_: h.dma_start(out=o, in_=i)), (slot[0], 16)))
        self._mark((slot[0], slot[1]), r, w)
        self.nops += 1

    def barrier(self):
        tgt = []
        for n in self.ENG:
            e = self.E[n]
            if e["cnt"] > 0:
                tgt.append((e["si"], e["cnt"]))
            for sl in e["slots"]:
                if sl[1] > 0:
                    tgt.append((sl[0], sl[1]))
        for n in self.ENG:
            e = self.E[n]
            waits = []
            for s, v in tgt:
                if s == e["si"]:
                    continue
                if e["waited"].get(s, 0) >= v:
                    continue
                e["waited"][s] = v
                waits.append((s, v))
            if waits:
                e["prog"].append((waits, None, None))

    def replay(self, en, h):
        for waits, fn, inc in self.E[en]["prog"]:
            for s, v in waits:
                h.wait_ge(self.sems[s], v)
            if fn is not None:
                ins = fn(h)
                ins.then_inc(self.sems[inc[0]], inc[1])


class Arena:
    def __init__(self, big, nbytes):
        self.big = big
        self.cap = nbytes
        self.top = 0

    def mark(self):
        return self.top

    def reset(self, m):
        self.top = m

    def alloc(self, nbytes, dt=BF16):
        off = (self.top + 63) // 64 * 64
        assert off + nbytes <= self.cap, ("SBUF arena overflow", off, nbytes, self.cap)
        self.top = off + nbytes
        ap = self.big[:, off // 2:(off + nbytes) // 2]
        if dt is not BF16:
            ap = ap.bitcast(dt)
        return ap


def v3(ap, a, b):
    return ap.rearrange("p (a b) -> p a b", a=a, b=b)


def build(n_layers=L_, dbg=False, stop_after=None):
    from contextlib import ExitStack
    nc = bass.Bass("TRN2", target_bir_lowering=False)

    def din(name, shape, dt=F32):
        return nc.dram_tensor(name, shape, dt, kind="ExternalInput").ap()

    x_d = din("x", [S_, D_])
    pos_d = din("pos", [128, S_], I32)
    w_in_d = din("w_in", [L_, D_, NIN])
    w_pa_d = din("w_pa", [L_, D_, D_])
    w_pb_d = din("w_pb", [L_, D_, D_])
    w_out_d = din("w_out", [L_, D_, D_])
    cst_d = din("cst", [128, NCST])
    prm_d = din("prm", [128, L_ * NPRM])
    nw_d = din("nw", [L_ + 1, 128, D_])
    out_d = nc.dram_tensor("out", [S_, D_], F32, kind="ExternalOutput").ap()
    kS = "ExternalOutput" if dbg else "Internal"
    scr_fm = nc.dram_tensor("scr_fm", [7, 8, 128, S_], BF16, kind=kS).ap()
    scr_tm = nc.dram_tensor("scr_tm", [3, 8, 128, S_], BF16, kind=kS).ap()
    scr_y = nc.dram_tensor("scr_y", [2, 8, 128, S_], BF16, kind=kS).ap()
    scr_x = nc.dram_tensor("scr_x", [S_, D_], F32, kind=kS).ap()

    with ExitStack() as stack:
        ARENA_BYTES = 212000
        big = stack.enter_context(nc.sbuf_tensor("big", [128, ARENA_BYTES // 2], BF16))
        psp = [stack.enter_context(nc.psum_tensor("pp%d" % i, [128, 1024], F32)) for i in range(4)]
        banks = [psp[i // 2][:, (i % 2) * 512:(i % 2 + 1) * 512] for i in range(8)]
        SC = Sched(nc, stack)
        A = Arena(big, ARENA_BYTES)
        op, dma = SC.op, SC.dma

        cst = A.alloc(NCST * 4, F32)
        prm = A.alloc(L_ * NPRM * 4, F32)
        cbf = A.alloc(4 * 128 * 2)
        ident_bf, perm_bf, ones_bf = cbf[:, 0:128], cbf[:, 128:256], cbf[:, 256:384]
        ccol = A.alloc(16 * 4, F32)
        Ct = A.alloc(S_ * 2)
        St = A.alloc(S_ * 2)
        abt = A.alloc(NT * 32 * 4, F32)
        nwb = A.alloc(D_ * 4, F32)
        lamc = A.alloc(8 * 4, F32)
        t_cst, t_prm, t_cbf, t_ccol, t_rope, t_abt, t_gt, t_nwb, t_lam = [Tok() for _ in range(9)]
        ident_f = cst[:, K_ID:K_ID + 128]
        ones_f = cst[:, K_ONE:K_ONE + 128]
        m_H = A.mark()
        hT = A.alloc(8 * S_ * 2)
        hT3 = v3(hT, 8, S_)
        t_hT = [Tok() for _ in range(NT)]
        m_W = A.mark()

        pb = [Tok() for _ in range(8)]

        dma("sync", cst, cst_d, w=[t_cst])
        dma("sync", prm, prm_d, w=[t_prm])
        op("vector", lambda h: h.tensor_copy(out=cbf[:, 0:128], in_=cst[:, K_ID:K_ID + 128]), r=[t_cst], w=[t_cbf])
        op("vector", lambda h: h.tensor_copy(out=cbf[:, 128:256], in_=cst[:, K_PERM:K_PERM + 128]), r=[t_cst], w=[t_cbf])
        op("vector", lambda h: h.tensor_copy(out=cbf[:, 256:384], in_=cst[:, K_ONE:K_ONE + 128]), r=[t_cst], w=[t_cbf])
        CC_EPS, CC_ONE, CC_DKEPS, CC_PI2 = 0, 1, 2, 3
        for ci, val in ((CC_EPS, EPS), (CC_ONE, 1.0), (CC_DKEPS, 128.0 * EPS), (CC_PI2, math.pi / 2)):
            op("vector", lambda h, ci=ci, val=val: h.memset(ccol[:, ci:ci + 1], val), w=[t_ccol])

        def cc(i):
            return ccol[:, i:i + 1]

        m0 = A.mark()
        posi = A.alloc(S_ * 4, I32)
        ang = A.alloc(S_ * 4, F32)
        uu = A.alloc(S_ * 4, F32)
        rr = A.alloc(S_ * 4, F32)
        ki = A.alloc(S_ * 4, I32)
        tp = Tok()
        dma("sync", posi, pos_d, w=[tp])
        V = "vector"
        op(V, lambda h: h.tensor_copy(out=ang, in_=posi), r=[tp], w=[tp])
        op(V, lambda h: h.tensor_scalar(out=ang, in0=ang, scalar1=cst[:, K_INVF:K_INVF + 1], scalar2=None, op0=ALU.mult), r=[tp, t_cst], w=[tp])
        op(V, lambda h: h.tensor_scalar(out=uu, in0=ang, scalar1=1.0 / TWO_PI, scalar2=None, op0=ALU.mult), r=[tp], w=[tp])
        op(V, lambda h: h.tensor_copy(out=ki, in_=uu), r=[tp], w=[tp])
        op(V, lambda h: h.tensor_copy(out=uu, in_=ki), r=[tp], w=[tp])
        op(V, lambda h: h.scalar_tensor_tensor(out=rr, in0=uu, scalar=-TWO_PI_HI, in1=ang, op0=ALU.mult, op1=ALU.add), r=[tp], w=[tp])
        op(V, lambda h: h.scalar_tensor_tensor(out=rr, in0=uu, scalar=-TWO_PI_LO, in1=rr, op0=ALU.mult, op1=ALU.add), r=[tp], w=[tp])

        def fixup(r_):
            op(V, lambda h: h.tensor_scalar(out=uu, in0=r_, scalar1=math.pi, scalar2=None, op0=ALU.is_gt), r=[tp], w=[tp])
            op(V, lambda h: h.scalar_tensor_tensor(out=r_, in0=uu, scalar=-TWO_PI, in1=r_, op0=ALU.mult, op1=ALU.add), r=[tp], w=[tp])
            op(V, lambda h: h.tensor_scalar(out=uu, in0=r_, scalar1=-math.pi, scalar2=None, op0=ALU.is_lt), r=[tp], w=[tp])
            op(V, lambda h: h.scalar_tensor_tensor(out=r_, in0=uu, scalar=TWO_PI, in1=r_, op0=ALU.mult, op1=ALU.add), r=[tp], w=[tp])

        fixup(rr)
        op("scalar", lambda h: h.activation(out=St, in_=rr, func=AF.Sin, scale=cst[:, K_SGN:K_SGN + 1]), r=[tp, t_cst], w=[t_rope])
        op(V, lambda h: h.tensor_scalar(out=ang, in0=rr, scalar1=math.pi / 2, scalar2=None, op0=ALU.add), r=[tp], w=[tp])
        fixup(ang)
        op("scalar", lambda h: h.activation(out=Ct, in_=ang, func=AF.Sin), r=[tp], w=[t_rope])
        SC.barrier()
        A.reset(m0)

        def norm_tile(xt, t_x, t, tmp, final, bank):
            junk, ss, hn, t_tmp = tmp["junk"], tmp["ss"], tmp["hn"], tmp["tok"]
            op("scalar", lambda h: h.activation(out=junk, in_=xt, func=AF.Square, accum_out=ss[:, 0:1]), r=[t_x], w=[t_tmp])
            op("scalar", lambda h: h.activation(out=ss[:, 1:2], in_=ss[:, 0:1], func=AF.Sqrt, scale=1.0 / D_, bias=cc(CC_EPS)), r=[t_ccol], w=[t_tmp])
            op(V, lambda h: h.reciprocal(out=ss[:, 2:3], in_=ss[:, 1:2]), w=[t_tmp])
            if final:
                op(V, lambda h: h.scalar_tensor_tensor(out=xt, in0=xt, scalar=ss[:, 2:3], in1=nwb, op0=ALU.mult, op1=ALU.mult), r=[t_nwb, t_tmp], w=[t_x])
                dma("sync", out_d[t * 128:(t + 1) * 128, :], xt, r=[t_x])
                return
            op(V, lambda h: h.scalar_tensor_tensor(out=hn, in0=xt, scalar=ss[:, 2:3], in1=nwb, op0=ALU.mult, op1=ALU.mult), r=[t_x, t_nwb], w=[t_tmp])
            pbk = banks[bank][:, :].bitcast(BF16)
            for kc in range(8):
                op("tensor", lambda h, kc=kc: h.transpose(out=pbk[:, kc * 128:(kc + 1) * 128], in_=hn[:, kc * 128:(kc + 1) * 128], identity=ident_bf),
                   r=[t_tmp, t_cbf], w=[pb[bank]])
            op("scalar", lambda h: h.copy(out=hT3[:, :, t * 128:(t + 1) * 128], in_=v3(pbk, 8, 128)), r=[pb[bank]], w=[t_hT[t]])

        def norm_tmp():
            hn_ = A.alloc(D_ * 2)
            return dict(junk=hn_, ss=A.alloc(16, F32), hn=hn_, tok=Tok())

        m0 = A.mark()
        dma("sync", nwb, nw_d[0], w=[t_nwb])
        xts = [(A.alloc(D_ * 4, F32), Tok()) for _ in range(3)]
        ntm = [norm_tmp() for _ in range(2)]
        for t in range(NT):
            xt, tx = xts[t % 3]
            dma("sync", xt, x_d[t * 128:(t + 1) * 128, :], w=[tx])
            dma("sync", scr_x[t * 128:(t + 1) * 128, :], xt, r=[tx])
            norm_tile(xt, tx, t, ntm[t % 2], False, t % 2)
        SC.barrier()
        A.reset(m0)

        def phase_B(l, pl):
            T, AC, G = "tensor", "scalar", "gpsimd"
            gta = A.alloc(16 * 1024, F32)

            def gflat(i):
                return gta[:, i * 256:(i + 1) * 256]

            def gtile(i):
                return v3(gflat(i), NT, 8)

            beta = [gtile(0), gtile(1)]
            g_ = [gtile(2), gtile(3)]
            Gc = [gtile(4), gtile(5)]
            nb = [gtile(6), gtile(7)]
            eG = [gtile(8), gtile(9)]
            nbeG = [gtile(10), gtile(11)]
            eGt = [gtile(12), gtile(13)]
            nA = A.alloc(16 * 4, F32)
            t_g = Tok()
            abt3 = v3(abt, NT, 32)
            for d in range(2):
                sl = slice(d * 256, (d + 1) * 256)
                op(AC, lambda h, d=d: h.activation(out=beta[d], in_=abt3[:, :, d * 8:(d + 1) * 8], func=AF.Sigmoid), r=[t_abt], w=[t_g])
                dtb = pl[:, P_DTB + 8 * d:P_DTB + 8 * d + 8].unsqueeze(1).broadcast_to([128, NT, 8])
                op(V, lambda h, d=d, dtb=dtb: h.tensor_tensor(out=g_[d], in0=abt3[:, :, 16 + 8 * d:24 + 8 * d], in1=dtb, op=ALU.add), r=[t_abt, t_prm], w=[t_g])
                op(AC, lambda h, d=d: h.activation(out=g_[d], in_=g_[d], func=AF.Exp), w=[t_g])
                op(AC, lambda h, d=d: h.activation(out=g_[d], in_=g_[d], func=AF.Ln, bias=cc(CC_ONE)), r=[t_ccol], w=[t_g])
                op(AC, lambda h, d=d: h.activation(out=nA[:, 8 * d:8 * d + 8], in_=pl[:, P_ALOG + 8 * d:P_ALOG + 8 * d + 8], func=AF.Exp), r=[t_prm], w=[t_g])
                nAb = nA[:, 8 * d:8 * d + 8].unsqueeze(1).broadcast_to([128, NT, 8])
                op(V, lambda h, d=d, nAb=nAb: h.scalar_tensor_tensor(out=g_[d], in0=g_[d], scalar=-1.0, in1=nAb, op0=ALU.mult, op1=ALU.mult), w=[t_g])
                tri = cst[:, K_TRF:K_TRF + 128] if d == 0 else cst[:, K_TRB:K_TRB + 128]
                op(T, lambda h, d=d, sl=sl, tri=tri: h.matmul(banks[0][:, sl], lhsT=tri, rhs=gflat(2 + d), start=True, stop=True), r=[t_g, t_cst], w=[pb[0]])
                op(T, lambda h, d=d, sl=sl: h.matmul(banks[1][:, sl], lhsT=ones_f, rhs=gflat(2 + d), start=True, stop=True), r=[t_g, t_cst], w=[pb[1]])
                op(V, lambda h, d=d, sl=sl: h.tensor_copy(out=gflat(4 + d), in_=banks[0][:, sl]), w=[t_g, pb[0]])
                op(AC, lambda h, d=d, sl=sl: h.activation(out=gflat(8 + d), in_=banks[0][:, sl], func=AF.Exp), w=[t_g, pb[0]])
                op(AC, lambda h, d=d, sl=sl: h.activation(out=gflat(12 + d), in_=banks[1][:, sl], func=AF.Exp), w=[t_g, pb[1]])
                op(V, lambda h, d=d, sl=sl: h.tensor_tensor(out=gflat(14 + d), in0=banks[1][:, sl], in1=gflat(4 + d), op=ALU.subtract), w=[t_g, pb[1]])
                op(AC, lambda h, d=d: h.activation(out=gflat(14 + d), in_=gflat(14 + d), func=AF.Exp), w=[t_g])
                op(V, lambda h, d=d: h.tensor_scalar(out=gflat(6 + d), in0=gflat(d), scalar1=-1.0, scalar2=None, op0=ALU.mult), w=[t_g])
                op(V, lambda h, d=d: h.tensor_tensor(out=gflat(10 + d), in0=gflat(6 + d), in1=gflat(8 + d), op=ALU.mult), w=[t_g])
            eD = [gtile(14), gtile(15)]

            xpad = A.alloc(4100 * 2)
            acc = A.alloc(S_ * 4, F32)
            sqr = A.alloc(S_ * 2)
            rsb = [A.alloc(512 * 4, F32) for _ in range(2)]
            qT = A.alloc(S_ * 2)
            kT = A.alloc(S_ * 2)
            vb = [A.alloc(S_ * 2) for _ in range(2)]
            kd = [A.alloc(S_ * 2) for _ in range(2)]
            za = A.alloc(S_ * 2)
            oacc = A.alloc(S_ * 4, F32)
            oacc3 = v3(oacc, NT, 128)
            yst = xpad[:, 2:2 + S_]
            ssum = A.alloc(96 * 4, F32)
            RN = 4
            NI = 4
            TTr = [[A.alloc(512, F32) for _ in range(RN)] for _ in range(2)]
            QKr = [[A.alloc(256) for _ in range(RN)] for _ in range(2)]
            tA = [A.alloc(512, F32) for _ in range(NI)]
            tB = [A.alloc(512, F32) for _ in range(NI)]
            dec = [A.alloc(512, F32) for _ in range(NI)]
            decT = [A.alloc(256) for _ in range(NI)]
            pair = [[A.alloc(1024, F32) for _ in range(2)] for _ in range(NI)]
            X = [[A.alloc(512, F32) for _ in range(2)] for _ in range(NI)]
            rhs = [A.alloc(512, F32) for _ in range(2)]
            vnew = [A.alloc(256) for _ in range(2)]
            tmp = [A.alloc(512, F32) for _ in range(2)]
            tmp2 = [A.alloc(512, F32) for _ in range(2)]
            S32 = [A.alloc(512, F32) for _ in range(2)]
            Sbf = [A.alloc(256) for _ in range(2)]
            tk = lambda: Tok()
            t_xp, t_acc, t_sqr, t_qT, t_kT, t_za, t_ss = [Tok() for _ in range(7)]
            t_yst = t_xp
            t_rs = [tk(), tk()]
            t_vb = [tk(), tk()]
            t_kd = [tk(), tk()]
            t_oacc = [tk() for _ in range(NT)]
            t_TT = [[tk() for _ in range(RN)] for _ in range(2)]
            t_QK = [[tk() for _ in range(RN)] for _ in range(2)]
            t_rhs, t_vnew, t_tmp, t_tmp2, t_S32, t_S = [[tk(), tk()] for _ in range(6)]
            t_tA, t_tB, t_dec, t_decT = [[tk() for _ in range(NI)] for _ in range(4)]
            t_pair = [[tk(), tk()] for _ in range(NI)]
            t_X = [[tk(), tk()] for _ in range(NI)]
            KK = [banks[it][:, 0:128] for it in range(NI)]
            QKT = [banks[it][:, 128:256] for it in range(NI)]
            Gb = [banks[it][:, 256:384] for it in range(NI)]
            MT = [banks[it][:, 384:512] for it in range(NI)]
            PPQ = [banks[it][:, 0:256] for it in range(NI)]
            XPQ = [banks[it][:, 256:384] for it in range(NI)]
            KS = [banks[4 + d][:, 0:128] for d in range(2)]
            VN = [banks[4 + d][:, 128:256] for d in range(2)]
            O1 = [banks[4 + d][:, 256:384] for d in range(2)]
            O2 = [banks[4 + d][:, 384:512] for d in range(2)]
            DS = [banks[6 + d][:, 0:128] for d in range(2)]
            PMm = [cst[:, K_PMF:K_PMF + 128], cst[:, K_PMB:K_PMB + 128]]
            NMm = [cst[:, K_NMF:K_NMF + 128], cst[:, K_NMB:K_NMB + 128]]

            op(V, lambda h: h.memset(xpad[:, 0:2], 0.0), w=[t_xp])
            op(V, lambda h: h.memset(xpad[:, 4098:4100], 0.0), w=[t_xp])

            for h_ in range(8):
                def tr_scaled(srcT, t_src, dst, t_dst, sc):
                    for cg in range(4):
                        bk = 6 + (cg % 2)
                        pbk = banks[bk][:, :].bitcast(BF16)
                        for j in range(8):
                            c = cg * 8 + j
                            op(T, lambda h, j=j, c=c, pbk=pbk: h.transpose(out=pbk[:, j * 128:(j + 1) * 128], in_=srcT[:, c * 128:(c + 1) * 128], identity=ident_bf),
                               r=[t_src, t_cbf], w=[pb[bk]])
                        for d in range(2):
                            scb = sc[d][:, cg * 8:(cg + 1) * 8, h_:h_ + 1].broadcast_to([128, 8, 128])
                            op(V, lambda h, d=d, cg=cg, pbk=pbk, scb=scb: h.tensor_tensor(out=v3(dst[d], NT, 128)[:, cg * 8:(cg + 1) * 8, :], in0=v3(pbk, 8, 128), in1=scb, op=ALU.mult),
                               r=[pb[bk], t_g], w=[t_dst[d]])

                for ti in range(3):
                    dma("sync", xpad[:, 2:4098], scr_fm[ti, h_], w=[t_xp])
                    base = P_CW + (ti * 8 + h_) * 5
                    op(V, lambda h, base=base: h.tensor_scalar(out=acc, in0=xpad[:, 0:4096], scalar1=pl[:, base:base + 1], scalar2=None, op0=ALU.mult), r=[t_xp, t_prm], w=[t_acc])
                    for k in range(1, 5):
                        op(V, lambda h, base=base, k=k: h.scalar_tensor_tensor(out=acc, in0=xpad[:, k:k + 4096], scalar=pl[:, base + k:base + k + 1], in1=acc, op0=ALU.mult, op1=ALU.add),
                           r=[t_xp, t_prm], w=[t_acc])
                    if ti == 2:
                        op(AC, lambda h: h.activation(out=sqr, in_=acc, func=AF.Silu), r=[t_acc], w=[t_sqr])
                        tr_scaled(sqr, t_sqr, vb, t_vb, beta)
                    else:
                        op(AC, lambda h: h.activation(out=acc, in_=acc, func=AF.Silu), w=[t_acc])
                        op(AC, lambda h: h.activation(out=sqr, in_=acc, func=AF.Square), r=[t_acc], w=[t_sqr])
                        dstT, t_dst = (qT, t_qT) if ti == 0 else (kT, t_kT)
                        for tt in range(8):
                            bk = 6 + (tt % 2)
                            tsl = slice(tt * 512, (tt + 1) * 512)
                            rs = rsb[tt % 2]
                            trs = t_rs[tt % 2]
                            op(T, lambda h, bk=bk, tsl=tsl: h.matmul(banks[bk][:, :], lhsT=ones_bf, rhs=sqr[:, tsl], start=True, stop=True), r=[t_sqr, t_cbf], w=[pb[bk]])
                            op(AC, lambda h, bk=bk, rs=rs, ti=ti: h.activation(out=rs, in_=banks[bk][:, :], func=AF.Ln, scale=(128.0 if ti == 0 else 1.0),
                                                                        bias=cc(CC_DKEPS if ti == 0 else CC_EPS)), r=[pb[bk], t_ccol], w=[trs])
                            op(AC, lambda h, rs=rs: h.activation(out=rs, in_=rs, func=AF.Exp, scale=-0.5), w=[trs])
                            op(V, lambda h, rs=rs, tsl=tsl, dstT=dstT: h.tensor_tensor(out=dstT[:, tsl], in0=acc[:, tsl], in1=rs, op=ALU.mult), r=[t_acc, trs], w=[t_dst])
                        if ti == 1:
                            tr_scaled(kT, t_kT, kd, t_kd, eD)
                dma("sync", za, scr_tm[1, h_], w=[t_za])
                op(G, lambda h: h.memset(oacc, 0.0), w=t_oacc)
                for d in range(2):
                    op(G, lambda h, d=d: h.memset(S32[d], 0.0), w=[t_S32[d]])
                    op(G, lambda h, d=d: h.memset(Sbf[d], 0.0), w=[t_S[d]])

                def prep_batch(steps, hh=h_):
                    items = []
                    for si, i in enumerate(steps):
                        for d in range(2):
                            items.append((si * 2 + d, d, (i, NT - 1 - i)[d], i % RN))
                    for it, d, c, s in items:
                        kc_ = kT[:, c * 128:(c + 1) * 128]
                        qc_ = qT[:, c * 128:(c + 1) * 128]
                        op(T, lambda h, it=it, kc_=kc_: h.matmul(KK[it], lhsT=kc_, rhs=kc_, start=True, stop=True), r=[t_kT], w=[pb[it]])
                        op(T, lambda h, it=it, kc_=kc_, qc_=qc_: h.matmul(QKT[it], lhsT=kc_, rhs=qc_, start=True, stop=True), r=[t_kT, t_qT], w=[pb[it]])
                        op(G, lambda h, it=it, d=d, c=c: h.tensor_scalar(out=tA[it], in0=ident_f, scalar1=Gc[d][:, c, hh:hh + 1], scalar2=None, op0=ALU.mult), r=[t_g, t_cst], w=[t_tA[it]])
                        op(T, lambda h, it=it: h.matmul(Gb[it], lhsT=ones_f, rhs=tA[it], start=True, stop=True), r=[t_tA[it], t_cst], w=[pb[it]])
                    yield
                    for it, d, c, s in items:
                        op(V, lambda h, it=it, d=d, c=c: h.scalar_tensor_tensor(out=tB[it], in0=Gb[it], scalar=Gc[d][:, c, hh:hh + 1], in1=PMm[d], op0=ALU.subtract, op1=ALU.add),
                           r=[t_g, t_cst], w=[t_tB[it], pb[it]])
                    yield
                    for it, d, c, s in items:
                        op(AC, lambda h, it=it: h.activation(out=dec[it], in_=tB[it], func=AF.Exp, scale=-1.0), r=[t_tB[it]], w=[t_dec[it]])
                        op(V, lambda h, it=it, d=d, c=c: h.scalar_tensor_tensor(out=tA[it], in0=Gb[it], scalar=Gc[d][:, c, hh:hh + 1], in1=NMm[d], op0=ALU.subtract, op1=ALU.add),
                           r=[t_g, t_cst], w=[t_tA[it], pb[it]])
                    yield
                    for it, d, c, s in items:
                        op(AC, lambda h, it=it: h.activation(out=decT[it], in_=tA[it], func=AF.Exp), r=[t_tA[it]], w=[t_decT[it]])
                        op(V, lambda h, it=it, d=d, c=c: h.scalar_tensor_tensor(out=pair[it][0][:, 0:128], in0=KK[it], scalar=nb[d][:, c, hh:hh + 1], in1=dec[it], op0=ALU.mult, op1=ALU.mult),
                           r=[t_dec[it], t_g], w=[t_pair[it][0], pb[it]])
                    yield
                    for it, d, c, s in items:
                        op(V, lambda h, it=it, d=d, s=s: h.tensor_tensor(out=QKr[d][s], in0=QKT[it], in1=decT[it], op=ALU.mult), r=[t_decT[it]], w=[t_QK[d][s], pb[it]])
                        op(T, lambda h, it=it: h.transpose(out=MT[it], in_=pair[it][0][:, 0:128], identity=ident_f), r=[t_pair[it][0], t_cst], w=[pb[it]])
                    yield
                    for it, d, c, s in items:
                        op(AC, lambda h, it=it: h.copy(out=pair[it][0][:, 128:256], in_=MT[it]), w=[t_pair[it][0], pb[it]])
                        op(G, lambda h, it=it: h.tensor_tensor(out=X[it][0], in0=pair[it][0][:, 128:256], in1=ident_f, op=ALU.add), r=[t_pair[it][0], t_cst], w=[t_X[it][0]])
                    yield
                    for lvl in range(1, 7):
                        a = (lvl - 1) % 2
                        b = lvl % 2
                        n = 256 if lvl < 6 else 128
                        for it, d, c, s in items:
                            P_ = pair[it][a][:, 0:128]
                            PT_ = pair[it][a][:, 128:256]
                            op(T, lambda h, it=it, P_=P_, PT_=PT_: h.matmul(PPQ[it][:, 0:128], lhsT=PT_, rhs=P_, start=True, stop=True), r=[t_pair[it][a]], w=[pb[it]])
                            if lvl < 6:
                                op(T, lambda h, it=it, P_=P_, PT_=PT_: h.matmul(PPQ[it][:, 128:256], lhsT=P_, rhs=PT_, start=True, stop=True), r=[t_pair[it][a]], w=[pb[it]])
                        yield
                        for it, d, c, s in items:
                            op(AC, lambda h, it=it, b=b, n=n: h.copy(out=pair[it][b][:, 0:n], in_=PPQ[it][:, 0:n]), w=[t_pair[it][b], pb[it]])
                        yield
                        for it, d, c, s in items:
                            op(T, lambda h, it=it, a=a, b=b: h.matmul(XPQ[it], lhsT=pair[it][b][:, 0:128], rhs=X[it][a], start=True, stop=True), r=[t_pair[it][b], t_X[it][a]], w=[pb[it]])
                        yield
                        for it, d, c, s in items:
                            dst, tdst = (X[it][b], t_X[it][b]) if lvl < 6 else (TTr[d][s], t_TT[d][s])
                            op(V, lambda h, it=it, a=a, dst=dst: h.tensor_tensor(out=dst, in0=XPQ[it], in1=X[it][a], op=ALU.add), r=[t_X[it][a]], w=[tdst, pb[it]])
                        yield

                def chain_steps(steps, hh=h_):
                    for i in steps:
                        s = i % RN
                        cs = (i, NT - 1 - i)
                        for d in range(2):
                            c = cs[d]
                            op(T, lambda h, d=d, c=c: h.matmul(KS[d], lhsT=kT[:, c * 128:(c + 1) * 128], rhs=Sbf[d], start=True, stop=True), r=[t_kT, t_S[d]], w=[pb[4 + d]])
                        yield
                        for d in range(2):
                            c = cs[d]
                            op(V, lambda h, d=d, c=c: h.scalar_tensor_tensor(out=rhs[d], in0=KS[d], scalar=nbeG[d][:, c, hh:hh + 1], in1=v3(vb[d], NT, 128)[:, c, :], op0=ALU.mult, op1=ALU.add),
                               r=[t_g, t_vb[d]], w=[t_rhs[d], pb[4 + d]])
                        yield
                        for d in range(2):
                            op(T, lambda h, d=d, s=s: h.matmul(VN[d], lhsT=TTr[d][s], rhs=rhs[d], start=True, stop=True), r=[t_TT[d][s], t_rhs[d]], w=[pb[4 + d]])
                        yield
                        for d in range(2):
                            op(AC, lambda h, d=d: h.copy(out=vnew[d], in_=VN[d]), w=[t_vnew[d], pb[4 + d]])
                        yield
                        for d in range(2):
                            c = cs[d]
                            op(T, lambda h, d=d, c=c: h.matmul(DS[d], lhsT=v3(kd[d], NT, 128)[:, c, :], rhs=vnew[d], start=True, stop=True), r=[t_kd[d], t_vnew[d]], w=[pb[6 + d]])
                            op(T, lambda h, d=d, c=c: h.matmul(O1[d], lhsT=qT[:, c * 128:(c + 1) * 128], rhs=Sbf[d], start=True, stop=True), r=[t_qT, t_S[d]], w=[pb[4 + d]])
                            op(T, lambda h, d=d, s=s: h.matmul(O2[d], lhsT=QKr[d][s], rhs=vnew[d], start=True, stop=True), r=[t_QK[d][s], t_vnew[d]], w=[pb[4 + d]])
                        yield
                        for d in range(2):
                            c = cs[d]
                            op(V, lambda h, d=d, c=c: h.scalar_tensor_tensor(out=Sbf[d], in0=S32[d], scalar=eGt[d][:, c, hh:hh + 1], in1=DS[d], op0=ALU.mult, op1=ALU.add),
                               r=[t_g, t_S32[d]], w=[t_S[d], pb[6 + d]])
                            op(V, lambda h, d=d, c=c: h.scalar_tensor_tensor(out=S32[d], in0=S32[d], scalar=eGt[d][:, c, hh:hh + 1], in1=DS[d], op0=ALU.mult, op1=ALU.add),
                               r=[t_g], w=[t_S32[d], pb[6 + d]])
                            op(AC, lambda h, d=d, c=c: h.activation(out=tmp[d], in_=O1[d], func=AF.Copy, scale=eG[d][:, c, hh:hh + 1]), r=[t_g], w=[t_tmp[d], pb[4 + d]])
                        yield
                        for d in range(2):
                            c = cs[d]
                            op(V, lambda h, d=d: h.tensor_tensor(out=tmp2[d], in0=O2[d], in1=tmp[d], op=ALU.add), r=[t_tmp[d]], w=[t_tmp2[d], pb[4 + d]])
                            op(G, lambda h, d=d, c=c: h.tensor_tensor(out=oacc3[:, c, :], in0=oacc3[:, c, :], in1=tmp2[d], op=ALU.add), r=[t_tmp2[d]], w=[t_oacc[c]])
                        yield

                for _ in prep_batch((0, 1)):
                    pass
                for bb in range(NT // 2):
                    pg = prep_batch((2 * bb + 2, 2 * bb + 3)) if bb + 1 < NT // 2 else None
                    cg = chain_steps((2 * bb, 2 * bb + 1))
                    while pg is not None or cg is not None:
                        if pg is not None:
                            for _ in range(2):
                                if next(pg, "END") == "END":
                                    pg = None
                                    break
                        if cg is not None:
                            if next(cg, "END") == "END":
                                cg = None

                op(AC, lambda h: h.activation(out=acc, in_=oacc, func=AF.Square), r=t_oacc, w=[t_acc])
                op(V, lambda h: h.tensor_reduce(out=ssum[:, 0:32], in_=v3(acc, NT, 128), axis=AX.X, op=ALU.add), r=[t_acc], w=[t_ss])
                op(AC, lambda h: h.activation(out=ssum[:, 32:64], in_=ssum[:, 0:32], func=AF.Sqrt, scale=1.0 / 128, bias=cc(CC_EPS)), r=[t_ccol], w=[t_ss])
                op(V, lambda h: h.reciprocal(out=ssum[:, 64:96], in_=ssum[:, 32:64]), w=[t_ss])
                rb = ssum[:, 64:96].unsqueeze(2).broadcast_to([128, NT, 128])
                op(V, lambda h, rb=rb: h.tensor_tensor(out=oacc3, in0=oacc3, in1=rb, op=ALU.mult), r=[t_ss], w=t_oacc)
                gw = pl[:, P_GNW:P_GNW + 128].unsqueeze(1).broadcast_to([128, NT, 128])
                op(G, lambda h, gw=gw: h.tensor_tensor(out=oacc3, in0=oacc3, in1=gw, op=ALU.mult), r=[t_prm], w=t_oacc)
                op(V, lambda h: h.tensor_tensor(out=sqr, in0=oacc, in1=za, op=ALU.mult), r=t_oacc + [t_za], w=[t_sqr])
                for cg in range(4):
                    bk = 6 + (cg % 2)
                    pbk = banks[bk][:, :].bitcast(BF16)
                    for j in range(8):
                        c = cg * 8 + j
                        op(T, lambda h, j=j, c=c, pbk=pbk: h.transpose(out=pbk[:, j * 128:(j + 1) * 128], in_=sqr[:, c * 128:(c + 1) * 128], identity=ident_bf),
                           r=[t_sqr, t_cbf], w=[pb[bk]])
                    op(AC, lambda h, cg=cg, pbk=pbk: h.copy(out=yst[:, cg * 1024:(cg + 1) * 1024], in_=pbk), r=[pb[bk]], w=[t_yst])
                dma("sync", scr_y[0, h_], yst, r=[t_yst])

        def phase_C(l, pl, lam_init):
            T, AC, G = "tensor", "scalar", "gpsimd"
            om = 1.0 - lam_init
            ltmp = A.alloc(128 * 4, F32)
            lp = v3(pl[:, P_LAM:P_LAM + 256], 4, 64)
            op(V, lambda h: h.tensor_tensor(out=ltmp[:, 0:64], in0=lp[:, 0, :], in1=lp[:, 1, :], op=ALU.mult), r=[t_prm], w=[t_lam])
            op(V, lambda h: h.tensor_tensor(out=ltmp[:, 64:128], in0=lp[:, 2, :], in1=lp[:, 3, :], op=ALU.mult), r=[t_prm], w=[t_lam])
            op(V, lambda h: h.tensor_reduce(out=lamc[:, 0:2], in_=v3(ltmp, 2, 64), axis=AX.X, op=ALU.add), w=[t_lam])
            op(AC, lambda h: h.activation(out=lamc[:, 2:4], in_=lamc[:, 0:2], func=AF.Exp), w=[t_lam])
            op(V, lambda h: h.tensor_tensor(out=lamc[:, 4:5], in0=lamc[:, 2:3], in1=lamc[:, 3:4], op=ALU.subtract), w=[t_lam])
            op(V, lambda h: h.tensor_scalar(out=lamc[:, 5:6], in0=lamc[:, 4:5], scalar1=float(lam_init), scalar2=-1.0, op0=ALU.add, op1=ALU.mult), w=[t_lam])
            op(V, lambda h: h.memset(lamc[:, 6:7], EPS / (om * om)), w=[t_lam])
            nlam = lamc[:, 5:6]
            sl_scale = 1.0 / (128.0 * om * om)

            qraw = A.alloc(S_ * 2)
            kraw = A.alloc(S_ * 2)
            qTs = [A.alloc(S_ * 2) for _ in range(2)]
            kTs = [A.alloc(S_ * 2) for _ in range(2)]
            vb13s = [v3(A.alloc(NT * 130 * 2), NT, 130) for _ in range(2)]
            zbs = [A.alloc(S_ * 2) for _ in range(2)]
            zb3s = [v3(z_, NT, 128) for z_ in zbs]
            yst = A.alloc(S_ * 2)
            NE = 3
            Eb = [A.alloc(1024 * 2) for _ in range(NE)]
            rt1 = [A.alloc(512 * 4, F32) for _ in range(2)]
            rt2 = [A.alloc(512 * 4, F32) for _ in range(2)]
            osb = [A.alloc(8 * 130 * 4, F32) for _ in range(2)]
            a0 = [A.alloc(128 * 4, F32) for _ in range(2)]
            aa = [A.alloc(128 * 4, F32) for _ in range(2)]
            ytok = [A.alloc(128 * 2) for _ in range(2)]
            jk = [A.alloc(128 * 2) for _ in range(2)]
            rc = [A.alloc(8 * 4, F32) for _ in range(2)]
            t_qr, t_kr, t_yst = [Tok() for _ in range(3)]
            t_qTs, t_kTs, t_vs, t_zs = [[Tok(), Tok()] for _ in range(4)]
            t_rt1 = [Tok(), Tok()]
            t_rt2 = [Tok(), Tok()]
            t_E = [Tok() for _ in range(NE)]
            t_osb = [Tok(), Tok()]
            t_ep = [Tok(), Tok()]
            for p_ in range(2):
                op(V, lambda h, p_=p_: h.memset(vb13s[p_][:, :, 128:130], 1.0), w=[t_vs[p_]])
            slwb = pl[:, P_SLW:P_SLW + 128].unsqueeze(1).broadcast_to([128, NT, 128])
            OG = []
            for gi in range(8):
                OG.append(banks[4 + gi // 3][:, (gi % 3) * 130:(gi % 3) * 130 + 129])

            def prologue(hn):
                p_ = hn % 2
                qT, kT, vb13, zb, zb3 = qTs[p_], kTs[p_], vb13s[p_], zbs[p_], zb3s[p_]
                t_qT, t_kT, t_v, t_z = t_qTs[p_], t_kTs[p_], t_vs[p_], t_zs[p_]
                dma("sync", qraw, scr_fm[3, hn], w=[t_qr])
                dma("sync", kraw, scr_fm[4, hn], w=[t_kr])
                dma("sync", vb13[:, :, 0:128], scr_tm[0, hn].rearrange("p (a b) -> p a b", a=NT, b=128), w=[t_v])
                dma("sync", zb, scr_tm[2, hn], w=[t_z])
                op(G, lambda h, zb3=zb3: h.tensor_tensor(out=zb3, in0=zb3, in1=slwb, op=ALU.mult), r=[t_prm], w=[t_z])
                for raw, t_raw, dstT, t_dst in ((qraw, t_qr, qT, t_qT), (kraw, t_kr, kT, t_kT)):
                    for tt in range(8):
                        tsl = slice(tt * 512, (tt + 1) * 512)
                        rb_ = tt % 4
                        r1, r2, tr1, tr2 = rt1[tt % 2], rt2[tt % 2], t_rt1[tt % 2], t_rt2[tt % 2]
                        op(T, lambda h, raw=raw, tsl=tsl, rb_=rb_: h.matmul(banks[rb_][:, :], lhsT=perm_bf, rhs=raw[:, tsl], start=True, stop=True), r=[t_raw, t_cbf], w=[pb[rb_]])
                        op(V, lambda h, tsl=tsl, rb_=rb_, r1=r1: h.tensor_tensor(out=r1, in0=banks[rb_][:, :], in1=St[:, tsl], op=ALU.mult), r=[pb[rb_], t_rope], w=[tr1])
                        op(G, lambda h, raw=raw, tsl=tsl, r2=r2: h.tensor_tensor(out=r2, in0=raw[:, tsl], in1=Ct[:, tsl], op=ALU.mult), r=[t_raw, t_rope], w=[tr2])
                        op(V, lambda h, dstT=dstT, tsl=tsl, r1=r1, r2=r2: h.tensor_tensor(out=dstT[:, tsl], in0=r1, in1=r2, op=ALU.add), r=[tr1, tr2], w=[t_dst])
            prologue(0)
            for h_ in range(8):
                p_ = h_ % 2
                qT, kT, vb13, zb3 = qTs[p_], kTs[p_], vb13s[p_], zb3s[p_]
                t_qT, t_kT, t_v, t_z = t_qTs[p_], t_kTs[p_], t_vs[p_], t_zs[p_]
                ne = 0
                for qt in range(8):
                    qsl = slice(qt * 512, (qt + 1) * 512)
                    if qt == 7 and h_ + 1 < 8:
                        prologue(h_ + 1)
                    def emit_S(kt, qsl=qsl):
                        ksl = slice(kt * 128, (kt + 1) * 128)
                        for sm in range(2):
                            bk = (kt % 2) * 2 + sm
                            rows = slice(sm * 64, (sm + 1) * 64)
                            op(T, lambda h, bk=bk, rows=rows, ksl=ksl, qsl=qsl, kT=kT, qT=qT: h.matmul(banks[bk][:, :], lhsT=kT[rows, ksl], rhs=qT[rows, qsl], start=True, stop=True),
                               r=[t_kT, t_qT], w=[pb[bk]])

                    emit_S(0)
                    for kt in range(NT):
                        par = kt % 2
                        es = ne % NE
                        ne += 1
                        if kt + 1 < NT:
                            emit_S(kt + 1)
                        op(AC, lambda h, par=par, es=es: h.activation(out=Eb[es], in_=psp[par][:, :], func=AF.Exp, scale=0.125), r=[pb[2 * par], pb[2 * par + 1]], w=[t_E[es]])
                        for sm in range(2):
                            for sub in range(4):
                                gi = sub * 2 + sm
                                first_in_bank = gi in (0, 4, 6)
                                op(T, lambda h, gi=gi, sm=sm, es=es, sub=sub, kt=kt, fb=first_in_bank, vb13=vb13: h.matmul(
                                    OG[gi], lhsT=Eb[es][:, sm * 512 + sub * 128:sm * 512 + (sub + 1) * 128], rhs=vb13[:, kt, 0:129],
                                    start=(kt == 0 and fb), stop=(kt == NT - 1), skip_group_check=True),
                                   r=[t_E[es], t_v], w=[pb[4 + gi // 3]])
                    ob = osb[qt % 2]
                    tob = t_osb[qt % 2]
                    op(V, lambda h, ob=ob: h.tensor_copy(out=ob[:, 0:390], in_=banks[4][:, 0:390]), r=[pb[4]], w=[tob])
                    op(AC, lambda h, ob=ob: h.copy(out=ob[:, 390:780], in_=banks[5][:, 0:390]), r=[pb[5]], w=[tob])
                    op(V, lambda h, ob=ob: h.tensor_copy(out=ob[:, 780:1040], in_=banks[6][:, 0:260]), r=[pb[6]], w=[tob])
                    b7bf = banks[7][:, :].bitcast(BF16)
                    for sub in range(4):
                        e = sub % 2
                        c = qt * 4 + sub
                        o0 = ob[:, (sub * 2) * 130:(sub * 2) * 130 + 130]
                        o1 = ob[:, (sub * 2 + 1) * 130:(sub * 2 + 1) * 130 + 130]
                        rc_, a0_, aa_, yt_, jk_, tep = rc[e], a0[e], aa[e], ytok[e], jk[e], t_ep[e]
                        op(V, lambda h, rc_=rc_, o0=o0: h.reciprocal(out=rc_[:, 0:1], in_=o0[:, 128:129]), r=[tob], w=[tep])
                        op(V, lambda h, rc_=rc_, o1=o1: h.reciprocal(out=rc_[:, 1:2], in_=o1[:, 128:129]), r=[tob], w=[tep])
                        op(V, lambda h, rc_=rc_: h.tensor_tensor(out=rc_[:, 2:3], in0=rc_[:, 1:2], in1=nlam, op=ALU.mult), r=[t_lam], w=[tep])
                        op(V, lambda h, rc_=rc_, a0_=a0_, o0=o0: h.tensor_scalar(out=a0_, in0=o0[:, 0:128], scalar1=rc_[:, 0:1], scalar2=None, op0=ALU.mult), r=[tob], w=[tep])
                        op(V, lambda h, rc_=rc_, a0_=a0_, aa_=aa_, o1=o1: h.scalar_tensor_tensor(out=aa_, in0=o1[:, 0:128], scalar=rc_[:, 2:3], in1=a0_, op0=ALU.mult, op1=ALU.add), r=[tob], w=[tep])
                        op(V, lambda h, rc_=rc_, aa_=aa_, jk_=jk_: h.scalar_tensor_tensor(out=jk_, in0=aa_, scalar=1.0, in1=aa_, op0=ALU.mult, op1=ALU.mult, accum_out=rc_[:, 3:4]), w=[tep])
                        op(AC, lambda h, rc_=rc_: h.activation(out=rc_[:, 4:5], in_=rc_[:, 3:4], func=AF.Ln, scale=sl_scale, bias=lamc[:, 6:7]), r=[t_lam], w=[tep])
                        op(AC, lambda h, rc_=rc_: h.activation(out=rc_[:, 5:6], in_=rc_[:, 4:5], func=AF.Exp, scale=-0.5), w=[tep])
                        op(V, lambda h, rc_=rc_, aa_=aa_, yt_=yt_, c=c, zb3=zb3: h.scalar_tensor_tensor(out=yt_, in0=aa_, scalar=rc_[:, 5:6], in1=zb3[:, c, :], op0=ALU.mult, op1=ALU.mult), r=[t_z], w=[tep])
                        op(T, lambda h, yt_=yt_, sub=sub: h.transpose(out=b7bf[:, sub * 128:(sub + 1) * 128], in_=yt_, identity=ident_bf), r=[tep, t_cbf], w=[pb[7]])
                    op(AC, lambda h, qsl=qsl: h.copy(out=yst[:, qsl], in_=b7bf[:, 0:512]), r=[pb[7]], w=[t_yst])
                dma("sync", scr_y[1, h_], yst, r=[t_yst])

        for l in range(n_layers):
            pl = prm[:, l * NPRM:(l + 1) * NPRM]
            lam_init = 0.8 - 0.6 * math.exp(-0.3 * l)
            m0 = A.mark()
            wv = w_in_d[l].rearrange("(kc p) n -> p kc n", p=128)
            wb = [(A.alloc(8 * 512 * 2), Tok()) for _ in range(3)]
            stg = [(A.alloc(S_ * 2), Tok()) for _ in range(2)]
            stT = [(A.alloc(4 * S_ * 2), Tok()) for _ in range(1)]
            nwl = 0
            nst = 0
            nbk = 0
            fm_list = [(C_QA, 0, None), (C_KA, 1, None), (C_VA, 2, None), (C_QB, 3, None), (C_KB, 4, None),
                       (C_GA, 5, AF.Sigmoid), (C_GB, 6, AF.Sigmoid)]
            nev = 0
            for c0, ti, fn_ in fm_list:
                for g in range(2):
                    wt, twt = wb[nwl % 3]
                    nwl += 1
                    wt3 = v3(wt, 8, 512)
                    dma("gpsimd", wt3, wv[:, :, c0 + g * 512:c0 + (g + 1) * 512], w=[twt])
                    for hb in range(4):
                        st, tst = stg[nst % 2]
                        nst += 1
                        for tt in range(8):
                            bk = nbk % 4
                            nbk += 1
                            for kc in range(8):
                                op("tensor", lambda h, bk=bk, wt3=wt3, hb=hb, kc=kc, tt=tt: h.matmul(
                                    banks[bk][:, :], lhsT=wt3[:, kc, hb * 128:(hb + 1) * 128], rhs=hT3[:, kc, tt * 512:(tt + 1) * 512],
                                    start=(kc == 0), stop=(kc == 7)), r=[twt] + t_hT[tt * 4:tt * 4 + 4], w=[pb[bk]])
                            dst = st[:, tt * 512:(tt + 1) * 512]
                            if fn_ is not None:
                                op("scalar", lambda h, bk=bk, dst=dst, fn_=fn_: h.activation(out=dst, in_=banks[bk][:, :], func=fn_), r=[pb[bk]], w=[tst])
                            elif nev % 2 == 0:
                                op("scalar", lambda h, bk=bk, dst=dst: h.copy(out=dst, in_=banks[bk][:, :]), r=[pb[bk]], w=[tst])
                            else:
                                op("vector", lambda h, bk=bk, dst=dst: h.tensor_copy(out=dst, in_=banks[bk][:, :]), r=[pb[bk]], w=[tst])
                            nev += 1
                        dma("sync", scr_fm[ti, g * 4 + hb], st, r=[tst])
            tm_list = [(C_VB, 0, None), (C_ZA, 1, AF.Silu), (C_ZB, 2, AF.Silu)]
            for c0, ti, fn_ in tm_list:
                for g in range(2):
                    wt, twt = wb[nwl % 3]
                    nwl += 1
                    wt3 = v3(wt, 8, 512)
                    dma("gpsimd", wt3, wv[:, :, c0 + g * 512:c0 + (g + 1) * 512], w=[twt])
                    st, tst = stT[0]
                    st4 = st.rearrange("p (a b c) -> p a b c", a=4, b=NT, c=128)
                    for t in range(NT):
                        bk = nbk % 4
                        nbk += 1
                        for kc in range(8):
                            op("tensor", lambda h, bk=bk, wt3=wt3, kc=kc, t=t: h.matmul(
                                banks[bk][:, :], lhsT=hT3[:, kc, t * 128:(t + 1) * 128], rhs=wt3[:, kc, :],
                                start=(kc == 0), stop=(kc == 7)), r=[twt, t_hT[t]], w=[pb[bk]])
                        dst = st4[:, :, t, :]
                        src = v3(banks[bk][:, :], 4, 128)
                        if fn_ is not None:
                            op("scalar", lambda h, dst=dst, src=src, fn_=fn_: h.activation(out=dst, in_=src, func=fn_), r=[pb[bk]], w=[tst])
                        elif t % 2 == 0:
                            op("scalar", lambda h, dst=dst, src=src: h.copy(out=dst, in_=src), r=[pb[bk]], w=[tst])
                        else:
                            op("vector", lambda h, dst=dst, src=src: h.tensor_copy(out=dst, in_=src), r=[pb[bk]], w=[tst])
                    for hb in range(4):
                        dma("sync", scr_tm[ti, g * 4 + hb], st[:, hb * S_:(hb + 1) * S_], r=[tst])
            wt, twt = wb[nwl % 3]
            nwl += 1
            wt3 = v3(wt, 8, 512)
            dma("gpsimd", wt3[:, :, 0:32], wv[:, :, C_AB:C_AB + 32], w=[twt])
            abt3 = v3(abt, NT, 32)
            for t in range(NT):
                bk = nbk % 4
                nbk += 1
                for kc in range(8):
                    op("tensor", lambda h, bk=bk, wt3=wt3, kc=kc, t=t: h.matmul(
                        banks[bk][:, 0:32], lhsT=hT3[:, kc, t * 128:(t + 1) * 128], rhs=wt3[:, kc, 0:32],
                        start=(kc == 0), stop=(kc == 7)), r=[twt, t_hT[t]], w=[pb[bk]])
                op("vector", lambda h, bk=bk, t=t: h.tensor_copy(out=abt3[:, t, :], in_=banks[bk][:, 0:32]), r=[pb[bk]], w=[t_abt])
            SC.barrier()
            A.reset(m0)
            if stop_after == "A":
                break

            A.reset(m_H)
            phase_B(l, pl)
            SC.barrier()
            if stop_after == "B":
                break
            A.reset(m_H)
            phase_C(l, pl, lam_init)
            SC.barrier()
            if stop_after == "C":
                break
            A.reset(m_W)
            last = (l == L_ - 1)
            dma("sync", nwb, nw_d[l + 1], w=[t_nwb])
            wts = []
            for wd in (w_pa_d, w_pb_d, w_out_d):
                wt = A.alloc(8 * D_ * 2)
                tw = Tok()
                wt3 = v3(wt, 8, D_)
                for hh in range(2):
                    dma("gpsimd", wt3[:, :, hh * 512:(hh + 1) * 512], wd[l].rearrange("(kc p) n -> p kc n", p=128)[:, :, hh * 512:(hh + 1) * 512], w=[tw])
                wts.append((wt3, tw))
            (wpa3, twpa), (wpb3, twpb), (wo3, two) = wts
            yb_ = [(A.alloc(8 * 512 * 2), A.alloc(8 * 512 * 2), Tok()) for _ in range(1)]
            gb_ = [(A.alloc(2 * 512 * 2), Tok()) for _ in range(2)]
            mg = A.alloc(8 * 512 * 2)
            mg3 = v3(mg, 8, 512)
            t_mg = Tok()
            tt1 = A.alloc(512 * 4, F32)
            tt2 = A.alloc(512 * 4, F32)
            t_tt = Tok()
            xts = [(A.alloc(D_ * 4, F32), Tok()) for _ in range(2)]
            ntm = [norm_tmp() for _ in range(1)]
            ng = 0
            for tt in range(8):
                ya, yb, ty = yb_[0]
                ya3, yb3 = v3(ya, 8, 512), v3(yb, 8, 512)
                dma("sync", ya3, scr_y[0][:, :, tt * 512:(tt + 1) * 512].rearrange("h p t -> p h t"), w=[ty])
                dma("sync", yb3, scr_y[1][:, :, tt * 512:(tt + 1) * 512].rearrange("h p t -> p h t"), w=[ty])
                for m in range(8):
                    gg, tg = gb_[ng % 2]
                    ng += 1
                    dma("sync", gg[:, 0:512], scr_fm[5, m][:, tt * 512:(tt + 1) * 512], w=[tg])
                    dma("sync", gg[:, 512:1024], scr_fm[6, m][:, tt * 512:(tt + 1) * 512], w=[tg])
                    for kc in range(8):
                        op("tensor", lambda h, kc=kc, m=m, ya3=ya3: h.matmul(banks[0][:, :], lhsT=wpa3[:, kc, m * 128:(m + 1) * 128], rhs=ya3[:, kc, :],
                                                                 start=(kc == 0), stop=(kc == 7)), r=[twpa, ty], w=[pb[0]])
                    for kc in range(8):
                        op("tensor", lambda h, kc=kc, m=m, yb3=yb3: h.matmul(banks[1][:, :], lhsT=wpb3[:, kc, m * 128:(m + 1) * 128], rhs=yb3[:, kc, :],
                                                                 start=(kc == 0), stop=(kc == 7)), r=[twpb, ty], w=[pb[1]])
                    op(V, lambda h, gg=gg: h.tensor_tensor(out=tt1, in0=banks[0][:, :], in1=gg[:, 0:512], op=ALU.mult), r=[pb[0], tg], w=[t_tt])
                    op(V, lambda h, gg=gg: h.tensor_tensor(out=tt2, in0=banks[1][:, :], in1=gg[:, 512:1024], op=ALU.mult), r=[pb[1], tg], w=[t_tt])
                    op("gpsimd", lambda h, m=m: h.tensor_tensor(out=mg3[:, m, :], in0=tt1, in1=tt2, op=ALU.add), r=[t_tt], w=[t_mg])
                for sub in range(4):
                    t = tt * 4 + sub
                    xt, tx = xts[t % 2]
                    dma("sync", xt, scr_x[t * 128:(t + 1) * 128, :], w=[tx])
                    for nh in range(2):
                        bk = 2 + nh
                        for kc in range(8):
                            op("tensor", lambda h, kc=kc, nh=nh, sub=sub, bk=bk: h.matmul(
                                banks[bk][:, :], lhsT=mg3[:, kc, sub * 128:(sub + 1) * 128], rhs=wo3[:, kc, nh * 512:(nh + 1) * 512],
                                start=(kc == 0), stop=(kc == 7)), r=[two, t_mg], w=[pb[bk]])
                        op(V, lambda h, nh=nh, bk=bk, xt=xt: h.tensor_tensor(out=xt[:, nh * 512:(nh + 1) * 512], in0=banks[bk][:, :], in1=xt[:, nh * 512:(nh + 1) * 512], op=ALU.add),
                           r=[pb[bk]], w=[tx])
                    if not last:
                        dma("sync", scr_x[t * 128:(t + 1) * 128, :], xt, r=[tx])
                    norm_tile(xt, tx, t, ntm[0], last, 4 + (t % 2))
            SC.barrier()
            A.reset(m_W)

        SC.barrier()

        with nc.Block() as block:
            @block.sync
            def _(h):
                SC.replay("sync", h)

            @block.scalar
            def _(h):
                SC.replay("scalar", h)

            @block.vector
            def _(h):
                SC.replay("vector", h)

            @block.gpsimd
            def _(h):
                SC.replay("gpsimd", h)

            @block.tensor
            def _(h):
                SC.replay("tensor", h)
    return nc


def _host_consts():
    c = np.zeros((128, NCST), np.float32)
    p = np.arange(128)[:, None]
    f = np.arange(128)[None, :]
    c[:, K_ID:K_ID + 128] = (p == f)
    c[:, K_PMF:K_PMF + 128] = np.where(p > f, 0.0, BIGM)
    c[:, K_NMF:K_NMF + 128] = np.where(f >= p, 0.0, -BIGM)
    c[:, K_PMB:K_PMB + 128] = np.where(p < f, 0.0, BIGM)
    c[:, K_NMB:K_NMB + 128] = np.where(f <= p, 0.0, -BIGM)
    c[:, K_TRF:K_TRF + 128] = (p <= f)
    c[:, K_TRB:K_TRB + 128] = (p >= f)
    perm = np.zeros((128, 128), np.float32)
    for base in (0, 64):
        for d in range(8):
            perm[base + d + 8, base + d] = 1.0
            perm[base + d, base + d + 8] = 1.0
    c[:, K_PERM:K_PERM + 128] = perm
    c[:, K_ONE:K_ONE + 128] = 1.0
    inv_freq = (500000.0 ** (-(np.arange(0, 16, 2, dtype=np.float32) / np.float32(16)))).astype(np.float32)
    for base in (0, 64):
        for d in range(8):
            c[base + d, K_INVF] = inv_freq[d]
            c[base + d + 8, K_INVF] = inv_freq[d]
            c[base + d, K_SGN] = -1.0
            c[base + d + 8, K_SGN] = 1.0
    return c


def _host_params(inputs):
    prm = np.zeros((128, L_, NPRM), np.float32)
    for l in range(L_):
        cw = inputs["conv_w"][l]
        cwr = cw.reshape(5, 3, 8, 128).transpose(3, 1, 2, 0)
        prm[:, l, P_CW:P_CW + 120] = cwr.reshape(128, 120)
        prm[:, l, P_ALOG:P_ALOG + 16] = inputs["a_log"][l].reshape(1, 16)
        prm[:, l, P_DTB:P_DTB + 16] = inputs["dt_bias"][l].reshape(1, 16)
        prm[:, l, P_GNW:P_GNW + 128] = inputs["gdn_norm_w"][l].reshape(1, 128)
        prm[:, l, P_SLW:P_SLW + 128] = inputs["diff_subln_w"][l].reshape(1, 128)
        prm[:, l, P_LAM:P_LAM + 256] = inputs["diff_lambda"][l].reshape(1, 256)
    nw = np.zeros((L_ + 1, 128, D_), np.float32)
    for l in range(L_):
        nw[l] = inputs["norm_w"][l][None, :]
    nw[L_] = inputs["final_norm_w"][None, :]
    return prm.reshape(128, L_ * NPRM), nw


def _in_maps(inputs):
    cst = _host_consts()
    prm, nw = _host_params(inputs)
    f32 = lambda a: np.ascontiguousarray(np.asarray(a, dtype=np.float32))
    w_in, w_pa, w_pb, w_out = (f32(inputs[k]) for k in ("w_in", "w_pa", "w_pb", "w_out"))
    maps = []
    for b in range(8):
        maps.append({
            "x": f32(inputs["x"][b]),
            "pos": np.ascontiguousarray(np.broadcast_to(np.asarray(inputs["positions"][b], dtype=np.int32)[None, :], (128, S_))),
            "w_in": w_in, "w_pa": w_pa, "w_pb": w_pb, "w_out": w_out,
            "cst": cst, "prm": prm, "nw": nw,
        })
    return maps


def kernel(**inputs):
    nc = build()
    maps = _in_maps(inputs)
    res = run_bass_kernel_spmd(nc, maps, core_ids=list(range(8)))
    return np.stack([np.asarray(r["out"], dtype=np.float32) for r in res.results], axis=0)
```

```python
import math
import numpy as np
import concourse.bass as bass
import concourse.mybir as mybir
from concourse.bass_utils import run_bass_kernel_spmd

F32, BF16, I32 = mybir.dt.float32, mybir.dt.bfloat16, mybir.dt.int32
AF = mybir.ActivationFunctionType
ALU = mybir.AluOpType
AX = mybir.AxisListType

S_ = 4096
D_ = 1024
L_ = 4
NIN = 10272
NT = 32
EPS = 1e-6
C_QA, C_KA, C_VA, C_AB, C_ZA, C_QB, C_KB, C_VB, C_ZB, C_GA, C_GB = (
    0, 1024, 2048, 3072, 3104, 4128, 5152, 6176, 7200, 8224, 9248)
NPRM = 664
P_CW, P_ALOG, P_DTB, P_GNW, P_SLW, P_LAM = 0, 120, 136, 152, 280, 408
NCST = 9 * 128 + 2
K_ID, K_PMF, K_NMF, K_PMB, K_NMB, K_TRF, K_TRB, K_PERM, K_ONE = [i * 128 for i in range(9)]
K_INVF, K_SGN = 9 * 128, 9 * 128 + 1
TWO_PI = 2.0 * math.pi
TWO_PI_HI = 6.28125
TWO_PI_LO = TWO_PI - TWO_PI_HI
BIGM = 30000.0


class Tok:
    __slots__ = ("w", "r")

    def __init__(self):
        self.w = None
        self.r = {}


class Sched:
    ENG = ("tensor", "scalar", "vector", "gpsimd", "sync")

    def __init__(self, nc, stack):
        self.nc = nc
        self.sems = []
        self.E = {}
        for n in self.ENG:
            s = stack.enter_context(nc.semaphore("s_" + n))
            self.sems.append(s)
            self.E[n] = dict(si=len(self.sems) - 1, cnt=0, waited={}, prog=[], slots=[], nxt=0)
        for n in ("sync", "gpsimd", "scalar"):
            for k in range(8):
                s = stack.enter_context(nc.semaphore("d_%s%d" % (n, k)))
                self.sems.append(s)
                self.E[n]["slots"].append([len(self.sems) - 1, 0])
        self.nops = 0

    def _deps(self, e, r, w):
        need = {}
        for t in r:
            if t.w is not None:
                s, v = t.w
                if need.get(s, 0) < v:
                    need[s] = v
        for t in w:
            if t.w is not None:
                s, v = t.w
                if need.get(s, 0) < v:
                    need[s] = v
            for s, v in t.r.items():
                if need.get(s, 0) < v:
                    need[s] = v
        waits = []
        for s, v in need.items():
            if s == e["si"] and e is self.E["tensor"]:
                continue
            if e["waited"].get(s, 0) >= v:
                continue
            e["waited"][s] = v
            waits.append((s, v))
        return waits

    def _mark(self, me, r, w):
        s, v = me
        for t in r:
            if t.r.get(s, 0) < v:
                t.r[s] = v
        for t in w:
            t.w = me
            t.r = {}

    def op(self, en, fn, r=(), w=()):
        e = self.E[en]
        waits = self._deps(e, r, w)
        e["cnt"] += 1
        e["prog"].append((waits, fn, (e["si"], 1)))
        self._mark((e["si"], e["cnt"]), r, w)
        self.nops += 1

    def dma(self, en, out, in_, r=(), w=()):
        e = self.E[en]
        waits = self._deps(e, r, w)
        slot = e["slots"][e["nxt"] % len(e["slots"])]
        e["nxt"] += 1
        if slot[1] > 0 and e["waited"].get(slot[0], 0) < slot[1]:
            waits.append((slot[0], slot[1]))
            e["waited"][slot[0]] = slot[1]
        slot[1] += 16
        e["prog"].append((waits, (lambda h, o=out, i=in_: h.dma_start(out=o, in_=i)), (slot[0], 16)))
        self._mark((slot[0], slot[1]), r, w)
        self.nops += 1

    def barrier(self):
        tgt = []
        for n in self.ENG:
            e = self.E[n]
            if e["cnt"] > 0:
                tgt.append((e["si"], e["cnt"]))
            for sl in e["slots"]:
                if sl[1] > 0:
                    tgt.append((sl[0], sl[1]))
        for n in self.ENG:
            e = self.E[n]
            waits = []
            for s, v in tgt:
                if s == e["si"]:
                    continue
                if e["waited"].get(s, 0) >= v:
                    continue
                e["waited"][s] = v
                waits.append((s, v))
            if waits:
                e["prog"].append((waits, None, None))

    def replay(self, en, h):
        for waits, fn, inc in self.E[en]["prog"]:
            for s, v in waits:
                h.wait_ge(self.sems[s], v)
            if fn is not None:
                ins = fn(h)
                ins.then_inc(self.sems[inc[0]], inc[1])


class Arena:
    def __init__(self, big, nbytes):
        self.big = big
        self.cap = nbytes
        self.top = 0

    def mark(self):
        return self.top

    def reset(self, m):
        self.top = m

    def alloc(self, nbytes, dt=BF16):
        off = (self.top + 63) // 64 * 64
        assert off + nbytes <= self.cap, ("SBUF arena overflow", off, nbytes, self.cap)
        self.top = off + nbytes
        ap = self.big[:, off // 2:(off + nbytes) // 2]
        if dt is not BF16:
            ap = ap.bitcast(dt)
        return ap


def v3(ap, a, b):
    return ap.rearrange("p (a b) -> p a b", a=a, b=b)


def build(n_layers=L_, dbg=False, stop_after=None):
    from contextlib import ExitStack
    nc = bass.Bass("TRN2", target_bir_lowering=False)

    def din(name, shape, dt=F32):
        return nc.dram_tensor(name, shape, dt, kind="ExternalInput").ap()

    x_d = din("x", [S_, D_])
    pos_d = din("pos", [128, S_], I32)
    w_in_d = din("w_in", [L_, D_, NIN])
    w_pa_d = din("w_pa", [L_, D_, D_])
    w_pb_d = din("w_pb", [L_, D_, D_])
    w_out_d = din("w_out", [L_, D_, D_])
    cst_d = din("cst", [128, NCST])
    prm_d = din("prm", [128, L_ * NPRM])
    nw_d = din("nw", [L_ + 1, 128, D_])
    out_d = nc.dram_tensor("out", [S_, D_], F32, kind="ExternalOutput").ap()
    kS = "ExternalOutput" if dbg else "Internal"
    scr_fm = nc.dram_tensor("scr_fm", [7, 8, 128, S_], BF16, kind=kS).ap()
    scr_tm = nc.dram_tensor("scr_tm", [3, 8, 128, S_], BF16, kind=kS).ap()
    scr_y = nc.dram_tensor("scr_y", [2, 8, 128, S_], BF16, kind=kS).ap()
    scr_x = nc.dram_tensor("scr_x", [S_, D_], F32, kind=kS).ap()

    with ExitStack() as stack:
        ARENA_BYTES = 212000
        big = stack.enter_context(nc.sbuf_tensor("big", [128, ARENA_BYTES // 2], BF16))
        psp = [stack.enter_context(nc.psum_tensor("pp%d" % i, [128, 1024], F32)) for i in range(4)]
        banks = [psp[i // 2][:, (i % 2) * 512:(i % 2 + 1) * 512] for i in range(8)]
        SC = Sched(nc, stack)
        A = Arena(big, ARENA_BYTES)
        op, dma = SC.op, SC.dma

        cst = A.alloc(NCST * 4, F32)
        prm = A.alloc(L_ * NPRM * 4, F32)
        cbf = A.alloc(4 * 128 * 2)
        ident_bf, perm_bf, ones_bf = cbf[:, 0:128], cbf[:, 128:256], cbf[:, 256:384]
        ccol = A.alloc(16 * 4, F32)
        Ct = A.alloc(S_ * 2)
        St = A.alloc(S_ * 2)
        abt = A.alloc(NT * 32 * 4, F32)
        nwb = A.alloc(D_ * 4, F32)
        lamc = A.alloc(8 * 4, F32)
        t_cst, t_prm, t_cbf, t_ccol, t_rope, t_abt, t_gt, t_nwb, t_lam = [Tok() for _ in range(9)]
        ident_f = cst[:, K_ID:K_ID + 128]
        ones_f = cst[:, K_ONE:K_ONE + 128]
        m_H = A.mark()
        hT = A.alloc(8 * S_ * 2)
        hT3 = v3(hT, 8, S_)
        t_hT = [Tok() for _ in range(NT)]
        m_W = A.mark()

        pb = [Tok() for _ in range(8)]

        dma("sync", cst, cst_d, w=[t_cst])
        dma("sync", prm, prm_d, w=[t_prm])
        op("vector", lambda h: h.tensor_copy(out=cbf[:, 0:128], in_=cst[:, K_ID:K_ID + 128]), r=[t_cst], w=[t_cbf])
        op("vector", lambda h: h.tensor_copy(out=cbf[:, 128:256], in_=cst[:, K_PERM:K_PERM + 128]), r=[t_cst], w=[t_cbf])
        op("vector", lambda h: h.tensor_copy(out=cbf[:, 256:384], in_=cst[:, K_ONE:K_ONE + 128]), r=[t_cst], w=[t_cbf])
        CC_EPS, CC_ONE, CC_DKEPS, CC_PI2 = 0, 1, 2, 3
        for ci, val in ((CC_EPS, EPS), (CC_ONE, 1.0), (CC_DKEPS, 128.0 * EPS), (CC_PI2, math.pi / 2)):
            op("vector", lambda h, ci=ci, val=val: h.memset(ccol[:, ci:ci + 1], val), w=[t_ccol])

        def cc(i):
            return ccol[:, i:i + 1]

        m0 = A.mark()
        posi = A.alloc(S_ * 4, I32)
        ang = A.alloc(S_ * 4, F32)
        uu = A.alloc(S_ * 4, F32)
        rr = A.alloc(S_ * 4, F32)
        ki = A.alloc(S_ * 4, I32)
        tp = Tok()
        dma("sync", posi, pos_d, w=[tp])
        V = "vector"
        op(V, lambda h: h.tensor_copy(out=ang, in_=posi), r=[tp], w=[tp])
        op(V, lambda h: h.tensor_scalar(out=ang, in0=ang, scalar1=cst[:, K_INVF:K_INVF + 1], scalar2=None, op0=ALU.mult), r=[tp, t_cst], w=[tp])
        op(V, lambda h: h.tensor_scalar(out=uu, in0=ang, scalar1=1.0 / TWO_PI, scalar2=None, op0=ALU.mult), r=[tp], w=[tp])
        op(V, lambda h: h.tensor_copy(out=ki, in_=uu), r=[tp], w=[tp])
        op(V, lambda h: h.tensor_copy(out=uu, in_=ki), r=[tp], w=[tp])
        op(V, lambda h: h.scalar_tensor_tensor(out=rr, in0=uu, scalar=-TWO_PI_HI, in1=ang, op0=ALU.mult, op1=ALU.add), r=[tp], w=[tp])
        op(V, lambda h: h.scalar_tensor_tensor(out=rr, in0=uu, scalar=-TWO_PI_LO, in1=rr, op0=ALU.mult, op1=ALU.add), r=[tp], w=[tp])

        def fixup(r_):
            op(V, lambda h: h.tensor_scalar(out=uu, in0=r_, scalar1=math.pi, scalar2=None, op0=ALU.is_gt), r=[tp], w=[tp])
            op(V, lambda h: h.scalar_tensor_tensor(out=r_, in0=uu, scalar=-TWO_PI, in1=r_, op0=ALU.mult, op1=ALU.add), r=[tp], w=[tp])
            op(V, lambda h: h.tensor_scalar(out=uu, in0=r_, scalar1=-math.pi, scalar2=None, op0=ALU.is_lt), r=[tp], w=[tp])
            op(V, lambda h: h.scalar_tensor_tensor(out=r_, in0=uu, scalar=TWO_PI, in1=r_, op0=ALU.mult, op1=ALU.add), r=[tp], w=[tp])

        fixup(rr)
        op("scalar", lambda h: h.activation(out=St, in_=rr, func=AF.Sin, scale=cst[:, K_SGN:K_SGN + 1]), r=[tp, t_cst], w=[t_rope])
        op(V, lambda h: h.tensor_scalar(out=ang, in0=rr, scalar1=math.pi / 2, scalar2=None, op0=ALU.add), r=[tp], w=[tp])
        fixup(ang)
        op("scalar", lambda h: h.activation(out=Ct, in_=ang, func=AF.Sin), r=[tp], w=[t_rope])
        SC.barrier()
        A.reset(m0)

        def norm_tile(xt, t_x, t, tmp, final, bank):
            junk, ss, hn, t_tmp = tmp["junk"], tmp["ss"], tmp["hn"], tmp["tok"]
            op("scalar", lambda h: h.activation(out=junk, in_=xt, func=AF.Square, accum_out=ss[:, 0:1]), r=[t_x], w=[t_tmp])
            op("scalar", lambda h: h.activation(out=ss[:, 1:2], in_=ss[:, 0:1], func=AF.Sqrt, scale=1.0 / D_, bias=cc(CC_EPS)), r=[t_ccol], w=[t_tmp])
            op(V, lambda h: h.reciprocal(out=ss[:, 2:3], in_=ss[:, 1:2]), w=[t_tmp])
            if final:
                op(V, lambda h: h.scalar_tensor_tensor(out=xt, in0=xt, scalar=ss[:, 2:3], in1=nwb, op0=ALU.mult, op1=ALU.mult), r=[t_nwb, t_tmp], w=[t_x])
                dma("sync", out_d[t * 128:(t + 1) * 128, :], xt, r=[t_x])
                return
            op(V, lambda h: h.scalar_tensor_tensor(out=hn, in0=xt, scalar=ss[:, 2:3], in1=nwb, op0=ALU.mult, op1=ALU.mult), r=[t_x, t_nwb], w=[t_tmp])
            pbk = banks[bank][:, :].bitcast(BF16)
            for kc in range(8):
                op("tensor", lambda h, kc=kc: h.transpose(out=pbk[:, kc * 128:(kc + 1) * 128], in_=hn[:, kc * 128:(kc + 1) * 128], identity=ident_bf),
                   r=[t_tmp, t_cbf], w=[pb[bank]])
            op("scalar", lambda h: h.copy(out=hT3[:, :, t * 128:(t + 1) * 128], in_=v3(pbk, 8, 128)), r=[pb[bank]], w=[t_hT[t]])

        def norm_tmp():
            hn_ = A.alloc(D_ * 2)
            return dict(junk=hn_, ss=A.alloc(16, F32), hn=hn_, tok=Tok())

        m0 = A.mark()
        dma("sync", nwb, nw_d[0], w=[t_nwb])
        xts = [(A.alloc(D_ * 4, F32), Tok()) for _ in range(3)]
        ntm = [norm_tmp() for _ in range(2)]
        for t in range(NT):
            xt, tx = xts[t % 3]
            dma("sync", xt, x_d[t * 128:(t + 1) * 128, :], w=[tx])
            dma("sync", scr_x[t * 128:(t + 1) * 128, :], xt, r=[tx])
            norm_tile(xt, tx, t, ntm[t % 2], False, t % 2)
        SC.barrier()
        A.reset(m0)

        def phase_B(l, pl):
            T, AC, G = "tensor", "scalar", "gpsimd"
            gta = A.alloc(16 * 1024, F32)

            def gflat(i):
                return gta[:, i * 256:(i + 1) * 256]

            def gtile(i):
                return v3(gflat(i), NT, 8)

            beta = [gtile(0), gtile(1)]
            g_ = [gtile(2), gtile(3)]
            Gc = [gtile(4), gtile(5)]
            nb = [gtile(6), gtile(7)]
            eG = [gtile(8), gtile(9)]
            nbeG = [gtile(10), gtile(11)]
            eGt = [gtile(12), gtile(13)]
            nA = A.alloc(16 * 4, F32)
            t_g = Tok()
            abt3 = v3(abt, NT, 32)
            for d in range(2):
                sl = slice(d * 256, (d + 1) * 256)
                op(AC, lambda h, d=d: h.activation(out=beta[d], in_=abt3[:, :, d * 8:(d + 1) * 8], func=AF.Sigmoid), r=[t_abt], w=[t_g])
                dtb = pl[:, P_DTB + 8 * d:P_DTB + 8 * d + 8].unsqueeze(1).broadcast_to([128, NT, 8])
                op(V, lambda h, d=d, dtb=dtb: h.tensor_tensor(out=g_[d], in0=abt3[:, :, 16 + 8 * d:24 + 8 * d], in1=dtb, op=ALU.add), r=[t_abt, t_prm], w=[t_g])
                op(AC, lambda h, d=d: h.activation(out=g_[d], in_=g_[d], func=AF.Exp), w=[t_g])
                op(AC, lambda h, d=d: h.activation(out=g_[d], in_=g_[d], func=AF.Ln, bias=cc(CC_ONE)), r=[t_ccol], w=[t_g])
                op(AC, lambda h, d=d: h.activation(out=nA[:, 8 * d:8 * d + 8], in_=pl[:, P_ALOG + 8 * d:P_ALOG + 8 * d + 8], func=AF.Exp), r=[t_prm], w=[t_g])
                nAb = nA[:, 8 * d:8 * d + 8].unsqueeze(1).broadcast_to([128, NT, 8])
                op(V, lambda h, d=d, nAb=nAb: h.scalar_tensor_tensor(out=g_[d], in0=g_[d], scalar=-1.0, in1=nAb, op0=ALU.mult, op1=ALU.mult), w=[t_g])
                tri = cst[:, K_TRF:K_TRF + 128] if d == 0 else cst[:, K_TRB:K_TRB + 128]
                op(T, lambda h, d=d, sl=sl, tri=tri: h.matmul(banks[0][:, sl], lhsT=tri, rhs=gflat(2 + d), start=True, stop=True), r=[t_g, t_cst], w=[pb[0]])
                op(T, lambda h, d=d, sl=sl: h.matmul(banks[1][:, sl], lhsT=ones_f, rhs=gflat(2 + d), start=True, stop=True), r=[t_g, t_cst], w=[pb[1]])
                op(V, lambda h, d=d, sl=sl: h.tensor_copy(out=gflat(4 + d), in_=banks[0][:, sl]), w=[t_g, pb[0]])
                op(AC, lambda h, d=d, sl=sl: h.activation(out=gflat(8 + d), in_=banks[0][:, sl], func=AF.Exp), w=[t_g, pb[0]])
                op(AC, lambda h, d=d, sl=sl: h.activation(out=gflat(12 + d), in_=banks[1][:, sl], func=AF.Exp), w=[t_g, pb[1]])
                op(V, lambda h, d=d, sl=sl: h.tensor_tensor(out=gflat(14 + d), in0=banks[1][:, sl], in1=gflat(4 + d), op=ALU.subtract), w=[t_g, pb[1]])
                op(AC, lambda h, d=d: h.activation(out=gflat(14 + d), in_=gflat(14 + d), func=AF.Exp), w=[t_g])
                op(V, lambda h, d=d: h.tensor_scalar(out=gflat(6 + d), in0=gflat(d), scalar1=-1.0, scalar2=None, op0=ALU.mult), w=[t_g])
                op(V, lambda h, d=d: h.tensor_tensor(out=gflat(10 + d), in0=gflat(6 + d), in1=gflat(8 + d), op=ALU.mult), w=[t_g])
            eD = [gtile(14), gtile(15)]

            xpad = A.alloc(4100 * 2)
            acc = A.alloc(S_ * 4, F32)
            sqr = A.alloc(S_ * 2)
            rsb = [A.alloc(512 * 4, F32) for _ in range(2)]
            qT = A.alloc(S_ * 2)
            kT = A.alloc(S_ * 2)
            vb = [A.alloc(S_ * 2) for _ in range(2)]
            kd = [A.alloc(S_ * 2) for _ in range(2)]
            za = A.alloc(S_ * 2)
            oacc = A.alloc(S_ * 4, F32)
            oacc3 = v3(oacc, NT, 128)
            yst = xpad[:, 2:2 + S_]
            ssum = A.alloc(96 * 4, F32)
            RN = 4
            NI = 4
            TTr = [[A.alloc(512, F32) for _ in range(RN)] for _ in range(2)]
            QKr = [[A.alloc(256) for _ in range(RN)] for _ in range(2)]
            tA = [A.alloc(512, F32) for _ in range(NI)]
            tB = [A.alloc(512, F32) for _ in range(NI)]
            dec = [A.alloc(512, F32) for _ in range(NI)]
            decT = [A.alloc(256) for _ in range(NI)]
            pair = [[A.alloc(1024, F32) for _ in range(2)] for _ in range(NI)]
            X = [[A.alloc(512, F32) for _ in range(2)] for _ in range(NI)]
            rhs = [A.alloc(512, F32) for _ in range(2)]
            vnew = [A.alloc(256) for _ in range(2)]
            tmp = [A.alloc(512, F32) for _ in range(2)]
            tmp2 = [A.alloc(512, F32) for _ in range(2)]
            S32 = [A.alloc(512, F32) for _ in range(2)]
            Sbf = [A.alloc(256) for _ in range(2)]
            tk = lambda: Tok()
            t_xp, t_acc, t_sqr, t_qT, t_kT, t_za, t_ss = [Tok() for _ in range(7)]
            t_yst = t_xp
            t_rs = [tk(), tk()]
            t_vb = [tk(), tk()]
            t_kd = [tk(), tk()]
            t_oacc = [tk() for _ in range(NT)]
            t_TT = [[tk() for _ in range(RN)] for _ in range(2)]
            t_QK = [[tk() for _ in range(RN)] for _ in range(2)]
            t_rhs, t_vnew, t_tmp, t_tmp2, t_S32, t_S = [[tk(), tk()] for _ in range(6)]
            t_tA, t_tB, t_dec, t_decT = [[tk() for _ in range(NI)] for _ in range(4)]
            t_pair = [[tk(), tk()] for _ in range(NI)]
            t_X = [[tk(), tk()] for _ in range(NI)]
            KK = [banks[it][:, 0:128] for it in range(NI)]
            QKT = [banks[it][:, 128:256] for it in range(NI)]
            Gb = [banks[it][:, 256:384] for it in range(NI)]
            MT = [banks[it][:, 384:512] for it in range(NI)]
            PPQ = [banks[it][:, 0:256] for it in range(NI)]
            XPQ = [banks[it][:, 256:384] for it in range(NI)]
            KS = [banks[4 + d][:, 0:128] for d in range(2)]
            VN = [banks[4 + d][:, 128:256] for d in range(2)]
            O1 = [banks[4 + d][:, 256:384] for d in range(2)]
            O2 = [banks[4 + d][:, 384:512] for d in range(2)]
            DS = [banks[6 + d][:, 0:128] for d in range(2)]
            PMm = [cst[:, K_PMF:K_PMF + 128], cst[:, K_PMB:K_PMB + 128]]
            NMm = [cst[:, K_NMF:K_NMF + 128], cst[:, K_NMB:K_NMB + 128]]

            op(V, lambda h: h.memset(xpad[:, 0:2], 0.0), w=[t_xp])
            op(V, lambda h: h.memset(xpad[:, 4098:4100], 0.0), w=[t_xp])

            for h_ in range(8):
                def tr_scaled(srcT, t_src, dst, t_dst, sc):
                    for cg in range(4):
                        bk = 6 + (cg % 2)
                        pbk = banks[bk][:, :].bitcast(BF16)
                        for j in range(8):
                            c = cg * 8 + j
                            op(T, lambda h, j=j, c=c, pbk=pbk: h.transpose(out=pbk[:, j * 128:(j + 1) * 128], in_=srcT[:, c * 128:(c + 1) * 128], identity=ident_bf),
                               r=[t_src, t_cbf], w=[pb[bk]])
                        for d in range(2):
                            scb = sc[d][:, cg * 8:(cg + 1) * 8, h_:h_ + 1].broadcast_to([128, 8, 128])
                            op(V, lambda h, d=d, cg=cg, pbk=pbk, scb=scb: h.tensor_tensor(out=v3(dst[d], NT, 128)[:, cg * 8:(cg + 1) * 8, :], in0=v3(pbk, 8, 128), in1=scb, op=ALU.mult),
                               r=[pb[bk], t_g], w=[t_dst[d]])

                for ti in range(3):
                    dma("sync", xpad[:, 2:4098], scr_fm[ti, h_], w=[t_xp])
                    base = P_CW + (ti * 8 + h_) * 5
                    op(V, lambda h, base=base: h.tensor_scalar(out=acc, in0=xpad[:, 0:4096], scalar1=pl[:, base:base + 1], scalar2=None, op0=ALU.mult), r=[t_xp, t_prm], w=[t_acc])
                    for k in range(1, 5):
                        op(V, lambda h, base=base, k=k: h.scalar_tensor_tensor(out=acc, in0=xpad[:, k:k + 4096], scalar=pl[:, base + k:base + k + 1], in1=acc, op0=ALU.mult, op1=ALU.add),
                           r=[t_xp, t_prm], w=[t_acc])
                    if ti == 2:
                        op(AC, lambda h: h.activation(out=sqr, in_=acc, func=AF.Silu), r=[t_acc], w=[t_sqr])
                        tr_scaled(sqr, t_sqr, vb, t_vb, beta)
                    else:
                        op(AC, lambda h: h.activation(out=acc, in_=acc, func=AF.Silu), w=[t_acc])
                        op(AC, lambda h: h.activation(out=sqr, in_=acc, func=AF.Square), r=[t_acc], w=[t_sqr])
                        dstT, t_dst = (qT, t_qT) if ti == 0 else (kT, t_kT)
                        for tt in range(8):
                            bk = 6 + (tt % 2)
                            tsl = slice(tt * 512, (tt + 1) * 512)
                            rs = rsb[tt % 2]
                            trs = t_rs[tt % 2]
                            op(T, lambda h, bk=bk, tsl=tsl: h.matmul(banks[bk][:, :], lhsT=ones_bf, rhs=sqr[:, tsl], start=True, stop=True), r=[t_sqr, t_cbf], w=[pb[bk]])
                            op(AC, lambda h, bk=bk, rs=rs, ti=ti: h.activation(out=rs, in_=banks[bk][:, :], func=AF.Ln, scale=(128.0 if ti == 0 else 1.0),
                                                                        bias=cc(CC_DKEPS if ti == 0 else CC_EPS)), r=[pb[bk], t_ccol], w=[trs])
                            op(AC, lambda h, rs=rs: h.activation(out=rs, in_=rs, func=AF.Exp, scale=-0.5), w=[trs])
                            op(V, lambda h, rs=rs, tsl=tsl, dstT=dstT: h.tensor_tensor(out=dstT[:, tsl], in0=acc[:, tsl], in1=rs, op=ALU.mult), r=[t_acc, trs], w=[t_dst])
                        if ti == 1:
                            tr_scaled(kT, t_kT, kd, t_kd, eD)
                dma("sync", za, scr_tm[1, h_], w=[t_za])
                op(G, lambda h: h.memset(oacc, 0.0), w=t_oacc)
                for d in range(2):
                    op(G, lambda h, d=d: h.memset(S32[d], 0.0), w=[t_S32[d]])
                    op(G, lambda h, d=d: h.memset(Sbf[d], 0.0), w=[t_S[d]])

                def prep_batch(steps, hh=h_):
                    items = []
                    for si, i in enumerate(steps):
                        for d in range(2):
                            items.append((si * 2 + d, d, (i, NT - 1 - i)[d], i % RN))
                    for it, d, c, s in items:
                        kc_ = kT[:, c * 128:(c + 1) * 128]
                        qc_ = qT[:, c * 128:(c + 1) * 128]
                        op(T, lambda h, it=it, kc_=kc_: h.matmul(KK[it], lhsT=kc_, rhs=kc_, start=True, stop=True), r=[t_kT], w=[pb[it]])
                        op(T, lambda h, it=it, kc_=kc_, qc_=qc_: h.matmul(QKT[it], lhsT=kc_, rhs=qc_, start=True, stop=True), r=[t_kT, t_qT], w=[pb[it]])
                        op(G, lambda h, it=it, d=d, c=c: h.tensor_scalar(out=tA[it], in0=ident_f, scalar1=Gc[d][:, c, hh:hh + 1], scalar2=None, op0=ALU.mult), r=[t_g, t_cst], w=[t_tA[it]])
                        op(T, lambda h, it=it: h.matmul(Gb[it], lhsT=ones_f, rhs=tA[it], start=True, stop=True), r=[t_tA[it], t_cst], w=[pb[it]])
                    yield
                    for it, d, c, s in items:
                        op(V, lambda h, it=it, d=d, c=c: h.scalar_tensor_tensor(out=tB[it], in0=Gb[it], scalar=Gc[d][:, c, hh:hh + 1], in1=PMm[d], op0=ALU.subtract, op1=ALU.add),
                           r=[t_g, t_cst], w=[t_tB[it], pb[it]])
                    yield
                    for it, d, c, s in items:
                        op(AC, lambda h, it=it: h.activation(out=dec[it], in_=tB[it], func=AF.Exp, scale=-1.0), r=[t_tB[it]], w=[t_dec[it]])
                        op(V, lambda h, it=it, d=d, c=c: h.scalar_tensor_tensor(out=tA[it], in0=Gb[it], scalar=Gc[d][:, c, hh:hh + 1], in1=NMm[d], op0=ALU.subtract, op1=ALU.add),
                           r=[t_g, t_cst], w=[t_tA[it], pb[it]])
                    yield
                    for it, d, c, s in items:
                        op(AC, lambda h, it=it: h.activation(out=decT[it], in_=tA[it], func=AF.Exp), r=[t_tA[it]], w=[t_decT[it]])
                        op(V, lambda h, it=it, d=d, c=c: h.scalar_tensor_tensor(out=pair[it][0][:, 0:128], in0=KK[it], scalar=nb[d][:, c, hh:hh + 1], in1=dec[it], op0=ALU.mult, op1=ALU.mult),
                           r=[t_dec[it], t_g], w=[t_pair[it][0], pb[it]])
                    yield
                    for it, d, c, s in items:
                        op(V, lambda h, it=it, d=d, s=s: h.tensor_tensor(out=QKr[d][s], in0=QKT[it], in1=decT[it], op=ALU.mult), r=[t_decT[it]], w=[t_QK[d][s], pb[it]])
                        op(T, lambda h, it=it: h.transpose(out=MT[it], in_=pair[it][0][:, 0:128], identity=ident_f), r=[t_pair[it][0], t_cst], w=[pb[it]])
                    yield
                    for it, d, c, s in items:
                        op(AC, lambda h, it=it: h.copy(out=pair[it][0][:, 128:256], in_=MT[it]), w=[t_pair[it][0], pb[it]])
                        op(G, lambda h, it=it: h.tensor_tensor(out=X[it][0], in0=pair[it][0][:, 128:256], in1=ident_f, op=ALU.add), r=[t_pair[it][0], t_cst], w=[t_X[it][0]])
                    yield
                    for lvl in range(1, 7):
                        a = (lvl - 1) % 2
                        b = lvl % 2
                        n = 256 if lvl < 6 else 128
                        for it, d, c, s in items:
                            P_ = pair[it][a][:, 0:128]
                            PT_ = pair[it][a][:, 128:256]
                            op(T, lambda h, it=it, P_=P_, PT_=PT_: h.matmul(PPQ[it][:, 0:128], lhsT=PT_, rhs=P_, start=True, stop=True), r=[t_pair[it][a]], w=[pb[it]])
                            if lvl < 6:
                                op(T, lambda h, it=it, P_=P_, PT_=PT_: h.matmul(PPQ[it][:, 128:256], lhsT=P_, rhs=PT_, start=True, stop=True), r=[t_pair[it][a]], w=[pb[it]])
                        yield
                        for it, d, c, s in items:
                            op(AC, lambda h, it=it, b=b, n=n: h.copy(out=pair[it][b][:, 0:n], in_=PPQ[it][:, 0:n]), w=[t_pair[it][b], pb[it]])
                        yield
                        for it, d, c, s in items:
                            op(T, lambda h, it=it, a=a, b=b: h.matmul(XPQ[it], lhsT=pair[it][b][:, 0:128], rhs=X[it][a], start=True, stop=True), r=[t_pair[it][b], t_X[it][a]], w=[pb[it]])
                        yield
                        for it, d, c, s in items:
                            dst, tdst = (X[it][b], t_X[it][b]) if lvl < 6 else (TTr[d][s], t_TT[d][s])
                            op(V, lambda h, it=it, a=a, dst=dst: h.tensor_tensor(out=dst, in0=XPQ[it], in1=X[it][a], op=ALU.add), r=[t_X[it][a]], w=[tdst, pb[it]])
                        yield

                def chain_steps(steps, hh=h_):
                    for i in steps:
                        s = i % RN
                        cs = (i, NT - 1 - i)
                        for d in range(2):
                            c = cs[d]
                            op(T, lambda h, d=d, c=c: h.matmul(KS[d], lhsT=kT[:, c * 128:(c + 1) * 128], rhs=Sbf[d], start=True, stop=True), r=[t_kT, t_S[d]], w=[pb[4 + d]])
                        yield
                        for d in range(2):
                            c = cs[d]
                            op(V, lambda h, d=d, c=c: h.scalar_tensor_tensor(out=rhs[d], in0=KS[d], scalar=nbeG[d][:, c, hh:hh + 1], in1=v3(vb[d], NT, 128)[:, c, :], op0=ALU.mult, op1=ALU.add),
                               r=[t_g, t_vb[d]], w=[t_rhs[d], pb[4 + d]])
                        yield
                        for d in range(2):
                            op(T, lambda h, d=d, s=s: h.matmul(VN[d], lhsT=TTr[d][s], rhs=rhs[d], start=True, stop=True), r=[t_TT[d][s], t_rhs[d]], w=[pb[4 + d]])
                        yield
                        for d in range(2):
                            op(AC, lambda h, d=d: h.copy(out=vnew[d], in_=VN[d]), w=[t_vnew[d], pb[4 + d]])
                        yield
                        for d in range(2):
                            c = cs[d]
                            op(T, lambda h, d=d, c=c: h.matmul(DS[d], lhsT=v3(kd[d], NT, 128)[:, c, :], rhs=vnew[d], start=True, stop=True), r=[t_kd[d], t_vnew[d]], w=[pb[6 + d]])
                            op(T, lambda h, d=d, c=c: h.matmul(O1[d], lhsT=qT[:, c * 128:(c + 1) * 128], rhs=Sbf[d], start=True, stop=True), r=[t_qT, t_S[d]], w=[pb[4 + d]])
                            op(T, lambda h, d=d, s=s: h.matmul(O2[d], lhsT=QKr[d][s], rhs=vnew[d], start=True, stop=True), r=[t_QK[d][s], t_vnew[d]], w=[pb[4 + d]])
                        yield
                        for d in range(2):
                            c = cs[d]
                            op(V, lambda h, d=d, c=c: h.scalar_tensor_tensor(out=Sbf[d], in0=S32[d], scalar=eGt[d][:, c, hh:hh + 1], in1=DS[d], op0=ALU.mult, op1=ALU.add),
                               r=[t_g, t_S32[d]], w=[t_S[d], pb[6 + d]])
                            op(V, lambda h, d=d, c=c: h.scalar_tensor_tensor(out=S32[d], in0=S32[d], scalar=eGt[d][:, c, hh:hh + 1], in1=DS[d], op0=ALU.mult, op1=ALU.add),
                               r=[t_g], w=[t_S32[d], pb[6 + d]])
                            op(AC, lambda h, d=d, c=c: h.activation(out=tmp[d], in_=O1[d], func=AF.Copy, scale=eG[d][:, c, hh:hh + 1]), r=[t_g], w=[t_tmp[d], pb[4 + d]])
                        yield
                        for d in range(2):
                            c = cs[d]
                            op(V, lambda h, d=d: h.tensor_tensor(out=tmp2[d], in0=O2[d], in1=tmp[d], op=ALU.add), r=[t_tmp[d]], w=[t_tmp2[d], pb[4 + d]])
                            op(G, lambda h, d=d, c=c: h.tensor_tensor(out=oacc3[:, c, :], in0=oacc3[:, c, :], in1=tmp2[d], op=ALU.add), r=[t_tmp2[d]], w=[t_oacc[c]])
                        yield

                for _ in prep_batch((0, 1)):
                    pass
                for bb in range(NT // 2):
                    pg = prep_batch((2 * bb + 2, 2 * bb + 3)) if bb + 1 < NT // 2 else None
                    cg = chain_steps((2 * bb, 2 * bb + 1))
                    while pg is not None or cg is not None:
                        if pg is not None:
                            for _ in range(2):
                                if next(pg, "END") == "END":
                                    pg = None
                                    break
                        if cg is not None:
                            if next(cg, "END") == "END":
                                cg = None

                op(AC, lambda h: h.activation(out=acc, in_=oacc, func=AF.Square), r=t_oacc, w=[t_acc])
                op(V, lambda h: h.tensor_reduce(out=ssum[:, 0:32], in_=v3(acc, NT, 128), axis=AX.X, op=ALU.add), r=[t_acc], w=[t_ss])
                op(AC, lambda h: h.activation(out=ssum[:, 32:64], in_=ssum[:, 0:32], func=AF.Sqrt, scale=1.0 / 128, bias=cc(CC_EPS)), r=[t_ccol], w=[t_ss])
                op(V, lambda h: h.reciprocal(out=ssum[:, 64:96], in_=ssum[:, 32:64]), w=[t_ss])
                rb = ssum[:, 64:96].unsqueeze(2).broadcast_to([128, NT, 128])
                op(V, lambda h, rb=rb: h.tensor_tensor(out=oacc3, in0=oacc3, in1=rb, op=ALU.mult), r=[t_ss], w=t_oacc)
                gw = pl[:, P_GNW:P_GNW + 128].unsqueeze(1).broadcast_to([128, NT, 128])
                op(G, lambda h, gw=gw: h.tensor_tensor(out=oacc3, in0=oacc3, in1=gw, op=ALU.mult), r=[t_prm], w=t_oacc)
                op(V, lambda h: h.tensor_tensor(out=sqr, in0=oacc, in1=za, op=ALU.mult), r=t_oacc + [t_za], w=[t_sqr])
                for cg in range(4):
                    bk = 6 + (cg % 2)
                    pbk = banks[bk][:, :].bitcast(BF16)
                    for j in range(8):
                        c = cg * 8 + j
                        op(T, lambda h, j=j, c=c, pbk=pbk: h.transpose(out=pbk[:, j * 128:(j + 1) * 128], in_=sqr[:, c * 128:(c + 1) * 128], identity=ident_bf),
                           r=[t_sqr, t_cbf], w=[pb[bk]])
                    op(AC, lambda h, cg=cg, pbk=pbk: h.copy(out=yst[:, cg * 1024:(cg + 1) * 1024], in_=pbk), r=[pb[bk]], w=[t_yst])
                dma("sync", scr_y[0, h_], yst, r=[t_yst])

        def phase_C(l, pl, lam_init):
            T, AC, G = "tensor", "scalar", "gpsimd"
            om = 1.0 - lam_init
            ltmp = A.alloc(128 * 4, F32)
            lp = v3(pl[:, P_LAM:P_LAM + 256], 4, 64)
            op(V, lambda h: h.tensor_tensor(out=ltmp[:, 0:64], in0=lp[:, 0, :], in1=lp[:, 1, :], op=ALU.mult), r=[t_prm], w=[t_lam])
            op(V, lambda h: h.tensor_tensor(out=ltmp[:, 64:128], in0=lp[:, 2, :], in1=lp[:, 3, :], op=ALU.mult), r=[t_prm], w=[t_lam])
            op(V, lambda h: h.tensor_reduce(out=lamc[:, 0:2], in_=v3(ltmp, 2, 64), axis=AX.X, op=ALU.add), w=[t_lam])
            op(AC, lambda h: h.activation(out=lamc[:, 2:4], in_=lamc[:, 0:2], func=AF.Exp), w=[t_lam])
            op(V, lambda h: h.tensor_tensor(out=lamc[:, 4:5], in0=lamc[:, 2:3], in1=lamc[:, 3:4], op=ALU.subtract), w=[t_lam])
            op(V, lambda h: h.tensor_scalar(out=lamc[:, 5:6], in0=lamc[:, 4:5], scalar1=float(lam_init), scalar2=-1.0, op0=ALU.add, op1=ALU.mult), w=[t_lam])
            op(V, lambda h: h.memset(lamc[:, 6:7], EPS / (om * om)), w=[t_lam])
            nlam = lamc[:, 5:6]
            sl_scale = 1.0 / (128.0 * om * om)

            qraw = A.alloc(S_ * 2)
            kraw = A.alloc(S_ * 2)
            qTs = [A.alloc(S_ * 2) for _ in range(2)]
            kTs = [A.alloc(S_ * 2) for _ in range(2)]
            vb13s = [v3(A.alloc(NT * 130 * 2), NT, 130) for _ in range(2)]
            zbs = [A.alloc(S_ * 2) for _ in range(2)]
            zb3s = [v3(z_, NT, 128) for z_ in zbs]
            yst = A.alloc(S_ * 2)
            NE = 3
            Eb = [A.alloc(1024 * 2) for _ in range(NE)]
            rt1 = [A.alloc(512 * 4, F32) for _ in range(2)]
            rt2 = [A.alloc(512 * 4, F32) for _ in range(2)]
            osb = [A.alloc(8 * 130 * 4, F32) for _ in range(2)]
            a0 = [A.alloc(128 * 4, F32) for _ in range(2)]
            aa = [A.alloc(128 * 4, F32) for _ in range(2)]
            ytok = [A.alloc(128 * 2) for _ in range(2)]
            jk = [A.alloc(128 * 2) for _ in range(2)]
            rc = [A.alloc(8 * 4, F32) for _ in range(2)]
            t_qr, t_kr, t_yst = [Tok() for _ in range(3)]
            t_qTs, t_kTs, t_vs, t_zs = [[Tok(), Tok()] for _ in range(4)]
            t_rt1 = [Tok(), Tok()]
            t_rt2 = [Tok(), Tok()]
            t_E = [Tok() for _ in range(NE)]
            t_osb = [Tok(), Tok()]
            t_ep = [Tok(), Tok()]
            for p_ in range(2):
                op(V, lambda h, p_=p_: h.memset(vb13s[p_][:, :, 128:130], 1.0), w=[t_vs[p_]])
            slwb = pl[:, P_SLW:P_SLW + 128].unsqueeze(1).broadcast_to([128, NT, 128])
            OG = []
            for gi in range(8):
                OG.append(banks[4 + gi // 3][:, (gi % 3) * 130:(gi % 3) * 130 + 129])

            def prologue(hn):
                p_ = hn % 2
                qT, kT, vb13, zb, zb3 = qTs[p_], kTs[p_], vb13s[p_], zbs[p_], zb3s[p_]
                t_qT, t_kT, t_v, t_z = t_qTs[p_], t_kTs[p_], t_vs[p_], t_zs[p_]
                dma("sync", qraw, scr_fm[3, hn], w=[t_qr])
                dma("sync", kraw, scr_fm[4, hn], w=[t_kr])
                dma("sync", vb13[:, :, 0:128], scr_tm[0, hn].rearrange("p (a b) -> p a b", a=NT, b=128), w=[t_v])
                dma("sync", zb, scr_tm[2, hn], w=[t_z])
                op(G, lambda h, zb3=zb3: h.tensor_tensor(out=zb3, in0=zb3, in1=slwb, op=ALU.mult), r=[t_prm], w=[t_z])
                return rope_gen(qT, kT, t_qT, t_kT)

            def rope_gen(qT, kT, t_qT, t_kT):
                for raw, t_raw, dstT, t_dst in ((qraw, t_qr, qT, t_qT), (kraw, t_kr, kT, t_kT)):
                    for tt in range(8):
                        tsl = slice(tt * 512, (tt + 1) * 512)
                        rb_ = 7
                        r1, r2, tr1, tr2 = rt1[tt % 2], rt2[tt % 2], t_rt1[tt % 2], t_rt2[tt % 2]
                        op(T, lambda h, raw=raw, tsl=tsl, rb_=rb_: h.matmul(banks[rb_][:, :], lhsT=perm_bf, rhs=raw[:, tsl], start=True, stop=True), r=[t_raw, t_cbf], w=[pb[rb_]])
                        op(V, lambda h, tsl=tsl, rb_=rb_, r1=r1: h.tensor_tensor(out=r1, in0=banks[rb_][:, :], in1=St[:, tsl], op=ALU.mult), r=[pb[rb_], t_rope], w=[tr1])
                        op(G, lambda h, raw=raw, tsl=tsl, r2=r2: h.tensor_tensor(out=r2, in0=raw[:, tsl], in1=Ct[:, tsl], op=ALU.mult), r=[t_raw, t_rope], w=[tr2])
                        op(V, lambda h, dstT=dstT, tsl=tsl, r1=r1, r2=r2: h.tensor_tensor(out=dstT[:, tsl], in0=r1, in1=r2, op=ALU.add), r=[tr1, tr2], w=[t_dst])
                        yield
            for _ in prologue(0):
                pass
            rg = None
            for h_ in range(8):
                p_ = h_ % 2
                qT, kT, vb13, zb3 = qTs[p_], kTs[p_], vb13s[p_], zb3s[p_]
                t_qT, t_kT, t_v, t_z = t_qTs[p_], t_kTs[p_], t_vs[p_], t_zs[p_]
                ne = 0
                for qt in range(8):
                    qsl = slice(qt * 512, (qt + 1) * 512)
                    if qt == 6 and h_ + 1 < 8:
                        rg = prologue(h_ + 1)
                    def emit_S(kt, qsl=qsl):
                        ksl = slice(kt * 128, (kt + 1) * 128)
                        for sm in range(2):
                            bk = (kt % 2) * 2 + sm
                            rows = slice(sm * 64, (sm + 1) * 64)
                            op(T, lambda h, bk=bk, rows=rows, ksl=ksl, qsl=qsl, kT=kT, qT=qT: h.matmul(banks[bk][:, :], lhsT=kT[rows, ksl], rhs=qT[rows, qsl], start=True, stop=True),
                               r=[t_kT, t_qT], w=[pb[bk]])

                    emit_S(0)
                    for kt in range(NT):
                        par = kt % 2
                        es = ne % NE
                        ne += 1
                        if kt + 1 < NT:
                            emit_S(kt + 1)
                        if rg is not None and kt % 2 == 1:
                            if next(rg, "END") == "END":
                                rg = None
                        op(AC, lambda h, par=par, es=es: h.activation(out=Eb[es], in_=psp[par][:, :], func=AF.Exp, scale=0.125), r=[pb[2 * par], pb[2 * par + 1]], w=[t_E[es]])
                        for sm in range(2):
                            for sub in range(4):
                                gi = sub * 2 + sm
                                first_in_bank = gi in (0, 4, 6)
                                op(T, lambda h, gi=gi, sm=sm, es=es, sub=sub, kt=kt, fb=first_in_bank, vb13=vb13: h.matmul(
                                    OG[gi], lhsT=Eb[es][:, sm * 512 + sub * 128:sm * 512 + (sub + 1) * 128], rhs=vb13[:, kt, 0:129],
                                    start=(kt == 0 and fb), stop=(kt == NT - 1), skip_group_check=True),
                                   r=[t_E[es], t_v], w=[pb[4 + gi // 3]])
                    ob = osb[qt % 2]
                    tob = t_osb[qt % 2]
                    op(V, lambda h, ob=ob: h.tensor_copy(out=ob[:, 0:390], in_=banks[4][:, 0:390]), r=[pb[4]], w=[tob])
                    op(AC, lambda h, ob=ob: h.copy(out=ob[:, 390:780], in_=banks[5][:, 0:390]), r=[pb[5]], w=[tob])
                    op(V, lambda h, ob=ob: h.tensor_copy(out=ob[:, 780:1040], in_=banks[6][:, 0:260]), r=[pb[6]], w=[tob])
                    b7bf = banks[7][:, :].bitcast(BF16)
                    for sub in range(4):
                        e = sub % 2
                        c = qt * 4 + sub
                        o0 = ob[:, (sub * 2) * 130:(sub * 2) * 130 + 130]
                        o1 = ob[:, (sub * 2 + 1) * 130:(sub * 2 + 1) * 130 + 130]
                        rc_, a0_, aa_, yt_, jk_, tep = rc[e], a0[e], aa[e], ytok[e], jk[e], t_ep[e]
                        op(V, lambda h, rc_=rc_, o0=o0: h.reciprocal(out=rc_[:, 0:1], in_=o0[:, 128:129]), r=[tob], w=[tep])
                        op(V, lambda h, rc_=rc_, o1=o1: h.reciprocal(out=rc_[:, 1:2], in_=o1[:, 128:129]), r=[tob], w=[tep])
                        op(V, lambda h, rc_=rc_: h.tensor_tensor(out=rc_[:, 2:3], in0=rc_[:, 1:2], in1=nlam, op=ALU.mult), r=[t_lam], w=[tep])
                        op(V, lambda h, rc_=rc_, a0_=a0_, o0=o0: h.tensor_scalar(out=a0_, in0=o0[:, 0:128], scalar1=rc_[:, 0:1], scalar2=None, op0=ALU.mult), r=[tob], w=[tep])
                        op(V, lambda h, rc_=rc_, a0_=a0_, aa_=aa_, o1=o1: h.scalar_tensor_tensor(out=aa_, in0=o1[:, 0:128], scalar=rc_[:, 2:3], in1=a0_, op0=ALU.mult, op1=ALU.add), r=[tob], w=[tep])
                        op(V, lambda h, rc_=rc_, aa_=aa_, jk_=jk_: h.scalar_tensor_tensor(out=jk_, in0=aa_, scalar=1.0, in1=aa_, op0=ALU.mult, op1=ALU.mult, accum_out=rc_[:, 3:4]), w=[tep])
                        op(AC, lambda h, rc_=rc_: h.activation(out=rc_[:, 4:5], in_=rc_[:, 3:4], func=AF.Ln, scale=sl_scale, bias=lamc[:, 6:7]), r=[t_lam], w=[tep])
                        op(AC, lambda h, rc_=rc_: h.activation(out=rc_[:, 5:6], in_=rc_[:, 4:5], func=AF.Exp, scale=-0.5), w=[tep])
                        op(V, lambda h, rc_=rc_, aa_=aa_, yt_=yt_, c=c, zb3=zb3: h.scalar_tensor_tensor(out=yt_, in0=aa_, scalar=rc_[:, 5:6], in1=zb3[:, c, :], op0=ALU.mult, op1=ALU.mult), r=[t_z], w=[tep])
                        op(T, lambda h, yt_=yt_, sub=sub: h.transpose(out=b7bf[:, sub * 128:(sub + 1) * 128], in_=yt_, identity=ident_bf), r=[tep, t_cbf], w=[pb[7]])
                    op(AC, lambda h, qsl=qsl: h.copy(out=yst[:, qsl], in_=b7bf[:, 0:512]), r=[pb[7]], w=[t_yst])
                dma("sync", scr_y[1, h_], yst, r=[t_yst])

        for l in range(n_layers):
            pl = prm[:, l * NPRM:(l + 1) * NPRM]
            lam_init = 0.8 - 0.6 * math.exp(-0.3 * l)
            m0 = A.mark()
            wv = w_in_d[l].rearrange("(kc p) n -> p kc n", p=128)
            wb = [(A.alloc(8 * 512 * 2), Tok()) for _ in range(3)]
            stg = [(A.alloc(S_ * 2), Tok()) for _ in range(2)]
            stT = [(A.alloc(4 * S_ * 2), Tok()) for _ in range(1)]
            nwl = 0
            nst = 0
            nbk = 0
            fm_list = [(C_QA, 0, None), (C_KA, 1, None), (C_VA, 2, None), (C_QB, 3, None), (C_KB, 4, None),
                       (C_GA, 5, AF.Sigmoid), (C_GB, 6, AF.Sigmoid)]
            nev = 0
            for c0, ti, fn_ in fm_list:
                for g in range(2):
                    wt, twt = wb[nwl % 3]
                    nwl += 1
                    wt3 = v3(wt, 8, 512)
                    dma("gpsimd", wt3, wv[:, :, c0 + g * 512:c0 + (g + 1) * 512], w=[twt])
                    for hb in range(4):
                        st, tst = stg[nst % 2]
                        nst += 1
                        for tt in range(8):
                            bk = nbk % 4
                            nbk += 1
                            for kc in range(8):
                                op("tensor", lambda h, bk=bk, wt3=wt3, hb=hb, kc=kc, tt=tt: h.matmul(
                                    banks[bk][:, :], lhsT=wt3[:, kc, hb * 128:(hb + 1) * 128], rhs=hT3[:, kc, tt * 512:(tt + 1) * 512],
                                    start=(kc == 0), stop=(kc == 7)), r=[twt] + t_hT[tt * 4:tt * 4 + 4], w=[pb[bk]])
                            dst = st[:, tt * 512:(tt + 1) * 512]
                            if fn_ is not None:
                                op("scalar", lambda h, bk=bk, dst=dst, fn_=fn_: h.activation(out=dst, in_=banks[bk][:, :], func=fn_), r=[pb[bk]], w=[tst])
                            elif nev % 2 == 0:
                                op("scalar", lambda h, bk=bk, dst=dst: h.copy(out=dst, in_=banks[bk][:, :]), r=[pb[bk]], w=[tst])
                            else:
                                op("vector", lambda h, bk=bk, dst=dst: h.tensor_copy(out=dst, in_=banks[bk][:, :]), r=[pb[bk]], w=[tst])
                            nev += 1
                        dma("sync", scr_fm[ti, g * 4 + hb], st, r=[tst])
            tm_list = [(C_VB, 0, None), (C_ZA, 1, AF.Silu), (C_ZB, 2, AF.Silu)]
            for c0, ti, fn_ in tm_list:
                for g in range(2):
                    wt, twt = wb[nwl % 3]
                    nwl += 1
                    wt3 = v3(wt, 8, 512)
                    dma("gpsimd", wt3, wv[:, :, c0 + g * 512:c0 + (g + 1) * 512], w=[twt])
                    st, tst = stT[0]
                    st4 = st.rearrange("p (a b c) -> p a b c", a=4, b=NT, c=128)
                    for t in range(NT):
                        bk = nbk % 4
                        nbk += 1
                        for kc in range(8):
                            op("tensor", lambda h, bk=bk, wt3=wt3, kc=kc, t=t: h.matmul(
                                banks[bk][:, :], lhsT=hT3[:, kc, t * 128:(t + 1) * 128], rhs=wt3[:, kc, :],
                                start=(kc == 0), stop=(kc == 7)), r=[twt, t_hT[t]], w=[pb[bk]])
                        dst = st4[:, :, t, :]
                        src = v3(banks[bk][:, :], 4, 128)
                        if fn_ is not None:
                            op("scalar", lambda h, dst=dst, src=src, fn_=fn_: h.activation(out=dst, in_=src, func=fn_), r=[pb[bk]], w=[tst])
                        elif t % 2 == 0:
                            op("scalar", lambda h, dst=dst, src=src: h.copy(out=dst, in_=src), r=[pb[bk]], w=[tst])
                        else:
                            op("vector", lambda h, dst=dst, src=src: h.tensor_copy(out=dst, in_=src), r=[pb[bk]], w=[tst])
                    for hb in range(4):
                        dma("sync", scr_tm[ti, g * 4 + hb], st[:, hb * S_:(hb + 1) * S_], r=[tst])
            wt, twt = wb[nwl % 3]
            nwl += 1
            wt3 = v3(wt, 8, 512)
            dma("gpsimd", wt3[:, :, 0:32], wv[:, :, C_AB:C_AB + 32], w=[twt])
            abt3 = v3(abt, NT, 32)
            for t in range(NT):
                bk = nbk % 4
                nbk += 1
                for kc in range(8):
                    op("tensor", lambda h, bk=bk, wt3=wt3, kc=kc, t=t: h.matmul(
                        banks[bk][:, 0:32], lhsT=hT3[:, kc, t * 128:(t + 1) * 128], rhs=wt3[:, kc, 0:32],
                        start=(kc == 0), stop=(kc == 7)), r=[twt, t_hT[t]], w=[pb[bk]])
                op("vector", lambda h, bk=bk, t=t: h.tensor_copy(out=abt3[:, t, :], in_=banks[bk][:, 0:32]), r=[pb[bk]], w=[t_abt])
            SC.barrier()
            A.reset(m0)
            if stop_after == "A":
                break

            A.reset(m_H)
            phase_B(l, pl)
            SC.barrier()
            if stop_after == "B":
                break
            A.reset(m_H)
            phase_C(l, pl, lam_init)
            SC.barrier()
            if stop_after == "C":
                break
            A.reset(m_W)
            last = (l == L_ - 1)
            dma("sync", nwb, nw_d[l + 1], w=[t_nwb])
            wts = []
            for wd in (w_pa_d, w_pb_d, w_out_d):
                wt = A.alloc(8 * D_ * 2)
                tw = Tok()
                wt3 = v3(wt, 8, D_)
                for hh in range(2):
                    dma("gpsimd", wt3[:, :, hh * 512:(hh + 1) * 512], wd[l].rearrange("(kc p) n -> p kc n", p=128)[:, :, hh * 512:(hh + 1) * 512], w=[tw])
                wts.append((wt3, tw))
            (wpa3, twpa), (wpb3, twpb), (wo3, two) = wts
            yb_ = [(A.alloc(8 * 512 * 2), A.alloc(8 * 512 * 2), Tok()) for _ in range(1)]
            gb_ = [(A.alloc(2 * 512 * 2), Tok()) for _ in range(2)]
            mg = A.alloc(8 * 512 * 2)
            mg3 = v3(mg, 8, 512)
            t_mg = Tok()
            tt1 = A.alloc(512 * 4, F32)
            tt2 = A.alloc(512 * 4, F32)
            t_tt = Tok()
            xts = [(A.alloc(D_ * 4, F32), Tok()) for _ in range(2)]
            ntm = [norm_tmp() for _ in range(1)]
            ng = 0
            for tt in range(8):
                ya, yb, ty = yb_[0]
                ya3, yb3 = v3(ya, 8, 512), v3(yb, 8, 512)
                dma("sync", ya3, scr_y[0][:, :, tt * 512:(tt + 1) * 512].rearrange("h p t -> p h t"), w=[ty])
                dma("sync", yb3, scr_y[1][:, :, tt * 512:(tt + 1) * 512].rearrange("h p t -> p h t"), w=[ty])
                for m in range(8):
                    gg, tg = gb_[ng % 2]
                    ng += 1
                    dma("sync", gg[:, 0:512], scr_fm[5, m][:, tt * 512:(tt + 1) * 512], w=[tg])
                    dma("sync", gg[:, 512:1024], scr_fm[6, m][:, tt * 512:(tt + 1) * 512], w=[tg])
                    for kc in range(8):
                        op("tensor", lambda h, kc=kc, m=m, ya3=ya3: h.matmul(banks[0][:, :], lhsT=wpa3[:, kc, m * 128:(m + 1) * 128], rhs=ya3[:, kc, :],
                                                                 start=(kc == 0), stop=(kc == 7)), r=[twpa, ty], w=[pb[0]])
                    for kc in range(8):
                        op("tensor", lambda h, kc=kc, m=m, yb3=yb3: h.matmul(banks[1][:, :], lhsT=wpb3[:, kc, m * 128:(m + 1) * 128], rhs=yb3[:, kc, :],
                                                                 start=(kc == 0), stop=(kc == 7)), r=[twpb, ty], w=[pb[1]])
                    op(V, lambda h, gg=gg: h.tensor_tensor(out=tt1, in0=banks[0][:, :], in1=gg[:, 0:512], op=ALU.mult), r=[pb[0], tg], w=[t_tt])
                    op(V, lambda h, gg=gg: h.tensor_tensor(out=tt2, in0=banks[1][:, :], in1=gg[:, 512:1024], op=ALU.mult), r=[pb[1], tg], w=[t_tt])
                    op("gpsimd", lambda h, m=m: h.tensor_tensor(out=mg3[:, m, :], in0=tt1, in1=tt2, op=ALU.add), r=[t_tt], w=[t_mg])
                for sub in range(4):
                    t = tt * 4 + sub
                    xt, tx = xts[t % 2]
                    dma("sync", xt, scr_x[t * 128:(t + 1) * 128, :], w=[tx])
                    for nh in range(2):
                        bk = 2 + nh
                        for kc in range(8):
                            op("tensor", lambda h, kc=kc, nh=nh, sub=sub, bk=bk: h.matmul(
                                banks[bk][:, :], lhsT=mg3[:, kc, sub * 128:(sub + 1) * 128], rhs=wo3[:, kc, nh * 512:(nh + 1) * 512],
                                start=(kc == 0), stop=(kc == 7)), r=[two, t_mg], w=[pb[bk]])
                        op(V, lambda h, nh=nh, bk=bk, xt=xt: h.tensor_tensor(out=xt[:, nh * 512:(nh + 1) * 512], in0=banks[bk][:, :], in1=xt[:, nh * 512:(nh + 1) * 512], op=ALU.add),
                           r=[pb[bk]], w=[tx])
                    if not last:
                        dma("sync", scr_x[t * 128:(t + 1) * 128, :], xt, r=[tx])
                    norm_tile(xt, tx, t, ntm[0], last, 4 + (t % 2))
            SC.barrier()
            A.reset(m_W)

        SC.barrier()

        with nc.Block() as block:
            @block.sync
            def _(h):
                SC.replay("sync", h)

            @block.scalar
            def _(h):
                SC.replay("scalar", h)

            @block.vector
            def _(h):
                SC.replay("vector", h)

            @block.gpsimd
            def _(h):
                SC.replay("gpsimd", h)

            @block.tensor
            def _(h):
                SC.replay("tensor", h)
    return nc


def _host_consts():
    c = np.zeros((128, NCST), np.float32)
    p = np.arange(128)[:, None]
    f = np.arange(128)[None, :]
    c[:, K_ID:K_ID + 128] = (p == f)
    c[:, K_PMF:K_PMF + 128] = np.where(p > f, 0.0, BIGM)
    c[:, K_NMF:K_NMF + 128] = np.where(f >= p, 0.0, -BIGM)
    c[:, K_PMB:K_PMB + 128] = np.where(p < f, 0.0, BIGM)
    c[:, K_NMB:K_NMB + 128] = np.where(f <= p, 0.0, -BIGM)
    c[:, K_TRF:K_TRF + 128] = (p <= f)
    c[:, K_TRB:K_TRB + 128] = (p >= f)
    perm = np.zeros((128, 128), np.float32)
    for base in (0, 64):
        for d in range(8):
            perm[base + d + 8, base + d] = 1.0
            perm[base + d, base + d + 8] = 1.0
    c[:, K_PERM:K_PERM + 128] = perm
    c[:, K_ONE:K_ONE + 128] = 1.0
    inv_freq = (500000.0 ** (-(np.arange(0, 16, 2, dtype=np.float32) / np.float32(16)))).astype(np.float32)
    for base in (0, 64):
        for d in range(8):
            c[base + d, K_INVF] = inv_freq[d]
            c[base + d + 8, K_INVF] = inv_freq[d]
            c[base + d, K_SGN] = -1.0
            c[base + d + 8, K_SGN] = 1.0
    return c


def _host_params(inputs):
    prm = np.zeros((128, L_, NPRM), np.float32)
    for l in range(L_):
        cw = inputs["conv_w"][l]
        cwr = cw.reshape(5, 3, 8, 128).transpose(3, 1, 2, 0)
        prm[:, l, P_CW:P_CW + 120] = cwr.reshape(128, 120)
        prm[:, l, P_ALOG:P_ALOG + 16] = inputs["a_log"][l].reshape(1, 16)
        prm[:, l, P_DTB:P_DTB + 16] = inputs["dt_bias"][l].reshape(1, 16)
        prm[:, l, P_GNW:P_GNW + 128] = inputs["gdn_norm_w"][l].reshape(1, 128)
        prm[:, l, P_SLW:P_SLW + 128] = inputs["diff_subln_w"][l].reshape(1, 128)
        prm[:, l, P_LAM:P_LAM + 256] = inputs["diff_lambda"][l].reshape(1, 256)
    nw = np.zeros((L_ + 1, 128, D_), np.float32)
    for l in range(L_):
        nw[l] = inputs["norm_w"][l][None, :]
    nw[L_] = inputs["final_norm_w"][None, :]
    return prm.reshape(128, L_ * NPRM), nw


def _in_maps(inputs):
    cst = _host_consts()
    prm, nw = _host_params(inputs)
    f32 = lambda a: np.ascontiguousarray(np.asarray(a, dtype=np.float32))
    w_in, w_pa, w_pb, w_out = (f32(inputs[k]) for k in ("w_in", "w_pa", "w_pb", "w_out"))
    maps = []
    for b in range(8):
        maps.append({
            "x": f32(inputs["x"][b]),
            "pos": np.ascontiguousarray(np.broadcast_to(np.asarray(inputs["positions"][b], dtype=np.int32)[None, :], (128, S_))),
            "w_in": w_in, "w_pa": w_pa, "w_pb": w_pb, "w_out": w_out,
            "cst": cst, "prm": prm, "nw": nw,
        })
    return maps


def kernel(**inputs):
    nc = build()
    maps = _in_maps(inputs)
    res = run_bass_kernel_spmd(nc, maps, core_ids=list(range(8)))
    return np.stack([np.asarray(r["out"], dtype=np.float32) for r in res.results], axis=0)
```
